# Optimizing a Trainium2 kernel written in Bass

```python
import jax
import jax.numpy as jnp
from jax import lax
import numpy as np

D_MODEL = 2048
BATCH = 2
SEQ = 8192
DEPTH = 4

GRID_W = 64
N_MIXERS = 2
N_NA_LAYERS = (DEPTH + N_MIXERS - 1) // N_MIXERS
N_MLA_LAYERS = DEPTH // N_MIXERS
RMS_EPS = 1e-6

NA_HEADS = 16
NA_HEAD_DIM = 128
NA_WIDTH = NA_HEADS * NA_HEAD_DIM
NA_KH_MAX = 8
NA_KW = 16

MLA_HEADS = 16
Q_LORA_RANK = 512
KV_LORA_RANK = 512
QK_NOPE_DIM = 128
QK_ROPE_DIM = 64
QK_HEAD_DIM = QK_NOPE_DIM + QK_ROPE_DIM
V_HEAD_DIM = 128
MLA_WIDTH = MLA_HEADS * V_HEAD_DIM
MLA_IN_DIM = Q_LORA_RANK + KV_LORA_RANK + QK_ROPE_DIM + MLA_WIDTH
ROPE_THETA = 10000.0
Q_BLOCK = 128

kernel_name = "hybrid_na_mla_sandwich_encoder"


def rms_norm(x, g):
    xf = x.astype(jnp.float32)
    y = xf * lax.rsqrt(jnp.mean(xf * xf, axis=-1, keepdims=True) + RMS_EPS)
    return (y * g.astype(jnp.float32)).astype(x.dtype)


def rope_tables(length):
    inv_freq = 1.0 / (ROPE_THETA ** (jnp.arange(0, QK_ROPE_DIM, 2, dtype=jnp.float32) / QK_ROPE_DIM))
    ang = jnp.arange(length, dtype=jnp.float32)[:, None] * inv_freq[None, :]
    return jnp.cos(ang), jnp.sin(ang)


def apply_rope(x, cos, sin):
    xf = x.astype(jnp.float32)
    x1, x2 = xf[..., 0::2], xf[..., 1::2]
    c, s = cos[None, :, None, :], sin[None, :, None, :]
    out = jnp.stack([x1 * c - x2 * s, x1 * s + x2 * c], axis=-1)
    return out.reshape(x.shape).astype(x.dtype)


def neighbourhood_attention(q, k, v, rpb):
    B, L, H, Dh = q.shape
    rows = L // GRID_W
    kh = min(NA_KH_MAX, rows)
    kw = NA_KW
    qg = q.reshape(B, rows, GRID_W, H, Dh)
    kg = k.reshape(B, rows, GRID_W, H, Dh)
    vg = v.reshape(B, rows, GRID_W, H, Dh)
    col = jnp.arange(GRID_W)
    col_start = jnp.clip(col - kw // 2, 0, GRID_W - kw)
    col_idx = col_start[:, None] + jnp.arange(kw)[None, :]
    dc = col_idx - col[:, None] + (kw - 1)
    scale = Dh ** -0.5

    def one_row(r):
        row_start = jnp.clip(r - kh // 2, 0, rows - kh)
        k_rows = lax.dynamic_slice_in_dim(kg, row_start, kh, axis=1)
        v_rows = lax.dynamic_slice_in_dim(vg, row_start, kh, axis=1)
        k_nb = jnp.take(k_rows, col_idx, axis=2)
        v_nb = jnp.take(v_rows, col_idx, axis=2)
        q_r = lax.dynamic_index_in_dim(qg, r, axis=1, keepdims=False)
        s = jnp.einsum('bqhd,bxqyhd->bhqxy', q_r, k_nb).astype(jnp.float32) * scale
        dr = row_start + jnp.arange(kh) - r + (NA_KH_MAX - 1)
        bias = rpb[:, dr[None, :, None], dc[:, None, :]]
        s = s + bias[None].astype(jnp.float32)
        p = jax.nn.softmax(s.reshape(B, H, GRID_W, kh * kw), axis=-1)
        p = p.reshape(B, H, GRID_W, kh, kw).astype(v.dtype)
        return jnp.einsum('bhqxy,bxqyhd->bqhd', p, v_nb)

    out = lax.map(one_row, jnp.arange(rows))
    return jnp.transpose(out, (1, 0, 2, 3, 4)).reshape(B, L, H * Dh)


def dense_attention(q, k, v, scale):
    B, L, H, Dq = q.shape
    Dv = v.shape[-1]
    nb = L // Q_BLOCK
    qb = jnp.transpose(q.reshape(B, nb, Q_BLOCK, H, Dq), (1, 0, 2, 3, 4))

    def one_block(qi):
        s = jnp.einsum('bqhd,bkhd->bhqk', qi, k).astype(jnp.float32) * scale
        p = jax.nn.softmax(s, axis=-1).astype(v.dtype)
        return jnp.einsum('bhqk,bkhd->bqhd', p, v)

    o = lax.map(one_block, qb)
    return jnp.transpose(o, (1, 0, 2, 3, 4)).reshape(B, L, H * Dv)


def na_mixer(h, w_in, rpb):
    B, L, _ = h.shape
    q, k, v, z = jnp.split(h @ w_in, 4, axis=-1)
    shp = (B, L, NA_HEADS, NA_HEAD_DIM)
    o = neighbourhood_attention(q.reshape(shp), k.reshape(shp), v.reshape(shp), rpb)
    return o, z


def mla_mixer(h, w_in, q_norm, w_q_b, kv_norm, w_kv_b):
    B, L, _ = h.shape
    splits = [Q_LORA_RANK, Q_LORA_RANK + KV_LORA_RANK, Q_LORA_RANK + KV_LORA_RANK + QK_ROPE_DIM]
    c_q, c_kv, k_rope, z = jnp.split(h @ w_in, splits, axis=-1)
    q = (rms_norm(c_q, q_norm) @ w_q_b).reshape(B, L, MLA_HEADS, QK_HEAD_DIM)
    kv = (rms_norm(c_kv, kv_norm) @ w_kv_b).reshape(B, L, MLA_HEADS, QK_NOPE_DIM + V_HEAD_DIM)
    cos, sin = rope_tables(L)
    q = jnp.concatenate([q[..., :QK_NOPE_DIM], apply_rope(q[..., QK_NOPE_DIM:], cos, sin)], axis=-1)
    k_r = apply_rope(k_rope[:, :, None, :], cos, sin)
    k = jnp.concatenate([kv[..., :QK_NOPE_DIM],
                         jnp.broadcast_to(k_r, (B, L, MLA_HEADS, QK_ROPE_DIM))], axis=-1)
    v = kv[..., QK_NOPE_DIM:]
    o = dense_attention(q, k, v, QK_HEAD_DIM ** -0.5)
    return o, z


def setup_inputs(seed: int = 0) -> dict:
    key = jax.random.key(seed)
    ks = jax.random.split(key, 13)

    def w(k, shape, fan_in):
        return jax.random.normal(k, shape, jnp.float32) * fan_in ** -0.5

    def gain(k, shape):
        return 1.0 + 0.05 * jax.random.normal(k, shape, jnp.float32)

    return {
        "x": jax.random.normal(ks[0], (BATCH, SEQ, D_MODEL), jnp.float32),
        "norm_pre": gain(ks[1], (DEPTH, D_MODEL)),
        "norm_post": gain(ks[2], (DEPTH, D_MODEL)),
        "na_w_in": w(ks[3], (N_NA_LAYERS, D_MODEL, 4 * NA_WIDTH), D_MODEL),
        "na_rpb": 0.1 * jax.random.normal(ks[4], (N_NA_LAYERS, NA_HEADS, 2 * NA_KH_MAX - 1, 2 * NA_KW - 1), jnp.float32),
        "na_w_out": w(ks[5], (N_NA_LAYERS, NA_WIDTH, D_MODEL), NA_WIDTH),
        "mla_w_in": w(ks[6], (N_MLA_LAYERS, D_MODEL, MLA_IN_DIM), D_MODEL),
        "mla_q_norm": gain(ks[7], (N_MLA_LAYERS, Q_LORA_RANK)),
        "mla_w_q_b": w(ks[8], (N_MLA_LAYERS, Q_LORA_RANK, MLA_HEADS * QK_HEAD_DIM), Q_LORA_RANK),
        "mla_kv_norm": gain(ks[9], (N_MLA_LAYERS, KV_LORA_RANK)),
        "mla_w_kv_b": w(ks[10], (N_MLA_LAYERS, KV_LORA_RANK, MLA_HEADS * (QK_NOPE_DIM + V_HEAD_DIM)), KV_LORA_RANK),
        "mla_w_out": w(ks[11], (N_MLA_LAYERS, MLA_WIDTH, D_MODEL), MLA_WIDTH),
    }


def reference(x, norm_pre, norm_post, na_w_in, na_rpb, na_w_out, mla_w_in, mla_q_norm,
              mla_w_q_b, mla_kv_norm, mla_w_kv_b, mla_w_out):
    for i in range(DEPTH):
        h = rms_norm(x, norm_pre[i])
        j = i // N_MIXERS
        if i % N_MIXERS == 0:
            o, z = na_mixer(h, na_w_in[j], na_rpb[j])
            w_out = na_w_out[j]
        else:
            o, z = mla_mixer(h, mla_w_in[j], mla_q_norm[j], mla_w_q_b[j], mla_kv_norm[j], mla_w_kv_b[j])
            w_out = mla_w_out[j]
        y = (o * jax.nn.silu(z)) @ w_out
        x = x + rms_norm(y, norm_post[i])
    return x
```

```python
import numpy as np
import ml_dtypes
from contextlib import ExitStack
import concourse.bass as bass
import concourse.mybir as mybir
from concourse.bass_utils import run_bass_kernel_spmd

F32 = mybir.dt.float32
BF16 = mybir.dt.bfloat16
AF = mybir.ActivationFunctionType
ALU = mybir.AluOpType
AX = mybir.AxisListType

D = 2048
NCORES = 8
TOK = 2048
HALO = 256
TH = TOK + 2 * HALO
NH = 16
EPS = 1e-6
NEG = -30000.0


class Buf:
    __slots__ = ("name", "w", "r", "dsem", "dcount")

    def __init__(self, name):
        self.name = name
        self.w = []
        self.r = []
        self.dsem = None
        self.dcount = 0


class Queue:
    def __init__(self, prog, name, eng):
        self.name = name
        self.eng = eng
        self.sem = prog.nc.alloc_semaphore("q_" + name)
        self.count = 0
        self.known = {}
        self.pending = []


class Prog:
    def __init__(self, nc):
        self.nc = nc
        self.pe = Queue(self, "pe", nc.tensor)
        self.act = Queue(self, "act", nc.scalar)
        self.dve = Queue(self, "dve", nc.vector)
        self.pool = Queue(self, "pool", nc.gpsimd)
        self.sp = Queue(self, "sp", nc.sync)
        self.queues = [self.pe, self.act, self.dve, self.pool, self.sp]
        self.dbufs = []
        self.out_events = []

    def buf(self, name):
        return Buf(name)

    def bufs(self, name, n):
        return [Buf(f"{name}{i}") for i in range(n)]

    def _wait(self, q, events):
        best = {}
        for (sem, val) in events:
            k = id(sem)
            if k not in best or best[k][1] < val:
                best[k] = (sem, val)
        for k, (sem, val) in best.items():
            if q.known.get(k, 0) >= val:
                continue
            q.eng.wait_ge(sem, val)
            q.known[k] = val

    def _deps(self, reads, writes):
        deps = []
        for b in reads:
            deps += b.w
        for b in writes:
            deps += b.w
            deps += b.r
        return deps

    def _check_pending(self, q, writes, reads):
        for qq in self.queues:
            for (b, kind) in qq.pending:
                if qq is q:
                    continue
                for wb in writes:
                    assert wb is not b, f"write to {b.name} while unsignaled op pending on {qq.name}"
                if kind == 'w':
                    for rb in reads:
                        assert rb is not b, f"read of {b.name} while unsignaled write pending on {qq.name}"

    @staticmethod
    def _rec(b, kind, ev):
        if kind == 'r':
            if ev not in b.r:
                b.r.append(ev)
        else:
            if ev not in b.w:
                b.w.append(ev)

    def op(self, q, fn, reads=(), writes=(), signal=True):
        self._check_pending(q, writes, reads)
        self._wait(q, self._deps(reads, writes))
        ins = fn(q.eng)
        for b in writes:
            if b.r:
                b.r = []
                b.w = []
        for b in reads:
            q.pending.append((b, 'r'))
        for b in writes:
            q.pending.append((b, 'w'))
        if signal:
            q.count += 1
            ins.then_inc(q.sem, 1)
            ev = (q.sem, q.count)
            for (b, kind) in q.pending:
                self._rec(b, kind, ev)
            q.pending = []
        return ins

    def dma(self, q, out, in_, reads=(), writes=(), sem_buf=None, **kw):
        self._check_pending(q, writes, reads)
        self._wait(q, self._deps(reads, writes))
        ins = q.eng.dma_start(out=out, in_=in_, **kw)
        sb = sem_buf
        if sb.dsem is None:
            sb.dsem = self.nc.alloc_semaphore("d_" + sb.name)
            self.dbufs.append(sb)
        sb.dcount += 16
        ins.then_inc(sb.dsem, 16)
        ev = (sb.dsem, sb.dcount)
        for b in writes:
            if b.r:
                b.r = []
                b.w = []
        for b in reads:
            self._rec(b, 'r', ev)
        for b in writes:
            self._rec(b, 'w', ev)
        return ev

    def barrier(self):
        evs = []
        for q in self.queues:
            assert not q.pending, f"pending on {q.name} at barrier"
            if q.count:
                evs.append((q.sem, q.count))
        for b in self.dbufs:
            evs.append((b.dsem, b.dcount))
        for q in self.queues:
            self._wait(q, evs)

    def finish(self):
        evs = [(b.dsem, b.dcount) for b in self.dbufs]
        for q in self.queues:
            if q.count:
                evs.append((q.sem, q.count))
        self._wait(self.sp, evs)


def mm_group(P, ps_ap, ps_bufs, pairs, reads, signal_last=True, first_start=True):
    n = len(pairs)
    for i, (l, r) in enumerate(pairs):
        P.op(P.pe, lambda e, l=l, r=r, i=i: e.matmul(ps_ap, l, r, start=(i == 0 and first_start), stop=(i == n - 1)),
             reads=reads, writes=ps_bufs, signal=(signal_last and i == n - 1))


def rstd_from_ss(P, rstd_ap, ss_ap, n, rbufs, sbufs):
    P.op(P.act, lambda e: e.activation(out=rstd_ap, in_=ss_ap, func=AF.Sqrt, scale=1.0 / n, bias=EPS),
         reads=sbufs, writes=rbufs)
    P.op(P.dve, lambda e: e.reciprocal(out=rstd_ap, in_=rstd_ap), reads=rbufs, writes=rbufs)


class Banks:
    def __init__(self, P, es):
        self.t = es.enter_context(P.nc.psum_tensor("psum_all", [128, 8 * 512], F32))
        self.b = P.bufs("bank", 8)
        self.rr = 0

    def f32(self, i, n=1):
        return self.t[:, i * 512:(i + n) * 512]

    def bf16(self, i):
        return self.t[:, i * 512:(i + 1) * 512].bitcast(BF16)

    def next(self):
        i = self.rr
        self.rr = (self.rr + 1) % 8
        return i


def phase_hT(P, es, bk, x_dram, row0, ntiles, gpre_t, gpre_b, ident, ident_b, hT, hT_bufs):
    nc = P.nc
    NS = 3
    xs = [es.enter_context(nc.sbuf_tensor(f"p1_xs{i}", [128, D], F32)) for i in range(NS)]
    xs_b = P.bufs("p1_xs", NS)
    hb = [es.enter_context(nc.sbuf_tensor(f"p1_hb{i}", [128, D], BF16)) for i in range(2)]
    hb_b = P.bufs("p1_hb", 2)
    junk = es.enter_context(nc.sbuf_tensor("p1_junk", [128, D], BF16))
    junk_b = P.buf("p1_junk")
    st = es.enter_context(nc.sbuf_tensor("p1_st", [128, 2 * NS], F32))
    ss_b = P.bufs("p1_ss", NS)
    rs_b = P.bufs("p1_rs", NS)
    for t in range(ntiles):
        s = t % NS
        P.dma(P.sp, xs[s][:], x_dram[row0 + t * 128: row0 + (t + 1) * 128, :], writes=[xs_b[s]], sem_buf=xs_b[s])
        ss_ap = st[:, 2 * s:2 * s + 1]
        rs_ap = st[:, 2 * s + 1:2 * s + 2]
        P.op(P.act, lambda e: e.activation(out=junk[:], in_=xs[s][:], func=AF.Square, accum_out=ss_ap),
             reads=[xs_b[s]], writes=[junk_b, ss_b[s]])
        rstd_from_ss(P, rs_ap, ss_ap, D, [rs_b[s]], [ss_b[s]])
        h = t % 2
        P.op(P.dve, lambda e: e.scalar_tensor_tensor(out=hb[h][:], in0=xs[s][:], scalar=rs_ap, in1=gpre_t[:],
                                                     op0=ALU.mult, op1=ALU.mult),
             reads=[xs_b[s], rs_b[s], gpre_b], writes=[hb_b[h]])
        for half in range(2):
            bi = bk.next()
            pst = bk.bf16(bi)
            for j in range(8):
                kc = half * 8 + j
                P.op(P.pe, lambda e, kc=kc, j=j: e.transpose(pst[:, j * 128:(j + 1) * 128], hb[h][:, kc * 128:(kc + 1) * 128], ident[:]),
                     reads=[hb_b[h], ident_b], writes=[bk.b[bi]], signal=(j == 7))
            eng = P.act if half == 0 else P.dve
            dst = hT[:, half * 8:(half + 1) * 8, t * 128:(t + 1) * 128]
            src = pst.rearrange("p (k n) -> p k n", k=8)
            if eng is P.act:
                P.op(eng, lambda e: e.copy(out=dst, in_=src), reads=[bk.b[bi]], writes=[hT_bufs[t]])
            else:
                P.op(eng, lambda e: e.tensor_copy(out=dst, in_=src), reads=[bk.b[bi]], writes=[hT_bufs[t]])


def phase_out(P, es, bk, gT, gT_bufs, wo, wo_b, gpost_t, gpost_b, x_dram, xrow0, out_dram, ntiles):
    nc = P.nc
    NS = 2
    xs = [es.enter_context(nc.sbuf_tensor(f"p3_xs{i}", [128, D], F32)) for i in range(NS)]
    xs_b = P.bufs("p3_xs", NS)
    ys = [es.enter_context(nc.sbuf_tensor(f"p3_ys{i}", [128, D], F32)) for i in range(NS)]
    ys_b = P.bufs("p3_ys", NS)
    junk = es.enter_context(nc.sbuf_tensor("p3_junk", [128, D], BF16))
    junk_b = P.buf("p3_junk")
    st = es.enter_context(nc.sbuf_tensor("p3_st", [128, 2 * NS], F32))
    ss_b = P.bufs("p3_ss", NS)
    rs_b = P.bufs("p3_rs", NS)
    for t in range(ntiles):
        s = t % NS
        P.dma(P.sp, xs[s][:], x_dram[xrow0 + t * 128: xrow0 + (t + 1) * 128, :], reads=[], writes=[xs_b[s]], sem_buf=xs_b[s])
        b0 = 4 * (t % 2)
        for nb in range(4):
            pairs = [(gT[:, kc, t * 128:(t + 1) * 128], wo[:, kc, nb * 512:(nb + 1) * 512]) for kc in range(16)]
            mm_group(P, bk.f32(b0 + nb), [bk.b[b0 + nb]], pairs, reads=list(gT_bufs) + [wo_b])
        ybanks = [bk.b[b0 + i] for i in range(4)]
        yps = bk.f32(b0, 4)
        ss_ap = st[:, 2 * s:2 * s + 1]
        rs_ap = st[:, 2 * s + 1:2 * s + 2]
        P.op(P.act, lambda e: e.activation(out=junk[:], in_=yps, func=AF.Square, accum_out=ss_ap),
             reads=ybanks, writes=[junk_b, ss_b[s]])
        rstd_from_ss(P, rs_ap, ss_ap, D, [rs_b[s]], [ss_b[s]])
        P.op(P.dve, lambda e: e.scalar_tensor_tensor(out=ys[s][:], in0=yps, scalar=rs_ap, in1=gpost_t[:],
                                                     op0=ALU.mult, op1=ALU.mult),
             reads=ybanks + [rs_b[s], gpost_b], writes=[ys_b[s]])
        P.op(P.pool, lambda e: e.tensor_tensor(out=ys[s][:], in0=ys[s][:], in1=xs[s][:], op=ALU.add),
             reads=[ys_b[s], xs_b[s]], writes=[ys_b[s]])
        P.dma(P.sp, out_dram[t * 128:(t + 1) * 128, :], ys[s][:], reads=[ys_b[s]], writes=[], sem_buf=ys_b[s])


def load_consts(P, es, ident_d):
    nc = P.nc
    ident = es.enter_context(nc.sbuf_tensor("ident_sb", [128, 128], BF16))
    ident_b = P.buf("ident")
    P.dma(P.pool, ident[:], ident_d[:, :], writes=[ident_b], sem_buf=ident_b)
    ones = es.enter_context(nc.sbuf_tensor("ones_sb", [128, 128], BF16))
    ones_b = P.buf("ones")
    P.op(P.dve, lambda e: e.memset(ones[:], 1.0), writes=[ones_b])
    return ident, ident_b, ones, ones_b


def na_chunks(m):
    if m < 2:
        return list(range(m, m + 6))
    if m >= 14:
        return list(range(m - 1, m + 5))
    return list(range(m, m + 5))


def build_na(nheads=NH):
    nc = bass.Bass("TRN2", target_bir_lowering=False)
    xh = nc.dram_tensor("xh", [TH, D], F32, kind="ExternalInput").ap()
    w_in = nc.dram_tensor("w_in", [NH, 128, 16, 512], F32, kind="ExternalInput").ap()
    w_out = nc.dram_tensor("w_out", [128, 16, D], F32, kind="ExternalInput").ap()
    gpre_d = nc.dram_tensor("gpre", [128, D], F32, kind="ExternalInput").ap()
    gpost_d = nc.dram_tensor("gpost", [128, D], F32, kind="ExternalInput").ap()
    bint_d = nc.dram_tensor("bint", [NH, 128, 5 * 128], F32, kind="ExternalInput").ap()
    bedge_d = nc.dram_tensor("bedge", [NH, 128, 4, 6 * 128], F32, kind="ExternalInput").ap()
    ident_d = nc.dram_tensor("ident", [128, 128], F32, kind="ExternalInput").ap()
    xout = nc.dram_tensor("xout", [TOK, D], F32, kind="ExternalOutput").ap()
    gscr = nc.dram_tensor("gscr", [D, TOK], BF16, kind="Internal").ap()
    gscr_b = [Buf(f"gscr{h}") for h in range(NH)]
    scale = float(128 ** -0.5)

    P = Prog(nc)
    with ExitStack() as es0:
        bk = Banks(P, es0)
        ident, ident_b, ones, ones_b = load_consts(P, es0, ident_d)
        with ExitStack() as es1:
            hT = es1.enter_context(nc.sbuf_tensor("hT", [128, 16, TH], BF16))
            hT_bufs = P.bufs("hT", TH // 128)
            with ExitStack() as es:
                gpre_t = es.enter_context(nc.sbuf_tensor("gpre_t", [128, D], F32))
                gpre_b = P.buf("gpre")
                P.dma(P.sp, gpre_t[:], gpre_d[:, :], writes=[gpre_b], sem_buf=gpre_b)
                phase_hT(P, es, bk, xh, 0, TH // 128, gpre_t, gpre_b, ident, ident_b, hT, hT_bufs)
            P.barrier()
            with ExitStack() as es:
                NW = 2
                wb = [es.enter_context(nc.sbuf_tensor(f"wb{i}", [128, 16, 512], BF16)) for i in range(NW)]
                wb_b = P.bufs("wb", NW)
                NHB = 2
                qT = [es.enter_context(nc.sbuf_tensor(f"qT{i}", [128, TOK], BF16)) for i in range(NHB)]
                kT = [es.enter_context(nc.sbuf_tensor(f"kT{i}", [128, TH], BF16)) for i in range(NHB)]
                gz = [es.enter_context(nc.sbuf_tensor(f"gz{i}", [128, TOK], BF16)) for i in range(NHB)]
                V = [es.enter_context(nc.sbuf_tensor(f"V{i}", [128, TH // 128, 128], BF16)) for i in range(NHB)]
                qT_b = P.bufs("qT", NHB); kT_b = P.bufs("kT", NHB); gz_b = P.bufs("gz", NHB); V_b = P.bufs("V", NHB)
                gTh = [es.enter_context(nc.sbuf_tensor(f"gTh{i}", [128, TOK], BF16)) for i in range(2)]
                gTh_b = P.bufs("gTh", 2)
                bint = [es.enter_context(nc.sbuf_tensor(f"bint{i}", [128, 5 * 128], F32)) for i in range(2)]
                bint_b = P.bufs("bint", 2)
                bedge = es.enter_context(nc.sbuf_tensor("bedge_sb", [128, 4, 6 * 128], F32))
                bedge_b = P.buf("bedge")
                NSB = 2
                Sb = [es.enter_context(nc.sbuf_tensor(f"Sb{i}", [128, 6 * 128], F32)) for i in range(NSB)]
                Sb_b = P.bufs("Sb", NSB)
                PT = [es.enter_context(nc.sbuf_tensor(f"PT{i}", [128, 6 * 128], BF16)) for i in range(NSB)]
                PT_b = P.bufs("PT", NSB)
                rsb = [es.enter_context(nc.sbuf_tensor(f"rsb{i}", [128, 128], F32)) for i in range(2)]
                rsb_b = P.bufs("rsb", 2)
                o1 = [es.enter_context(nc.sbuf_tensor(f"o1{i}", [128, 128], F32)) for i in range(2)]
                o1_b = P.bufs("o1", 2)

                def load_w(h):
                    s = h % NW
                    for g in range(4):
                        P.dma(P.pool, wb[s][:, 4 * g:4 * g + 4, :], w_in[h, :, 4 * g:4 * g + 4, :], writes=[wb_b[s]], sem_buf=wb_b[s])

                def load_bias(h):
                    P.dma(P.sp, bint[h % 2][:], bint_d[h], writes=[bint_b[h % 2]], sem_buf=bint_b[h % 2])
                    P.dma(P.sp, bedge[:], bedge_d[h], writes=[bedge_b], sem_buf=bedge_b)

                load_w(0)
                for h in range(nheads):
                    s = h % NW
                    hs = h % NHB
                    if h + 1 < nheads:
                        load_w(h + 1)
                    load_bias(h)
                    w = wb[s]
                    allh = list(hT_bufs)
                    own = hT_bufs[2:18]
                    ev = 0
                    for (j, dst, dst_b, ntok, tok0, hb_list, kind) in (
                            (0, qT[hs], qT_b[hs], TOK, HALO, own, "q"),
                            (1, kT[hs], kT_b[hs], TH, 0, allh, "k"),
                            (3, gz[hs], gz_b[hs], TOK, HALO, own, "z")):
                        for blk in range(ntok // 512):
                            bi = bk.next()
                            c0 = tok0 + blk * 512
                            pairs = [(w[:, kc, j * 128:(j + 1) * 128], hT[:, kc, c0:c0 + 512]) for kc in range(16)]
                            mm_group(P, bk.f32(bi), [bk.b[bi]], pairs, reads=hT_bufs[c0 // 128:c0 // 128 + 4] + [wb_b[s]])
                            dap = dst[:, blk * 512:(blk + 1) * 512]
                            if kind == "z":
                                P.op(P.act, lambda e: e.activation(out=dap, in_=bk.f32(bi), func=AF.Silu),
                                     reads=[bk.b[bi]], writes=[dst_b])
                            elif ev % 2 == 0:
                                P.op(P.act, lambda e: e.copy(out=dap, in_=bk.f32(bi)), reads=[bk.b[bi]], writes=[dst_b])
                            else:
                                P.op(P.dve, lambda e: e.tensor_copy(out=dap, in_=bk.f32(bi)), reads=[bk.b[bi]], writes=[dst_b])
                            ev += 1
                    for tg in range(TH // 512):
                        bi = bk.next()
                        for i in range(4):
                            t = tg * 4 + i
                            pairs = [(hT[:, kc, t * 128:(t + 1) * 128], w[:, kc, 256:384]) for kc in range(16)]
                            mm_group(P, bk.f32(bi)[:, i * 128:(i + 1) * 128], [bk.b[bi]], pairs, reads=allh + [wb_b[s]],
                                     signal_last=(i == 3))
                        dap = V[hs][:, tg * 4:(tg + 1) * 4, :]
                        sap = bk.f32(bi).rearrange("p (a b) -> p a b", a=4)
                        if tg % 2 == 0:
                            P.op(P.dve, lambda e: e.tensor_copy(out=dap, in_=sap), reads=[bk.b[bi]], writes=[V_b[hs]])
                        else:
                            P.op(P.act, lambda e: e.copy(out=dap, in_=sap), reads=[bk.b[bi]], writes=[V_b[hs]])

                    SB = [(0, 1), (2, 3)]
                    OB = [(4, 5), (6, 7)]

                    def issue_S(m):
                        cl = na_chunks(m)
                        b0, b1 = SB[m % 2]
                        for ci, c in enumerate(cl):
                            bi = b0 if ci < 4 else b1
                            col = (ci % 4) * 128
                            P.op(P.pe, lambda e, c=c, bi=bi, col=col: e.matmul(
                                bk.f32(bi)[:, col:col + 128], kT[hs][:, c * 128:(c + 1) * 128],
                                qT[hs][:, m * 128:(m + 1) * 128], start=True, stop=True),
                                reads=[kT_b[hs], qT_b[hs]], writes=[bk.b[b0], bk.b[b1]], signal=(ci == len(cl) - 1))

                    def softmax(m):
                        cl = na_chunks(m)
                        n = len(cl) * 128
                        b0, b1 = SB[m % 2]
                        sps = bk.f32(b0, 2)[:, 0:n]
                        if m < 2:
                            bt, btb = bedge[:, m, 0:n], bedge_b
                        elif m >= 14:
                            bt, btb = bedge[:, m - 12, 0:n], bedge_b
                        else:
                            bt, btb = bint[h % 2][:, 0:n], bint_b[h % 2]
                        sl = m % NSB
                        P.op(P.dve, lambda e: e.scalar_tensor_tensor(out=Sb[sl][:, 0:n], in0=sps, scalar=scale, in1=bt,
                                                                     op0=ALU.mult, op1=ALU.add),
                             reads=[bk.b[b0], bk.b[b1], btb], writes=[Sb_b[sl]])
                        P.op(P.act, lambda e: e.activation(out=PT[sl][:, 0:n], in_=Sb[sl][:, 0:n], func=AF.Exp),
                             reads=[Sb_b[sl]], writes=[PT_b[sl]])

                    def issue_O(m):
                        cl = na_chunks(m)
                        sl = m % NSB
                        ob, mb = OB[m % 2]
                        pairs = [(V[hs][:, c, :], PT[sl][:, ci * 128:(ci + 1) * 128]) for ci, c in enumerate(cl)]
                        mm_group(P, bk.f32(ob)[:, 0:128], [bk.b[ob]], pairs, reads=[V_b[hs], PT_b[sl]])
                        pairs = [(ones[:], PT[sl][:, ci * 128:(ci + 1) * 128]) for ci, c in enumerate(cl)]
                        mm_group(P, bk.f32(mb)[:, 0:128], [bk.b[mb]], pairs, reads=[ones_b, PT_b[sl]])

                    def fin(m):
                        ob, mb = OB[m % 2]
                        i2 = m % 2
                        P.op(P.dve, lambda e: e.reciprocal(out=rsb[i2][:], in_=bk.f32(mb)[:, 0:128]),
                             reads=[bk.b[mb]], writes=[rsb_b[i2]])
                        P.op(P.dve, lambda e: e.tensor_tensor(out=o1[i2][:], in0=bk.f32(ob)[:, 0:128], in1=rsb[i2][:], op=ALU.mult),
                             reads=[bk.b[ob], rsb_b[i2]], writes=[o1_b[i2]])
                        P.op(P.pool, lambda e: e.tensor_tensor(out=gTh[h % 2][:, m * 128:(m + 1) * 128], in0=o1[i2][:],
                                                               in1=gz[hs][:, m * 128:(m + 1) * 128], op=ALU.mult),
                             reads=[o1_b[i2], gz_b[hs]], writes=[gTh_b[h % 2]])

                    issue_S(0)
                    for m in range(16):
                        softmax(m)
                        if m + 1 < 16:
                            issue_S(m + 1)
                        issue_O(m)
                        fin(m)
                    P.dma(P.sp, gscr[h * 128:(h + 1) * 128, :], gTh[h % 2][:], reads=[gTh_b[h % 2]], writes=[gscr_b[h]], sem_buf=gTh_b[h % 2])
            P.barrier()
        with ExitStack() as es:
            gT = es.enter_context(nc.sbuf_tensor("gT", [128, 16, TOK], BF16))
            gT_bufs = P.bufs("gT", 4)
            wo = es.enter_context(nc.sbuf_tensor("wo", [128, 16, D], BF16))
            wo_b = P.buf("wo")
            gpost_t = es.enter_context(nc.sbuf_tensor("gpost_t", [128, D], F32))
            gpost_b = P.buf("gpost")
            P.dma(P.sp, gpost_t[:], gpost_d[:, :], writes=[gpost_b], sem_buf=gpost_b)
            for kc in range(16):
                P.dma(P.sp, gT[:, kc, :], gscr[kc * 128:(kc + 1) * 128, :], reads=[gscr_b[kc]], writes=[gT_bufs[kc // 4]], sem_buf=gT_bufs[kc // 4])
            for g in range(8):
                P.dma(P.pool, wo[:, 2 * g:2 * g + 2, :], w_out[:, 2 * g:2 * g + 2, :], writes=[wo_b], sem_buf=wo_b)
            phase_out(P, es, bk, gT, gT_bufs, wo, wo_b, gpost_t, gpost_b, xh, HALO, xout, TOK // 128)
            P.finish()
    return nc


def na_bias_tiles(rpb):
    H = rpb.shape[0]
    rows = 128
    out = []
    kc = np.arange(64)
    qc = np.arange(64)
    col_start = np.clip(qc - 8, 0, 48)
    col_ok = (kc[:, None] >= col_start[None, :]) & (kc[:, None] < col_start[None, :] + 16)
    dc = np.clip(kc[:, None] - qc[None, :] + 15, 0, 30)

    def tile_for(Mg, chunks_g):
        t = np.full((H, 128, len(chunks_g), 128), NEG, np.float32)
        for ci, cg in enumerate(chunks_g):
            for kr2 in range(2):
                kr = 2 * cg + kr2
                if kr < 0 or kr >= rows:
                    continue
                for qr2 in range(2):
                    r = 2 * Mg + qr2
                    rs = min(max(r - 4, 0), rows - 8)
                    if not (rs <= kr < rs + 8):
                        continue
                    dr = kr - r + 7
                    vals = rpb[:, dr, :][:, dc]
                    vals = np.where(col_ok[None], vals, np.float32(NEG))
                    t[:, kr2 * 64:(kr2 + 1) * 64, ci, qr2 * 64:(qr2 + 1) * 64] = vals
        return t

    for q in range(4):
        Mi = 16 * q + 8
        bint = tile_for(Mi, [Mi - 2 + i for i in range(5)]).reshape(H, 128, 640)
        be = np.empty((H, 128, 4, 768), np.float32)
        for idx, m in enumerate((0, 1, 14, 15)):
            Mg = 16 * q + m
            cl = na_chunks(m)
            be[:, :, idx, :] = tile_for(Mg, [16 * q + (c - 2) for c in cl]).reshape(H, 128, 768)
        out.append((np.ascontiguousarray(bint), np.ascontiguousarray(be)))
    return out


def run_na_layer(x, g_pre, g_post, w_in, rpb, w_out, nheads=NH):
    B, L, _ = x.shape
    nc = build_na(nheads)
    w_in_r = np.ascontiguousarray(w_in.reshape(16, 128, 4, NH, 128).transpose(3, 1, 0, 2, 4).reshape(NH, 128, 16, 512))
    w_out_r = np.ascontiguousarray(w_out.reshape(16, 128, D).transpose(1, 0, 2))
    gpre = np.ascontiguousarray(np.broadcast_to(g_pre[None, :], (128, D)))
    gpost = np.ascontiguousarray(np.broadcast_to(g_post[None, :], (128, D)))
    ident = np.eye(128, dtype=np.float32)
    tiles = na_bias_tiles(rpb)
    in_maps = []
    for c in range(NCORES):
        b, q = divmod(c, 4)
        xh = np.zeros((TH, D), np.float32)
        lo = q * TOK - HALO
        hi = lo + TH
        slo, shi = max(lo, 0), min(hi, L)
        xh[slo - lo: shi - lo] = x[b, slo:shi]
        in_maps.append({"xh": xh, "w_in": w_in_r, "w_out": w_out_r, "gpre": gpre, "gpost": gpost,
                        "bint": tiles[q][0], "bedge": tiles[q][1], "ident": ident})
    res = run_bass_kernel_spmd(nc, in_maps, core_ids=list(range(NCORES)))
    out = np.empty_like(x)
    for c in range(NCORES):
        b, q = divmod(c, 4)
        out[b, q * TOK:(q + 1) * TOK] = res.results[c]["xout"]
    return out


def rope_combine(P, bk, bA, bB, cosT, sinT, c_b, dst_ap, dst_b, t1, t1_b, t2, t2_b, c0, n=512):
    P.op(P.dve, lambda e: e.tensor_tensor(out=t1[:, 0:n], in0=bk.f32(bA)[0:64, 0:n], in1=cosT[:, c0:c0 + n], op=ALU.mult),
         reads=[bk.b[bA], c_b], writes=[t1_b])
    P.op(P.dve, lambda e: e.tensor_tensor(out=t2[:, 0:n], in0=bk.f32(bB)[0:64, 0:n], in1=sinT[:, c0:c0 + n], op=ALU.mult),
         reads=[bk.b[bB], c_b], writes=[t2_b])
    P.op(P.pool, lambda e: e.tensor_tensor(out=dst_ap, in0=t1[:, 0:n], in1=t2[:, 0:n], op=ALU.add),
         reads=[t1_b, t2_b], writes=[dst_b])


def build_mla_a():
    nc = bass.Bass("TRN2", target_bir_lowering=False)
    x_d = nc.dram_tensor("x", [TOK, D], F32, kind="ExternalInput").ap()
    gpre_d = nc.dram_tensor("gpre", [128, D], F32, kind="ExternalInput").ap()
    wlat_d = nc.dram_tensor("wlat", [128, 16, 1088], F32, kind="ExternalInput").ap()
    wz_d = nc.dram_tensor("wz", [NH, 128, 16, 128], F32, kind="ExternalInput").ap()
    wq_d = nc.dram_tensor("wq", [128, 4, 3072], F32, kind="ExternalInput").ap()
    wkv_d = nc.dram_tensor("wkv", [128, 4, 4096], F32, kind="ExternalInput").ap()
    gq_d = nc.dram_tensor("gqkv", [128, 1024], F32, kind="ExternalInput").ap()
    cos_d = nc.dram_tensor("cosT", [64, TOK], F32, kind="ExternalInput").ap()
    sin_d = nc.dram_tensor("sinT", [64, TOK], F32, kind="ExternalInput").ap()
    ident_d = nc.dram_tensor("ident", [128, 128], F32, kind="ExternalInput").ap()
    QN = nc.dram_tensor("QN", [NH, 128, TOK], BF16, kind="ExternalOutput").ap()
    QR = nc.dram_tensor("QR", [NH, 64, TOK], BF16, kind="ExternalOutput").ap()
    KN = nc.dram_tensor("KN", [NH, 128, TOK], BF16, kind="ExternalOutput").ap()
    KR = nc.dram_tensor("KR", [64, TOK], BF16, kind="ExternalOutput").ap()
    VR = nc.dram_tensor("VR", [NH, 128, 16 * 128], BF16, kind="ExternalOutput").ap()
    GZ = nc.dram_tensor("GZ", [NH, 128, TOK], BF16, kind="ExternalOutput").ap()

    P = Prog(nc)
    with ExitStack() as es0:
        bk = Banks(P, es0)
        ident, ident_b, ones, ones_b = load_consts(P, es0, ident_d)
        cnT = es0.enter_context(nc.sbuf_tensor("cnT", [128, 8, TOK], BF16))
        cnT_b = P.bufs("cnT", 16)
        cosT = es0.enter_context(nc.sbuf_tensor("cos_sb", [64, TOK], F32))
        sinT = es0.enter_context(nc.sbuf_tensor("sin_sb", [64, TOK], F32))
        cs_b = P.buf("cossin")
        P.dma(P.sp, cosT[:], cos_d[:, :], writes=[cs_b], sem_buf=cs_b)
        P.dma(P.sp, sinT[:], sin_d[:, :], writes=[cs_b], sem_buf=cs_b)
        t1 = es0.enter_context(nc.sbuf_tensor("rope_t1", [64, 512], F32))
        t2 = es0.enter_context(nc.sbuf_tensor("rope_t2", [64, 512], F32))
        t1_b = P.buf("t1"); t2_b = P.buf("t2")
        with ExitStack() as es1:
            hT = es1.enter_context(nc.sbuf_tensor("hT", [128, 16, TOK], BF16))
            hT_bufs = P.bufs("hT", 16)
            with ExitStack() as es:
                gpre_t = es.enter_context(nc.sbuf_tensor("gpre_t", [128, D], F32))
                gpre_b = P.buf("gpre")
                P.dma(P.sp, gpre_t[:], gpre_d[:, :], writes=[gpre_b], sem_buf=gpre_b)
                phase_hT(P, es, bk, x_d, 0, 16, gpre_t, gpre_b, ident, ident_b, hT, hT_bufs)
            P.barrier()
            with ExitStack() as es:
                wlat = es.enter_context(nc.sbuf_tensor("wlat_sb", [128, 16, 1024], BF16))
                wlat_b = P.buf("wlat")
                wkr = es.enter_context(nc.sbuf_tensor("wkr_sb", [128, 16, 64], BF16))
                wkr_b = P.buf("wkr")
                wkrr = es.enter_context(nc.sbuf_tensor("wkrr_sb", [128, 16, 64], BF16))
                wkrr_b = P.buf("wkrr")
                gq_t = es.enter_context(nc.sbuf_tensor("gq_t", [128, 1024], F32))
                gq_b = P.buf("gq")
                P.dma(P.sp, gq_t[:], gq_d[:, :], writes=[gq_b], sem_buf=gq_b)
                for g in range(4):
                    for half in range(2):
                        P.dma(P.pool, wlat[:, 4 * g:4 * g + 4, half * 512:(half + 1) * 512],
                              wlat_d[:, 4 * g:4 * g + 4, half * 512:(half + 1) * 512], writes=[wlat_b], sem_buf=wlat_b)
                P.dma(P.pool, wkr[:], wlat_d[:, :, 1024:1088], writes=[wkr_b], sem_buf=wkr_b)
                wv = wkr[:].rearrange("p k (i two) -> p k i two", two=2)
                rv = wkrr[:].rearrange("p k (i two) -> p k i two", two=2)
                P.op(P.act, lambda e: e.mul(out=rv[:, :, :, 0], in_=wv[:, :, :, 1], mul=-1.0), reads=[wkr_b], writes=[wkrr_b])
                P.op(P.act, lambda e: e.copy(out=rv[:, :, :, 1], in_=wv[:, :, :, 0]), reads=[wkr_b], writes=[wkrr_b])
                junk = es.enter_context(nc.sbuf_tensor("a_junk", [128, 512], BF16))
                junk_b = P.buf("a_junk")
                st = es.enter_context(nc.sbuf_tensor("a_st", [128, 8], F32))
                ss_b = P.bufs("a_ss", 2); rs_b = P.bufs("a_rs", 2)
                cn = [es.enter_context(nc.sbuf_tensor(f"a_cn{i}", [128, 1024], BF16)) for i in range(2)]
                cn_b = P.bufs("a_cn", 2)
                for t in range(16):
                    s = t % 2
                    bq = bk.next(); bkv = bk.next()
                    for (bi, c0) in ((bq, 0), (bkv, 512)):
                        pairs = [(hT[:, kc, t * 128:(t + 1) * 128], wlat[:, kc, c0:c0 + 512]) for kc in range(16)]
                        mm_group(P, bk.f32(bi), [bk.b[bi]], pairs, reads=[hT_bufs[t], wlat_b])
                    ss_ap = st[:, 4 * s:4 * s + 2]
                    rs_ap = st[:, 4 * s + 2:4 * s + 4]
                    P.op(P.act, lambda e: e.activation(out=junk[:], in_=bk.f32(bq), func=AF.Square, accum_out=ss_ap[:, 0:1]),
                         reads=[bk.b[bq]], writes=[junk_b, ss_b[s]])
                    P.op(P.act, lambda e: e.activation(out=junk[:], in_=bk.f32(bkv), func=AF.Square, accum_out=ss_ap[:, 1:2]),
                         reads=[bk.b[bkv]], writes=[junk_b, ss_b[s]])
                    rstd_from_ss(P, rs_ap, ss_ap, 512, [rs_b[s]], [ss_b[s]])
                    P.op(P.dve, lambda e: e.scalar_tensor_tensor(out=cn[s][:, 0:512], in0=bk.f32(bq), scalar=rs_ap[:, 0:1],
                                                                 in1=gq_t[:, 0:512], op0=ALU.mult, op1=ALU.mult),
                         reads=[bk.b[bq], rs_b[s], gq_b], writes=[cn_b[s]])
                    P.op(P.dve, lambda e: e.scalar_tensor_tensor(out=cn[s][:, 512:1024], in0=bk.f32(bkv), scalar=rs_ap[:, 1:2],
                                                                 in1=gq_t[:, 512:1024], op0=ALU.mult, op1=ALU.mult),
                         reads=[bk.b[bkv], rs_b[s], gq_b], writes=[cn_b[s]])
                    bt = bk.next()
                    pst = bk.bf16(bt)
                    for j in range(8):
                        P.op(P.pe, lambda e, j=j: e.transpose(pst[:, j * 128:(j + 1) * 128], cn[s][:, j * 128:(j + 1) * 128], ident[:]),
                             reads=[cn_b[s], ident_b], writes=[bk.b[bt]], signal=(j == 7))
                    P.op(P.act, lambda e: e.copy(out=cnT[:, :, t * 128:(t + 1) * 128], in_=pst.rearrange("p (k n) -> p k n", k=8)),
                         reads=[bk.b[bt]], writes=[cnT_b[t]])
                krT = es.enter_context(nc.sbuf_tensor("krT", [64, TOK], BF16))
                krT_b = P.buf("krT")
                for blk in range(4):
                    bA = bk.next(); bB = bk.next()
                    for (bi, wsrc, wsb) in ((bA, wkr, wkr_b), (bB, wkrr, wkrr_b)):
                        pairs = [(wsrc[:, kc, :], hT[:, kc, blk * 512:(blk + 1) * 512]) for kc in range(16)]
                        mm_group(P, bk.f32(bi)[0:64, :], [bk.b[bi]], pairs, reads=hT_bufs[blk * 4:blk * 4 + 4] + [wsb])
                    rope_combine(P, bk, bA, bB, cosT, sinT, cs_b, krT[:, blk * 512:(blk + 1) * 512], krT_b, t1, t1_b, t2, t2_b, blk * 512)
                P.dma(P.sp, KR[:, :], krT[:], reads=[krT_b], sem_buf=krT_b)
                wz = [es.enter_context(nc.sbuf_tensor(f"wz{i}", [128, 16, 128], BF16)) for i in range(2)]
                wz_b = P.bufs("wz", 2)
                gz = [es.enter_context(nc.sbuf_tensor(f"gz{i}", [128, TOK], BF16)) for i in range(2)]
                gz_b = P.bufs("gz", 2)
                P.dma(P.pool, wz[0][:], wz_d[0], writes=[wz_b[0]], sem_buf=wz_b[0])
                for h in range(NH):
                    s = h % 2
                    if h + 1 < NH:
                        P.dma(P.pool, wz[(h + 1) % 2][:], wz_d[h + 1], writes=[wz_b[(h + 1) % 2]], sem_buf=wz_b[(h + 1) % 2])
                    for blk in range(4):
                        bi = bk.next()
                        pairs = [(wz[s][:, kc, :], hT[:, kc, blk * 512:(blk + 1) * 512]) for kc in range(16)]
                        mm_group(P, bk.f32(bi), [bk.b[bi]], pairs, reads=hT_bufs[blk * 4:blk * 4 + 4] + [wz_b[s]])
                        P.op(P.act, lambda e: e.activation(out=gz[s][:, blk * 512:(blk + 1) * 512], in_=bk.f32(bi), func=AF.Silu),
                             reads=[bk.b[bi]], writes=[gz_b[s]])
                    P.dma(P.sp, GZ[h], gz[s][:], reads=[gz_b[s]], sem_buf=gz_b[s])
            P.barrier()
        with ExitStack() as es:
            wq = es.enter_context(nc.sbuf_tensor("wq_sb", [128, 4, 3072], BF16))
            wq_b = P.buf("wq")
            wqr = es.enter_context(nc.sbuf_tensor("wqr_sb", [128, 4, NH, 64], BF16))
            wqr_b = P.buf("wqr")
            wkv = es.enter_context(nc.sbuf_tensor("wkv_sb", [128, 4, 4096], BF16))
            wkv_b = P.buf("wkv")
            for kc in range(4):
                for j in range(6):
                    P.dma(P.pool, wq[:, kc, j * 512:(j + 1) * 512], wq_d[:, kc, j * 512:(j + 1) * 512], writes=[wq_b], sem_buf=wq_b)
            for kc in range(4):
                for j in range(8):
                    P.dma(P.pool, wkv[:, kc, j * 512:(j + 1) * 512], wkv_d[:, kc, j * 512:(j + 1) * 512], writes=[wkv_b], sem_buf=wkv_b)
            for kc in range(4):
                src = wq[:, kc, :].rearrange("p (h c) -> p h c", h=NH)[:, :, 128:192].rearrange("p h (i two) -> p h i two", two=2)
                dst = wqr[:, kc, :, :].rearrange("p h (i two) -> p h i two", two=2)
                P.op(P.act, lambda e: e.mul(out=dst[:, :, :, 0], in_=src[:, :, :, 1], mul=-1.0), reads=[wq_b], writes=[wqr_b])
                P.op(P.act, lambda e: e.copy(out=dst[:, :, :, 1], in_=src[:, :, :, 0]), reads=[wq_b], writes=[wqr_b])
            Vown = es.enter_context(nc.sbuf_tensor("Vown", [128, NH, 16, 128], BF16))
            Vown_b = P.buf("Vown")
            qn = [es.enter_context(nc.sbuf_tensor(f"qn{i}", [128, TOK], BF16)) for i in range(2)]
            qn_b = P.bufs("qn", 2)
            kn = [es.enter_context(nc.sbuf_tensor(f"kn{i}", [128, TOK], BF16)) for i in range(2)]
            kn_b = P.bufs("kn", 2)
            qr = [es.enter_context(nc.sbuf_tensor(f"qr{i}", [64, TOK], BF16)) for i in range(2)]
            qr_b = P.bufs("qr", 2)
            allcn = list(cnT_b)
            for t in range(16):
                for hg in range(4):
                    bi = bk.next()
                    pairs = [(cnT[:, 4 + kc, t * 128:(t + 1) * 128], wkv[:, kc, 2048 + hg * 512:2048 + (hg + 1) * 512]) for kc in range(4)]
                    mm_group(P, bk.f32(bi), [bk.b[bi]], pairs, reads=[cnT_b[t], wkv_b])
                    dst = Vown[:, hg * 4:(hg + 1) * 4, t, :]
                    src = bk.f32(bi).rearrange("p (a d) -> p a d", a=4)
                    if (t * 4 + hg) % 2 == 0:
                        P.op(P.dve, lambda e: e.tensor_copy(out=dst, in_=src), reads=[bk.b[bi]], writes=[Vown_b])
                    else:
                        P.op(P.act, lambda e: e.copy(out=dst, in_=src), reads=[bk.b[bi]], writes=[Vown_b])
            for h in range(NH):
                P.dma(P.sp, VR[h], Vown[:, h, :, :].rearrange("p t d -> p (t d)"), reads=[Vown_b], sem_buf=Vown_b)
            for h in range(NH):
                s = h % 2
                for blk in range(4):
                    tb = cnT_b[blk * 4:blk * 4 + 4]
                    bi = bk.next()
                    pairs = [(wq[:, kc, h * 192:h * 192 + 128], cnT[:, kc, blk * 512:(blk + 1) * 512]) for kc in range(4)]
                    mm_group(P, bk.f32(bi), [bk.b[bi]], pairs, reads=tb + [wq_b])
                    P.op(P.act, lambda e: e.copy(out=qn[s][:, blk * 512:(blk + 1) * 512], in_=bk.f32(bi)), reads=[bk.b[bi]], writes=[qn_b[s]])
                    bi = bk.next()
                    pairs = [(wkv[:, kc, h * 128:(h + 1) * 128], cnT[:, 4 + kc, blk * 512:(blk + 1) * 512]) for kc in range(4)]
                    mm_group(P, bk.f32(bi), [bk.b[bi]], pairs, reads=tb + [wkv_b])
                    P.op(P.dve, lambda e: e.tensor_copy(out=kn[s][:, blk * 512:(blk + 1) * 512], in_=bk.f32(bi)), reads=[bk.b[bi]], writes=[kn_b[s]])
                    bA = bk.next(); bB = bk.next()
                    pairs = [(wq[:, kc, h * 192 + 128:h * 192 + 192], cnT[:, kc, blk * 512:(blk + 1) * 512]) for kc in range(4)]
                    mm_group(P, bk.f32(bA)[0:64, :], [bk.b[bA]], pairs, reads=tb + [wq_b])
                    pairs = [(wqr[:, kc, h, :], cnT[:, kc, blk * 512:(blk + 1) * 512]) for kc in range(4)]
                    mm_group(P, bk.f32(bB)[0:64, :], [bk.b[bB]], pairs, reads=tb + [wqr_b])
                    rope_combine(P, bk, bA, bB, cosT, sinT, cs_b, qr[s][:, blk * 512:(blk + 1) * 512], qr_b[s], t1, t1_b, t2, t2_b, blk * 512)
                P.dma(P.sp, QN[h], qn[s][:], reads=[qn_b[s]], sem_buf=qn_b[s])
                P.dma(P.sp, KN[h], kn[s][:], reads=[kn_b[s]], sem_buf=kn_b[s])
                P.dma(P.sp, QR[h], qr[s][:], reads=[qr_b[s]], sem_buf=qr_b[s])
            P.finish()
    return nc


LK = 8192


def build_mla_b(nheads=NH):
    nc = bass.Bass("TRN2", target_bir_lowering=False)
    x_d = nc.dram_tensor("x", [TOK, D], F32, kind="ExternalInput").ap()
    QN = nc.dram_tensor("QN", [NH, 128, TOK], BF16, kind="ExternalInput").ap()
    QR = nc.dram_tensor("QR", [NH, 64, TOK], BF16, kind="ExternalInput").ap()
    GZ = nc.dram_tensor("GZ", [NH, 128, TOK], BF16, kind="ExternalInput").ap()
    KN = nc.dram_tensor("KN", [NH, 128, LK], BF16, kind="ExternalInput").ap()
    KR = nc.dram_tensor("KR", [64, LK], BF16, kind="ExternalInput").ap()
    VR = nc.dram_tensor("VR", [NH, 128, LK], BF16, kind="ExternalInput").ap()
    w_out = nc.dram_tensor("w_out", [128, 16, D], F32, kind="ExternalInput").ap()
    gpost_d = nc.dram_tensor("gpost", [128, D], F32, kind="ExternalInput").ap()
    ident_d = nc.dram_tensor("ident", [128, 128], F32, kind="ExternalInput").ap()
    xout = nc.dram_tensor("xout", [TOK, D], F32, kind="ExternalOutput").ap()
    gscr = nc.dram_tensor("gscr", [D, TOK], BF16, kind="Internal").ap()
    gscr_b = [Buf(f"gscr{h}") for h in range(NH)]
    scale = float(192 ** -0.5)
    NKC = LK // 128

    P = Prog(nc)
    with ExitStack() as es0:
        bk = Banks(P, es0)
        ident, ident_b, ones, ones_b = load_consts(P, es0, ident_d)
        with ExitStack() as es:
            kr = es.enter_context(nc.sbuf_tensor("kr_sb", [64, LK], BF16))
            kr_b = P.buf("kr")
            P.dma(P.sp, kr[:], KR[:, :], writes=[kr_b], sem_buf=kr_b)
            NS = 2
            kn = [es.enter_context(nc.sbuf_tensor(f"kn{i}", [128, LK], BF16)) for i in range(NS)]
            vv = [es.enter_context(nc.sbuf_tensor(f"vv{i}", [128, NKC, 130], BF16)) for i in range(NS)]
            qn = [es.enter_context(nc.sbuf_tensor(f"qn{i}", [128, TOK], BF16)) for i in range(NS)]
            qr = [es.enter_context(nc.sbuf_tensor(f"qr{i}", [64, TOK], BF16)) for i in range(NS)]
            gz = [es.enter_context(nc.sbuf_tensor(f"gz{i}", [128, TOK], BF16)) for i in range(NS)]
            kn_b = P.bufs("kn", NS); vv_b = P.bufs("vv", NS); qn_b = P.bufs("qn", NS); qr_b = P.bufs("qr", NS); gz_b = P.bufs("gz", NS)
            for i in range(NS):
                P.op(P.dve, lambda e: e.memset(vv[i][:, :, 128:130], 1.0), writes=[vv_b[i]])
            gTh = [es.enter_context(nc.sbuf_tensor(f"gTh{i}", [128, TOK], BF16)) for i in range(2)]
            gTh_b = P.bufs("gTh", 2)
            NPT = 4
            PT = [es.enter_context(nc.sbuf_tensor(f"PT{i}", [128, 512], BF16)) for i in range(NPT)]
            PT_b = P.bufs("PT", NPT)
            on = [es.enter_context(nc.sbuf_tensor(f"on{i}", [128, 128], BF16)) for i in range(2)]
            on_b = P.bufs("on", 2)
            rs1 = [es.enter_context(nc.sbuf_tensor(f"rs1{i}", [128, 1], F32)) for i in range(2)]
            rs1_b = P.bufs("rs1", 2)
            NSB = 3
            TBK = 3
            tb_bufs = P.bufs("tbk", 4)
            o_bufs = [[P.buf(f"oacc{p}{j}") for j in range(4)] for p in range(2)]

            def o_ap(p, j):
                bi = 4 + 2 * p + j // 2
                c0 = (j % 2) * 256
                return bk.f32(bi)[:, c0:c0 + 129]

            def load_head(h):
                s = h % NS
                for g in range(4):
                    P.dma(P.sp, kn[s][:, g * 2048:(g + 1) * 2048], KN[h, :, g * 2048:(g + 1) * 2048], writes=[kn_b[s]], sem_buf=kn_b[s])
                for g in range(4):
                    P.dma(P.pool, vv[s][:, g * 16:(g + 1) * 16, 0:128], VR[h, :, g * 2048:(g + 1) * 2048].rearrange("p (k d) -> p k d", d=128),
                          writes=[vv_b[s]], sem_buf=vv_b[s])
                P.dma(P.sp, qn[s][:], QN[h], writes=[qn_b[s]], sem_buf=qn_b[s])
                P.dma(P.sp, qr[s][:], QR[h], writes=[qr_b[s]], sem_buf=qr_b[s])
                P.dma(P.sp, gz[s][:], GZ[h], writes=[gz_b[s]], sem_buf=gz_b[s])

            unit = 0
            fin_i = 0
            load_head(0)
            for h in range(nheads):
                s = h % NS
                if h + 1 < nheads:
                    load_head(h + 1)
                for qb in range(4):
                    p2 = qb % 2
                    q0 = qb * 512

                    def issue_S(kc, u):
                        bi = u % NSB
                        P.op(P.pe, lambda e: e.matmul(bk.f32(bi), kn[s][:, kc * 128:(kc + 1) * 128], qn[s][:, q0:q0 + 512], start=True, stop=False),
                             reads=[kn_b[s], qn_b[s]], writes=[bk.b[bi]], signal=False)
                        P.op(P.pe, lambda e: e.matmul(bk.f32(bi), kr[:, kc * 128:(kc + 1) * 128], qr[s][:, q0:q0 + 512], start=False, stop=True),
                             reads=[kr_b, qr_b[s]], writes=[bk.b[bi]], signal=True)

                    def do_exp(kc, u):
                        bi = u % NSB
                        pt = u % NPT
                        P.op(P.act, lambda e: e.activation(out=PT[pt][:], in_=bk.f32(bi), func=AF.Exp, scale=scale),
                             reads=[bk.b[bi]], writes=[PT_b[pt]])

                    def issue_PV(kc, u):
                        pt = u % NPT
                        for j in range(4):
                            P.op(P.pe, lambda e, j=j: e.matmul(o_ap(p2, j), PT[pt][:, j * 128:(j + 1) * 128], vv[s][:, kc, 0:129],
                                                               start=(kc == 0 and j % 2 == 0), stop=(kc == NKC - 1), skip_group_check=True),
                                 reads=[vv_b[s], PT_b[pt]], writes=[o_bufs[p2][j]], signal=(j == 3))

                    LOOK = 2
                    for kc in range(min(LOOK, NKC)):
                        issue_S(kc, unit + kc)
                    for kc in range(NKC):
                        do_exp(kc, unit + kc)
                        if kc + LOOK < NKC:
                            issue_S(kc + LOOK, unit + kc + LOOK)
                        issue_PV(kc, unit + kc)
                    unit += NKC
                    for j in range(4):
                        f2 = fin_i % 2
                        fin_i += 1
                        oj = o_ap(p2, j)
                        P.op(P.dve, lambda e: e.reciprocal(out=rs1[f2][:], in_=oj[:, 128:129]), reads=[o_bufs[p2][j]], writes=[rs1_b[f2]])
                        P.op(P.dve, lambda e: e.tensor_scalar(out=on[f2][:], in0=oj[:, 0:128], scalar1=rs1[f2][:, 0:1], scalar2=None, op0=ALU.mult),
                             reads=[o_bufs[p2][j], rs1_b[f2]], writes=[on_b[f2]])
                        tps = bk.bf16(TBK)[:, j * 128:(j + 1) * 128]
                        P.op(P.pe, lambda e: e.transpose(tps, on[f2][:], ident[:]), reads=[on_b[f2], ident_b], writes=[tb_bufs[j]])
                        P.op(P.dve, lambda e: e.tensor_tensor(out=gTh[h % 2][:, q0 + j * 128:q0 + (j + 1) * 128], in0=tps,
                                                              in1=gz[s][:, q0 + j * 128:q0 + (j + 1) * 128], op=ALU.mult),
                             reads=[tb_bufs[j], gz_b[s]], writes=[gTh_b[h % 2]])
                P.dma(P.sp, gscr[h * 128:(h + 1) * 128, :], gTh[h % 2][:], reads=[gTh_b[h % 2]], writes=[gscr_b[h]], sem_buf=gTh_b[h % 2])
        P.barrier()
        with ExitStack() as es:
            gT = es.enter_context(nc.sbuf_tensor("gT", [128, 16, TOK], BF16))
            gT_bufs = P.bufs("gT", 4)
            wo = es.enter_context(nc.sbuf_tensor("wo", [128, 16, D], BF16))
            wo_b = P.buf("wo")
            gpost_t = es.enter_context(nc.sbuf_tensor("gpost_t", [128, D], F32))
            gpost_b = P.buf("gpost")
            P.dma(P.sp, gpost_t[:], gpost_d[:, :], writes=[gpost_b], sem_buf=gpost_b)
            for kc in range(16):
                P.dma(P.sp, gT[:, kc, :], gscr[kc * 128:(kc + 1) * 128, :], reads=[gscr_b[kc]], writes=[gT_bufs[kc // 4]], sem_buf=gT_bufs[kc // 4])
            for g in range(8):
                P.dma(P.pool, wo[:, 2 * g:2 * g + 2, :], w_out[:, 2 * g:2 * g + 2, :], writes=[wo_b], sem_buf=wo_b)
            phase_out(P, es, bk, gT, gT_bufs, wo, wo_b, gpost_t, gpost_b, x_d, 0, xout, TOK // 128)
            P.finish()
    return nc


def rope_tables_T(q):
    inv_freq = (1.0 / (np.float32(10000.0) ** (np.arange(0, 64, 2, dtype=np.float32) / np.float32(64)))).astype(np.float32)
    pos = np.arange(q * TOK, (q + 1) * TOK, dtype=np.float32)
    ang = (pos[:, None] * inv_freq[None, :]).astype(np.float32)
    c = np.cos(ang).astype(np.float32)
    s = np.sin(ang).astype(np.float32)
    cT = np.repeat(c.T, 2, axis=0)
    sT = np.repeat(s.T, 2, axis=0)
    return np.ascontiguousarray(cT), np.ascontiguousarray(sT)


def run_mla_layer(x, g_pre, g_post, w_in, q_norm, w_q_b, kv_norm, w_kv_b, w_out, nheads=NH):
    B, L, _ = x.shape
    nca = build_mla_a()
    wlat = np.ascontiguousarray(w_in[:, :1088].reshape(16, 128, 1088).transpose(1, 0, 2))
    wz = np.ascontiguousarray(w_in[:, 1088:].reshape(16, 128, NH, 128).transpose(2, 1, 0, 3))
    wq = np.ascontiguousarray(w_q_b.reshape(4, 128, 3072).transpose(1, 0, 2))
    wkv4 = w_kv_b.reshape(4, 128, NH, 2, 128)
    wkv = np.ascontiguousarray(wkv4.transpose(1, 0, 3, 2, 4).reshape(128, 4, 4096))
    gqkv = np.ascontiguousarray(np.broadcast_to(np.concatenate([q_norm, kv_norm])[None, :], (128, 1024)))
    gpre = np.ascontiguousarray(np.broadcast_to(g_pre[None, :], (128, D)))
    gpost = np.ascontiguousarray(np.broadcast_to(g_post[None, :], (128, D)))
    ident = np.eye(128, dtype=np.float32)
    in_maps = []
    for c in range(NCORES):
        b, q = divmod(c, 4)
        cT, sT = rope_tables_T(q)
        in_maps.append({"x": np.ascontiguousarray(x[b, q * TOK:(q + 1) * TOK]), "gpre": gpre, "wlat": wlat, "wz": wz, "wq": wq,
                        "wkv": wkv, "gqkv": gqkv, "cosT": cT, "sinT": sT, "ident": ident})
    ra = run_bass_kernel_spmd(nca, in_maps, core_ids=list(range(NCORES))).results
    ncb = build_mla_b(nheads)
    w_out_r = np.ascontiguousarray(w_out.reshape(16, 128, D).transpose(1, 0, 2))
    in_maps = []
    for b in range(B):
        KN = np.concatenate([ra[4 * b + q]["KN"] for q in range(4)], axis=2)
        KR = np.concatenate([ra[4 * b + q]["KR"] for q in range(4)], axis=1)
        VR = np.concatenate([ra[4 * b + q]["VR"] for q in range(4)], axis=2)
        for q in range(4):
            r = ra[4 * b + q]
            in_maps.append({"x": np.ascontiguousarray(x[b, q * TOK:(q + 1) * TOK]), "QN": r["QN"], "QR": r["QR"], "GZ": r["GZ"],
                            "KN": KN, "KR": KR, "VR": VR, "w_out": w_out_r, "gpost": gpost, "ident": ident})
    rb = run_bass_kernel_spmd(ncb, in_maps, core_ids=list(range(NCORES))).results
    out = np.empty_like(x)
    for c in range(NCORES):
        b, q = divmod(c, 4)
        out[b, q * TOK:(q + 1) * TOK] = rb[c]["xout"]
    return out


def kernel(x, norm_pre, norm_post, na_w_in, na_rpb, na_w_out, mla_w_in, mla_q_norm, mla_w_q_b, mla_kv_norm,
           mla_w_kv_b, mla_w_out):
    x = np.ascontiguousarray(np.asarray(x, dtype=np.float32))
    f = lambda a: np.asarray(a, dtype=np.float32)
    norm_pre, norm_post = f(norm_pre), f(norm_post)
    for i in range(4):
        j = i // 2
        if i % 2 == 0:
            x = run_na_layer(x, norm_pre[i], norm_post[i], f(na_w_in[j]), f(na_rpb[j]), f(na_w_out[j]))
        else:
            x = run_mla_layer(x, norm_pre[i], norm_post[i], f(mla_w_in[j]), f(mla_q_norm[j]), f(mla_w_q_b[j]),
                              f(mla_kv_norm[j]), f(mla_w_kv_b[j]), f(mla_w_out[j]))
    return x
```

```python
import numpy as np
import ml_dtypes
from contextlib import ExitStack
import concourse.bass as bass
import concourse.mybir as mybir
from concourse.bass_utils import run_bass_kernel_spmd

F32 = mybir.dt.float32
BF16 = mybir.dt.bfloat16
AF = mybir.ActivationFunctionType
ALU = mybir.AluOpType
AX = mybir.AxisListType

D = 2048
NCORES = 8
TOK = 2048
HALO = 256
TH = TOK + 2 * HALO
NH = 16
EPS = 1e-6
NEG = -30000.0


class Buf:
    __slots__ = ("name", "w", "r", "dsem", "dcount")

    def __init__(self, name):
        self.name = name
        self.w = []
        self.r = []
        self.dsem = None
        self.dcount = 0


class Queue:
    def __init__(self, prog, name, eng):
        self.name = name
        self.eng = eng
        self.sem = prog.nc.alloc_semaphore("q_" + name)
        self.count = 0
        self.known = {}
        self.pending = []


class Prog:
    def __init__(self, nc):
        self.nc = nc
        self.pe = Queue(self, "pe", nc.tensor)
        self.act = Queue(self, "act", nc.scalar)
        self.dve = Queue(self, "dve", nc.vector)
        self.pool = Queue(self, "pool", nc.gpsimd)
        self.sp = Queue(self, "sp", nc.sync)
        self.queues = [self.pe, self.act, self.dve, self.pool, self.sp]
        self.dbufs = []
        self.out_events = []

    def buf(self, name):
        return Buf(name)

    def bufs(self, name, n):
        return [Buf(f"{name}{i}") for i in range(n)]

    def _wait(self, q, events):
        best = {}
        for (sem, val) in events:
            k = id(sem)
            if k not in best or best[k][1] < val:
                best[k] = (sem, val)
        for k, (sem, val) in best.items():
            if q.known.get(k, 0) >= val:
                continue
            q.eng.wait_ge(sem, val)
            q.known[k] = val

    def _deps(self, reads, writes):
        deps = []
        for b in reads:
            deps += b.w
        for b in writes:
            deps += b.w
            deps += b.r
        return deps

    def _check_pending(self, q, writes, reads):
        for qq in self.queues:
            for (b, kind) in qq.pending:
                if qq is q:
                    continue
                for wb in writes:
                    assert wb is not b, f"write to {b.name} while unsignaled op pending on {qq.name}"
                if kind == 'w':
                    for rb in reads:
                        assert rb is not b, f"read of {b.name} while unsignaled write pending on {qq.name}"

    @staticmethod
    def _rec(b, kind, ev):
        if kind == 'r':
            if ev not in b.r:
                b.r.append(ev)
        else:
            if ev not in b.w:
                b.w.append(ev)

    def op(self, q, fn, reads=(), writes=(), signal=True):
        self._check_pending(q, writes, reads)
        self._wait(q, self._deps(reads, writes))
        ins = fn(q.eng)
        for b in writes:
            if b.r:
                b.r = []
                b.w = []
        for b in reads:
            q.pending.append((b, 'r'))
        for b in writes:
            q.pending.append((b, 'w'))
        if signal:
            q.count += 1
            ins.then_inc(q.sem, 1)
            ev = (q.sem, q.count)
            for (b, kind) in q.pending:
                self._rec(b, kind, ev)
            q.pending = []
        return ins

    def dma(self, q, out, in_, reads=(), writes=(), sem_buf=None, **kw):
        self._check_pending(q, writes, reads)
        self._wait(q, self._deps(reads, writes))
        ins = q.eng.dma_start(out=out, in_=in_, **kw)
        sb = sem_buf
        if sb.dsem is None:
            sb.dsem = self.nc.alloc_semaphore("d_" + sb.name)
            self.dbufs.append(sb)
        sb.dcount += 16
        ins.then_inc(sb.dsem, 16)
        ev = (sb.dsem, sb.dcount)
        for b in writes:
            if b.r:
                b.r = []
                b.w = []
        for b in reads:
            self._rec(b, 'r', ev)
        for b in writes:
            self._rec(b, 'w', ev)
        return ev

    def barrier(self):
        evs = []
        for q in self.queues:
            assert not q.pending, f"pending on {q.name} at barrier"
            if q.count:
                evs.append((q.sem, q.count))
        for b in self.dbufs:
            evs.append((b.dsem, b.dcount))
        for q in self.queues:
            self._wait(q, evs)

    def finish(self):
        evs = [(b.dsem, b.dcount) for b in self.dbufs]
        for q in self.queues:
            if q.count:
                evs.append((q.sem, q.count))
        self._wait(self.sp, evs)


def mm_group(P, ps_ap, ps_bufs, pairs, reads, signal_last=True, first_start=True):
    n = len(pairs)
    for i, (l, r) in enumerate(pairs):
        P.op(P.pe, lambda e, l=l, r=r, i=i: e.matmul(ps_ap, l, r, start=(i == 0 and first_start), stop=(i == n - 1)),
             reads=reads, writes=ps_bufs, signal=(signal_last and i == n - 1))


def rstd_from_ss(P, rstd_ap, ss_ap, n, rbufs, sbufs):
    P.op(P.act, lambda e: e.activation(out=rstd_ap, in_=ss_ap, func=AF.Sqrt, scale=1.0 / n, bias=EPS),
         reads=sbufs, writes=rbufs)
    P.op(P.dve, lambda e: e.reciprocal(out=rstd_ap, in_=rstd_ap), reads=rbufs, writes=rbufs)


class Banks:
    def __init__(self, P, es):
        self.t = es.enter_context(P.nc.psum_tensor("psum_all", [128, 8 * 512], F32))
        self.b = P.bufs("bank", 8)
        self.rr = 0

    def f32(self, i, n=1):
        return self.t[:, i * 512:(i + n) * 512]

    def bf16(self, i):
        return self.t[:, i * 512:(i + 1) * 512].bitcast(BF16)

    def next(self):
        i = self.rr
        self.rr = (self.rr + 1) % 8
        return i


def phase_hT(P, es, bk, x_dram, row0, ntiles, gpre_t, gpre_b, ident, ident_b, hT, hT_bufs):
    nc = P.nc
    NS = 3
    xs = [es.enter_context(nc.sbuf_tensor(f"p1_xs{i}", [128, D], F32)) for i in range(NS)]
    xs_b = P.bufs("p1_xs", NS)
    hb = [es.enter_context(nc.sbuf_tensor(f"p1_hb{i}", [128, D], BF16)) for i in range(2)]
    hb_b = P.bufs("p1_hb", 2)
    junk = es.enter_context(nc.sbuf_tensor("p1_junk", [128, D], BF16))
    junk_b = P.buf("p1_junk")
    st = es.enter_context(nc.sbuf_tensor("p1_st", [128, 2 * NS], F32))
    ss_b = P.bufs("p1_ss", NS)
    rs_b = P.bufs("p1_rs", NS)
    for t in range(ntiles):
        s = t % NS
        P.dma(P.sp, xs[s][:], x_dram[row0 + t * 128: row0 + (t + 1) * 128, :], writes=[xs_b[s]], sem_buf=xs_b[s])
        ss_ap = st[:, 2 * s:2 * s + 1]
        rs_ap = st[:, 2 * s + 1:2 * s + 2]
        P.op(P.act, lambda e: e.activation(out=junk[:], in_=xs[s][:], func=AF.Square, accum_out=ss_ap),
             reads=[xs_b[s]], writes=[junk_b, ss_b[s]])
        rstd_from_ss(P, rs_ap, ss_ap, D, [rs_b[s]], [ss_b[s]])
        h = t % 2
        P.op(P.dve, lambda e: e.scalar_tensor_tensor(out=hb[h][:], in0=xs[s][:], scalar=rs_ap, in1=gpre_t[:],
                                                     op0=ALU.mult, op1=ALU.mult),
             reads=[xs_b[s], rs_b[s], gpre_b], writes=[hb_b[h]])
        for half in range(2):
            bi = bk.next()
            pst = bk.bf16(bi)
            for j in range(8):
                kc = half * 8 + j
                P.op(P.pe, lambda e, kc=kc, j=j: e.transpose(pst[:, j * 128:(j + 1) * 128], hb[h][:, kc * 128:(kc + 1) * 128], ident[:]),
                     reads=[hb_b[h], ident_b], writes=[bk.b[bi]], signal=(j == 7))
            eng = P.act if half == 0 else P.dve
            dst = hT[:, half * 8:(half + 1) * 8, t * 128:(t + 1) * 128]
            src = pst.rearrange("p (k n) -> p k n", k=8)
            if eng is P.act:
                P.op(eng, lambda e: e.copy(out=dst, in_=src), reads=[bk.b[bi]], writes=[hT_bufs[t]])
            else:
                P.op(eng, lambda e: e.tensor_copy(out=dst, in_=src), reads=[bk.b[bi]], writes=[hT_bufs[t]])


def phase_out(P, es, bk, gT, gT_bufs, wo, wo_b, gpost_t, gpost_b, x_dram, xrow0, out_dram, ntiles):
    nc = P.nc
    NS = 2
    xs = [es.enter_context(nc.sbuf_tensor(f"p3_xs{i}", [128, D], F32)) for i in range(NS)]
    xs_b = P.bufs("p3_xs", NS)
    ys = [es.enter_context(nc.sbuf_tensor(f"p3_ys{i}", [128, D], F32)) for i in range(NS)]
    ys_b = P.bufs("p3_ys", NS)
    junk = es.enter_context(nc.sbuf_tensor("p3_junk", [128, D], BF16))
    junk_b = P.buf("p3_junk")
    st = es.enter_context(nc.sbuf_tensor("p3_st", [128, 2 * NS], F32))
    ss_b = P.bufs("p3_ss", NS)
    rs_b = P.bufs("p3_rs", NS)
    for t in range(ntiles):
        s = t % NS
        P.dma(P.sp, xs[s][:], x_dram[xrow0 + t * 128: xrow0 + (t + 1) * 128, :], reads=[], writes=[xs_b[s]], sem_buf=xs_b[s])
        b0 = 4 * (t % 2)
        for nb in range(4):
            pairs = [(gT[:, kc, t * 128:(t + 1) * 128], wo[:, kc, nb * 512:(nb + 1) * 512]) for kc in range(16)]
            mm_group(P, bk.f32(b0 + nb), [bk.b[b0 + nb]], pairs, reads=list(gT_bufs) + [wo_b])
        ybanks = [bk.b[b0 + i] for i in range(4)]
        yps = bk.f32(b0, 4)
        ss_ap = st[:, 2 * s:2 * s + 1]
        rs_ap = st[:, 2 * s + 1:2 * s + 2]
        P.op(P.act, lambda e: e.activation(out=junk[:], in_=yps, func=AF.Square, accum_out=ss_ap),
             reads=ybanks, writes=[junk_b, ss_b[s]])
        rstd_from_ss(P, rs_ap, ss_ap, D, [rs_b[s]], [ss_b[s]])
        P.op(P.dve, lambda e: e.scalar_tensor_tensor(out=ys[s][:], in0=yps, scalar=rs_ap, in1=gpost_t[:],
                                                     op0=ALU.mult, op1=ALU.mult),
             reads=ybanks + [rs_b[s], gpost_b], writes=[ys_b[s]])
        P.op(P.pool, lambda e: e.tensor_tensor(out=ys[s][:], in0=ys[s][:], in1=xs[s][:], op=ALU.add),
             reads=[ys_b[s], xs_b[s]], writes=[ys_b[s]])
        P.dma(P.sp, out_dram[t * 128:(t + 1) * 128, :], ys[s][:], reads=[ys_b[s]], writes=[], sem_buf=ys_b[s])


def load_consts(P, es, ident_d):
    nc = P.nc
    ident = es.enter_context(nc.sbuf_tensor("ident_sb", [128, 128], BF16))
    ident_b = P.buf("ident")
    P.dma(P.pool, ident[:], ident_d[:, :], writes=[ident_b], sem_buf=ident_b)
    ones = es.enter_context(nc.sbuf_tensor("ones_sb", [128, 128], BF16))
    ones_b = P.buf("ones")
    P.op(P.dve, lambda e: e.memset(ones[:], 1.0), writes=[ones_b])
    return ident, ident_b, ones, ones_b


def na_chunks(m):
    if m < 2:
        return list(range(m, m + 6))
    if m >= 14:
        return list(range(m - 1, m + 5))
    return list(range(m, m + 5))


def build_na(nheads=NH):
    nc = bass.Bass("TRN2", target_bir_lowering=False)
    xh = nc.dram_tensor("xh", [TH, D], F32, kind="ExternalInput").ap()
    w_in = nc.dram_tensor("w_in", [NH, 128, 16, 512], F32, kind="ExternalInput").ap()
    w_out = nc.dram_tensor("w_out", [128, 16, D], F32, kind="ExternalInput").ap()
    gpre_d = nc.dram_tensor("gpre", [128, D], F32, kind="ExternalInput").ap()
    gpost_d = nc.dram_tensor("gpost", [128, D], F32, kind="ExternalInput").ap()
    bint_d = nc.dram_tensor("bint", [NH, 128, 5 * 128], F32, kind="ExternalInput").ap()
    bedge_d = nc.dram_tensor("bedge", [NH, 128, 4, 6 * 128], F32, kind="ExternalInput").ap()
    ident_d = nc.dram_tensor("ident", [128, 128], F32, kind="ExternalInput").ap()
    xout = nc.dram_tensor("xout", [TOK, D], F32, kind="ExternalOutput").ap()
    gscr = nc.dram_tensor("gscr", [D, TOK], BF16, kind="Internal").ap()
    gscr_b = [Buf(f"gscr{h}") for h in range(NH)]
    scale = float(128 ** -0.5)

    P = Prog(nc)
    with ExitStack() as es0:
        bk = Banks(P, es0)
        ident, ident_b, ones, ones_b = load_consts(P, es0, ident_d)
        with ExitStack() as es1:
            hT = es1.enter_context(nc.sbuf_tensor("hT", [128, 16, TH], BF16))
            hT_bufs = P.bufs("hT", TH // 128)
            with ExitStack() as es:
                gpre_t = es.enter_context(nc.sbuf_tensor("gpre_t", [128, D], F32))
                gpre_b = P.buf("gpre")
                P.dma(P.sp, gpre_t[:], gpre_d[:, :], writes=[gpre_b], sem_buf=gpre_b)
                phase_hT(P, es, bk, xh, 0, TH // 128, gpre_t, gpre_b, ident, ident_b, hT, hT_bufs)
            P.barrier()
            with ExitStack() as es:
                NW = 2
                wb = [es.enter_context(nc.sbuf_tensor(f"wb{i}", [128, 16, 512], BF16)) for i in range(NW)]
                wb_b = P.bufs("wb", NW)
                NHB = 2
                qT = [es.enter_context(nc.sbuf_tensor(f"qT{i}", [128, TOK], BF16)) for i in range(NHB)]
                kT = [es.enter_context(nc.sbuf_tensor(f"kT{i}", [128, TH], BF16)) for i in range(NHB)]
                gz = [es.enter_context(nc.sbuf_tensor(f"gz{i}", [128, TOK], BF16)) for i in range(NHB)]
                V = [es.enter_context(nc.sbuf_tensor(f"V{i}", [128, TH // 128, 128], BF16)) for i in range(NHB)]
                qT_b = P.bufs("qT", NHB); kT_b = P.bufs("kT", NHB); gz_b = P.bufs("gz", NHB); V_b = P.bufs("V", NHB)
                vT = es.enter_context(nc.sbuf_tensor("vT", [128, TH], BF16))
                vT_b = P.bufs("vT", TH // 512)
                gTh = [es.enter_context(nc.sbuf_tensor(f"gTh{i}", [128, TOK], BF16)) for i in range(2)]
                gTh_b = P.bufs("gTh", 2)
                bint = [es.enter_context(nc.sbuf_tensor(f"bint{i}", [128, 5 * 128], F32)) for i in range(2)]
                bint_b = P.bufs("bint", 2)
                bedge = es.enter_context(nc.sbuf_tensor("bedge_sb", [128, 4, 6 * 128], F32))
                bedge_b = P.buf("bedge")
                NSB = 2
                Sb = [es.enter_context(nc.sbuf_tensor(f"Sb{i}", [128, 6 * 128], F32)) for i in range(NSB)]
                Sb_b = P.bufs("Sb", NSB)
                PT = [es.enter_context(nc.sbuf_tensor(f"PT{i}", [128, 6 * 128], BF16)) for i in range(NSB)]
                PT_b = P.bufs("PT", NSB)
                rsb = [es.enter_context(nc.sbuf_tensor(f"rsb{i}", [128, 128], F32)) for i in range(2)]
                rsb_b = P.bufs("rsb", 2)
                o1 = [es.enter_context(nc.sbuf_tensor(f"o1{i}", [128, 128], F32)) for i in range(2)]
                o1_b = P.bufs("o1", 2)

                def load_w(h):
                    s = h % NW
                    for g in range(4):
                        P.dma(P.pool, wb[s][:, 4 * g:4 * g + 4, :], w_in[h, :, 4 * g:4 * g + 4, :], writes=[wb_b[s]], sem_buf=wb_b[s])

                def load_bias(h):
                    P.dma(P.sp, bint[h % 2][:], bint_d[h], writes=[bint_b[h % 2]], sem_buf=bint_b[h % 2])
                    P.dma(P.sp, bedge[:], bedge_d[h], writes=[bedge_b], sem_buf=bedge_b)

                load_w(0)
                for h in range(nheads):
                    s = h % NW
                    hs = h % NHB
                    if h + 1 < nheads:
                        load_w(h + 1)
                    load_bias(h)
                    w = wb[s]
                    allh = list(hT_bufs)
                    own = hT_bufs[2:18]
                    ev = 0
                    for (j, dst, dst_b, ntok, tok0, hb_list, kind) in (
                            (0, qT[hs], qT_b[hs], TOK, HALO, own, "q"),
                            (1, kT[hs], kT_b[hs], TH, 0, allh, "k"),
                            (3, gz[hs], gz_b[hs], TOK, HALO, own, "z")):
                        for blk in range(ntok // 512):
                            bi = bk.next()
                            c0 = tok0 + blk * 512
                            pairs = [(w[:, kc, j * 128:(j + 1) * 128], hT[:, kc, c0:c0 + 512]) for kc in range(16)]
                            mm_group(P, bk.f32(bi), [bk.b[bi]], pairs, reads=hT_bufs[c0 // 128:c0 // 128 + 4] + [wb_b[s]])
                            dap = dst[:, blk * 512:(blk + 1) * 512]
                            if kind == "z":
                                P.op(P.act, lambda e: e.activation(out=dap, in_=bk.f32(bi), func=AF.Silu),
                                     reads=[bk.b[bi]], writes=[dst_b])
                            elif ev % 2 == 0:
                                P.op(P.act, lambda e: e.copy(out=dap, in_=bk.f32(bi)), reads=[bk.b[bi]], writes=[dst_b])
                            else:
                                P.op(P.dve, lambda e: e.tensor_copy(out=dap, in_=bk.f32(bi)), reads=[bk.b[bi]], writes=[dst_b])
                            ev += 1
                    for blk in range(TH // 512):
                        bi = bk.next()
                        c0 = blk * 512
                        pairs = [(w[:, kc, 256:384], hT[:, kc, c0:c0 + 512]) for kc in range(16)]
                        mm_group(P, bk.f32(bi), [bk.b[bi]], pairs, reads=hT_bufs[c0 // 128:c0 // 128 + 4] + [wb_b[s]])
                        dap = vT[:, c0:c0 + 512]
                        if blk % 2 == 0:
                            P.op(P.dve, lambda e: e.tensor_copy(out=dap, in_=bk.f32(bi)), reads=[bk.b[bi]], writes=[vT_b[blk]])
                        else:
                            P.op(P.act, lambda e: e.copy(out=dap, in_=bk.f32(bi)), reads=[bk.b[bi]], writes=[vT_b[blk]])
                    for tg in range(TH // 512):
                        bi = bk.next()
                        pst = bk.bf16(bi)
                        for i in range(4):
                            t = tg * 4 + i
                            P.op(P.pe, lambda e, i=i, t=t: e.transpose(pst[:, i * 128:(i + 1) * 128], vT[:, t * 128:(t + 1) * 128], ident[:]),
                                 reads=[vT_b[tg], ident_b], writes=[bk.b[bi]], signal=(i == 3))
                        dap = V[hs][:, tg * 4:(tg + 1) * 4, :]
                        sap = pst[:, 0:512].rearrange("p (a b) -> p a b", a=4)
                        if tg % 2 == 0:
                            P.op(P.act, lambda e: e.copy(out=dap, in_=sap), reads=[bk.b[bi]], writes=[V_b[hs]])
                        else:
                            P.op(P.dve, lambda e: e.tensor_copy(out=dap, in_=sap), reads=[bk.b[bi]], writes=[V_b[hs]])

                    SB = [(0, 1), (2, 3)]
                    OB = [(4, 5), (6, 7)]

                    def issue_S(m):
                        cl = na_chunks(m)
                        b0, b1 = SB[m % 2]
                        for ci, c in enumerate(cl):
                            bi = b0 if ci < 4 else b1
                            col = (ci % 4) * 128
                            P.op(P.pe, lambda e, c=c, bi=bi, col=col: e.matmul(
                                bk.f32(bi)[:, col:col + 128], kT[hs][:, c * 128:(c + 1) * 128],
                                qT[hs][:, m * 128:(m + 1) * 128], start=True, stop=True),
                                reads=[kT_b[hs], qT_b[hs]], writes=[bk.b[b0], bk.b[b1]], signal=(ci == len(cl) - 1))

                    def softmax(m):
                        cl = na_chunks(m)
                        n = len(cl) * 128
                        b0, b1 = SB[m % 2]
                        sps = bk.f32(b0, 2)[:, 0:n]
                        if m < 2:
                            bt, btb = bedge[:, m, 0:n], bedge_b
                        elif m >= 14:
                            bt, btb = bedge[:, m - 12, 0:n], bedge_b
                        else:
                            bt, btb = bint[h % 2][:, 0:n], bint_b[h % 2]
                        sl = m % NSB
                        P.op(P.dve, lambda e: e.scalar_tensor_tensor(out=Sb[sl][:, 0:n], in0=sps, scalar=scale, in1=bt,
                                                                     op0=ALU.mult, op1=ALU.add),
                             reads=[bk.b[b0], bk.b[b1], btb], writes=[Sb_b[sl]])
                        P.op(P.act, lambda e: e.activation(out=PT[sl][:, 0:n], in_=Sb[sl][:, 0:n], func=AF.Exp),
                             reads=[Sb_b[sl]], writes=[PT_b[sl]])

                    def issue_O(m):
                        cl = na_chunks(m)
                        sl = m % NSB
                        ob, mb = OB[m % 2]
                        pairs = [(V[hs][:, c, :], PT[sl][:, ci * 128:(ci + 1) * 128]) for ci, c in enumerate(cl)]
                        mm_group(P, bk.f32(ob)[:, 0:128], [bk.b[ob]], pairs, reads=[V_b[hs], PT_b[sl]])
                        pairs = [(ones[:], PT[sl][:, ci * 128:(ci + 1) * 128]) for ci, c in enumerate(cl)]
                        mm_group(P, bk.f32(mb)[:, 0:128], [bk.b[mb]], pairs, reads=[ones_b, PT_b[sl]])

                    def fin(m):
                        ob, mb = OB[m % 2]
                        i2 = m % 2
                        P.op(P.dve, lambda e: e.reciprocal(out=rsb[i2][:], in_=bk.f32(mb)[:, 0:128]),
                             reads=[bk.b[mb]], writes=[rsb_b[i2]])
                        P.op(P.dve, lambda e: e.tensor_tensor(out=o1[i2][:], in0=bk.f32(ob)[:, 0:128], in1=rsb[i2][:], op=ALU.mult),
                             reads=[bk.b[ob], rsb_b[i2]], writes=[o1_b[i2]])
                        P.op(P.pool, lambda e: e.tensor_tensor(out=gTh[h % 2][:, m * 128:(m + 1) * 128], in0=o1[i2][:],
                                                               in1=gz[hs][:, m * 128:(m + 1) * 128], op=ALU.mult),
                             reads=[o1_b[i2], gz_b[hs]], writes=[gTh_b[h % 2]])

                    issue_S(0)
                    for m in range(16):
                        softmax(m)
                        if m + 1 < 16:
                            issue_S(m + 1)
                        issue_O(m)
                        fin(m)
                    P.dma(P.sp, gscr[h * 128:(h + 1) * 128, :], gTh[h % 2][:], reads=[gTh_b[h % 2]], writes=[gscr_b[h]], sem_buf=gTh_b[h % 2])
            P.barrier()
        with ExitStack() as es:
            gT = es.enter_context(nc.sbuf_tensor("gT", [128, 16, TOK], BF16))
            gT_bufs = P.bufs("gT", 4)
            wo = es.enter_context(nc.sbuf_tensor("wo", [128, 16, D], BF16))
            wo_b = P.buf("wo")
            gpost_t = es.enter_context(nc.sbuf_tensor("gpost_t", [128, D], F32))
            gpost_b = P.buf("gpost")
            P.dma(P.sp, gpost_t[:], gpost_d[:, :], writes=[gpost_b], sem_buf=gpost_b)
            for kc in range(16):
                P.dma(P.sp, gT[:, kc, :], gscr[kc * 128:(kc + 1) * 128, :], reads=[gscr_b[kc]], writes=[gT_bufs[kc // 4]], sem_buf=gT_bufs[kc // 4])
            for g in range(8):
                P.dma(P.pool, wo[:, 2 * g:2 * g + 2, :], w_out[:, 2 * g:2 * g + 2, :], writes=[wo_b], sem_buf=wo_b)
            phase_out(P, es, bk, gT, gT_bufs, wo, wo_b, gpost_t, gpost_b, xh, HALO, xout, TOK // 128)
            P.finish()
    return nc


def na_bias_tiles(rpb):
    H = rpb.shape[0]
    rows = 128
    out = []
    kc = np.arange(64)
    qc = np.arange(64)
    col_start = np.clip(qc - 8, 0, 48)
    col_ok = (kc[:, None] >= col_start[None, :]) & (kc[:, None] < col_start[None, :] + 16)
    dc = np.clip(kc[:, None] - qc[None, :] + 15, 0, 30)

    def tile_for(Mg, chunks_g):
        t = np.full((H, 128, len(chunks_g), 128), NEG, np.float32)
        for ci, cg in enumerate(chunks_g):
            for kr2 in range(2):
                kr = 2 * cg + kr2
                if kr < 0 or kr >= rows:
                    continue
                for qr2 in range(2):
                    r = 2 * Mg + qr2
                    rs = min(max(r - 4, 0), rows - 8)
                    if not (rs <= kr < rs + 8):
                        continue
                    dr = kr - r + 7
                    vals = rpb[:, dr, :][:, dc]
                    vals = np.where(col_ok[None], vals, np.float32(NEG))
                    t[:, kr2 * 64:(kr2 + 1) * 64, ci, qr2 * 64:(qr2 + 1) * 64] = vals
        return t

    for q in range(4):
        Mi = 16 * q + 8
        bint = tile_for(Mi, [Mi - 2 + i for i in range(5)]).reshape(H, 128, 640)
        be = np.empty((H, 128, 4, 768), np.float32)
        for idx, m in enumerate((0, 1, 14, 15)):
            Mg = 16 * q + m
            cl = na_chunks(m)
            be[:, :, idx, :] = tile_for(Mg, [16 * q + (c - 2) for c in cl]).reshape(H, 128, 768)
        out.append((np.ascontiguousarray(bint), np.ascontiguousarray(be)))
    return out


def run_na_layer(x, g_pre, g_post, w_in, rpb, w_out, nheads=NH):
    B, L, _ = x.shape
    nc = build_na(nheads)
    w_in_r = np.ascontiguousarray(w_in.reshape(16, 128, 4, NH, 128).transpose(3, 1, 0, 2, 4).reshape(NH, 128, 16, 512))
    w_out_r = np.ascontiguousarray(w_out.reshape(16, 128, D).transpose(1, 0, 2))
    gpre = np.ascontiguousarray(np.broadcast_to(g_pre[None, :], (128, D)))
    gpost = np.ascontiguousarray(np.broadcast_to(g_post[None, :], (128, D)))
    ident = np.eye(128, dtype=np.float32)
    tiles = na_bias_tiles(rpb)
    in_maps = []
    for c in range(NCORES):
        b, q = divmod(c, 4)
        xh = np.zeros((TH, D), np.float32)
        lo = q * TOK - HALO
        hi = lo + TH
        slo, shi = max(lo, 0), min(hi, L)
        xh[slo - lo: shi - lo] = x[b, slo:shi]
        in_maps.append({"xh": xh, "w_in": w_in_r, "w_out": w_out_r, "gpre": gpre, "gpost": gpost,
                        "bint": tiles[q][0], "bedge": tiles[q][1], "ident": ident})
    res = run_bass_kernel_spmd(nc, in_maps, core_ids=list(range(NCORES)))
    out = np.empty_like(x)
    for c in range(NCORES):
        b, q = divmod(c, 4)
        out[b, q * TOK:(q + 1) * TOK] = res.results[c]["xout"]
    return out


def rope_combine(P, bk, bA, bB, cosT, sinT, c_b, dst_ap, dst_b, t1, t1_b, t2, t2_b, c0, n=512):
    P.op(P.dve, lambda e: e.tensor_tensor(out=t1[:, 0:n], in0=bk.f32(bA)[0:64, 0:n], in1=cosT[:, c0:c0 + n], op=ALU.mult),
         reads=[bk.b[bA], c_b], writes=[t1_b])
    P.op(P.dve, lambda e: e.tensor_tensor(out=t2[:, 0:n], in0=bk.f32(bB)[0:64, 0:n], in1=sinT[:, c0:c0 + n], op=ALU.mult),
         reads=[bk.b[bB], c_b], writes=[t2_b])
    P.op(P.pool, lambda e: e.tensor_tensor(out=dst_ap, in0=t1[:, 0:n], in1=t2[:, 0:n], op=ALU.add),
         reads=[t1_b, t2_b], writes=[dst_b])


def build_mla_a():
    nc = bass.Bass("TRN2", target_bir_lowering=False)
    x_d = nc.dram_tensor("x", [TOK, D], F32, kind="ExternalInput").ap()
    gpre_d = nc.dram_tensor("gpre", [128, D], F32, kind="ExternalInput").ap()
    wlat_d = nc.dram_tensor("wlat", [128, 16, 1088], F32, kind="ExternalInput").ap()
    wz_d = nc.dram_tensor("wz", [NH, 128, 16, 128], F32, kind="ExternalInput").ap()
    wq_d = nc.dram_tensor("wq", [128, 4, 3072], F32, kind="ExternalInput").ap()
    wkv_d = nc.dram_tensor("wkv", [128, 4, 4096], F32, kind="ExternalInput").ap()
    gq_d = nc.dram_tensor("gqkv", [128, 1024], F32, kind="ExternalInput").ap()
    cos_d = nc.dram_tensor("cosT", [64, TOK], F32, kind="ExternalInput").ap()
    sin_d = nc.dram_tensor("sinT", [64, TOK], F32, kind="ExternalInput").ap()
    ident_d = nc.dram_tensor("ident", [128, 128], F32, kind="ExternalInput").ap()
    QN = nc.dram_tensor("QN", [NH, 128, TOK], BF16, kind="ExternalOutput").ap()
    QR = nc.dram_tensor("QR", [NH, 64, TOK], BF16, kind="ExternalOutput").ap()
    KN = nc.dram_tensor("KN", [NH, 128, TOK], BF16, kind="ExternalOutput").ap()
    KR = nc.dram_tensor("KR", [64, TOK], BF16, kind="ExternalOutput").ap()
    VR = nc.dram_tensor("VR", [NH, 128, 16 * 128], BF16, kind="ExternalOutput").ap()
    GZ = nc.dram_tensor("GZ", [NH, 128, TOK], BF16, kind="ExternalOutput").ap()

    P = Prog(nc)
    with ExitStack() as es0:
        bk = Banks(P, es0)
        ident, ident_b, ones, ones_b = load_consts(P, es0, ident_d)
        cnT = es0.enter_context(nc.sbuf_tensor("cnT", [128, 8, TOK], BF16))
        cnT_b = P.bufs("cnT", 16)
        cosT = es0.enter_context(nc.sbuf_tensor("cos_sb", [64, TOK], F32))
        sinT = es0.enter_context(nc.sbuf_tensor("sin_sb", [64, TOK], F32))
        cs_b = P.buf("cossin")
        P.dma(P.sp, cosT[:], cos_d[:, :], writes=[cs_b], sem_buf=cs_b)
        P.dma(P.sp, sinT[:], sin_d[:, :], writes=[cs_b], sem_buf=cs_b)
        t1 = es0.enter_context(nc.sbuf_tensor("rope_t1", [64, 512], F32))
        t2 = es0.enter_context(nc.sbuf_tensor("rope_t2", [64, 512], F32))
        t1_b = P.buf("t1"); t2_b = P.buf("t2")
        with ExitStack() as es1:
            hT = es1.enter_context(nc.sbuf_tensor("hT", [128, 16, TOK], BF16))
            hT_bufs = P.bufs("hT", 16)
            with ExitStack() as es:
                gpre_t = es.enter_context(nc.sbuf_tensor("gpre_t", [128, D], F32))
                gpre_b = P.buf("gpre")
                P.dma(P.sp, gpre_t[:], gpre_d[:, :], writes=[gpre_b], sem_buf=gpre_b)
                phase_hT(P, es, bk, x_d, 0, 16, gpre_t, gpre_b, ident, ident_b, hT, hT_bufs)
            P.barrier()
            with ExitStack() as es:
                wlat = es.enter_context(nc.sbuf_tensor("wlat_sb", [128, 16, 1024], BF16))
                wlat_b = P.buf("wlat")
                wkr = es.enter_context(nc.sbuf_tensor("wkr_sb", [128, 16, 64], BF16))
                wkr_b = P.buf("wkr")
                wkrr = es.enter_context(nc.sbuf_tensor("wkrr_sb", [128, 16, 64], BF16))
                wkrr_b = P.buf("wkrr")
                gq_t = es.enter_context(nc.sbuf_tensor("gq_t", [128, 1024], F32))
                gq_b = P.buf("gq")
                P.dma(P.sp, gq_t[:], gq_d[:, :], writes=[gq_b], sem_buf=gq_b)
                for g in range(4):
                    for half in range(2):
                        P.dma(P.pool, wlat[:, 4 * g:4 * g + 4, half * 512:(half + 1) * 512],
                              wlat_d[:, 4 * g:4 * g + 4, half * 512:(half + 1) * 512], writes=[wlat_b], sem_buf=wlat_b)
                P.dma(P.pool, wkr[:], wlat_d[:, :, 1024:1088], writes=[wkr_b], sem_buf=wkr_b)
                wv = wkr[:].rearrange("p k (i two) -> p k i two", two=2)
                rv = wkrr[:].rearrange("p k (i two) -> p k i two", two=2)
                P.op(P.act, lambda e: e.mul(out=rv[:, :, :, 0], in_=wv[:, :, :, 1], mul=-1.0), reads=[wkr_b], writes=[wkrr_b])
                P.op(P.act, lambda e: e.copy(out=rv[:, :, :, 1], in_=wv[:, :, :, 0]), reads=[wkr_b], writes=[wkrr_b])
                junk = es.enter_context(nc.sbuf_tensor("a_junk", [128, 512], BF16))
                junk_b = P.buf("a_junk")
                st = es.enter_context(nc.sbuf_tensor("a_st", [128, 8], F32))
                ss_b = P.bufs("a_ss", 2); rs_b = P.bufs("a_rs", 2)
                cn = [es.enter_context(nc.sbuf_tensor(f"a_cn{i}", [128, 1024], BF16)) for i in range(2)]
                cn_b = P.bufs("a_cn", 2)
                for t in range(16):
                    s = t % 2
                    bq = bk.next(); bkv = bk.next()
                    for (bi, c0) in ((bq, 0), (bkv, 512)):
                        pairs = [(hT[:, kc, t * 128:(t + 1) * 128], wlat[:, kc, c0:c0 + 512]) for kc in range(16)]
                        mm_group(P, bk.f32(bi), [bk.b[bi]], pairs, reads=[hT_bufs[t], wlat_b])
                    ss_ap = st[:, 4 * s:4 * s + 2]
                    rs_ap = st[:, 4 * s + 2:4 * s + 4]
                    P.op(P.act, lambda e: e.activation(out=junk[:], in_=bk.f32(bq), func=AF.Square, accum_out=ss_ap[:, 0:1]),
                         reads=[bk.b[bq]], writes=[junk_b, ss_b[s]])
                    P.op(P.act, lambda e: e.activation(out=junk[:], in_=bk.f32(bkv), func=AF.Square, accum_out=ss_ap[:, 1:2]),
                         reads=[bk.b[bkv]], writes=[junk_b, ss_b[s]])
                    rstd_from_ss(P, rs_ap, ss_ap, 512, [rs_b[s]], [ss_b[s]])
                    P.op(P.dve, lambda e: e.scalar_tensor_tensor(out=cn[s][:, 0:512], in0=bk.f32(bq), scalar=rs_ap[:, 0:1],
                                                                 in1=gq_t[:, 0:512], op0=ALU.mult, op1=ALU.mult),
                         reads=[bk.b[bq], rs_b[s], gq_b], writes=[cn_b[s]])
                    P.op(P.dve, lambda e: e.scalar_tensor_tensor(out=cn[s][:, 512:1024], in0=bk.f32(bkv), scalar=rs_ap[:, 1:2],
                                                                 in1=gq_t[:, 512:1024], op0=ALU.mult, op1=ALU.mult),
                         reads=[bk.b[bkv], rs_b[s], gq_b], writes=[cn_b[s]])
                    bt = bk.next()
                    pst = bk.bf16(bt)
                    for j in range(8):
                        P.op(P.pe, lambda e, j=j: e.transpose(pst[:, j * 128:(j + 1) * 128], cn[s][:, j * 128:(j + 1) * 128], ident[:]),
                             reads=[cn_b[s], ident_b], writes=[bk.b[bt]], signal=(j == 7))
                    P.op(P.act, lambda e: e.copy(out=cnT[:, :, t * 128:(t + 1) * 128], in_=pst.rearrange("p (k n) -> p k n", k=8)),
                         reads=[bk.b[bt]], writes=[cnT_b[t]])
                krT = es.enter_context(nc.sbuf_tensor("krT", [64, TOK], BF16))
                krT_b = P.buf("krT")
                for blk in range(4):
                    bA = bk.next(); bB = bk.next()
                    for (bi, wsrc, wsb) in ((bA, wkr, wkr_b), (bB, wkrr, wkrr_b)):
                        pairs = [(wsrc[:, kc, :], hT[:, kc, blk * 512:(blk + 1) * 512]) for kc in range(16)]
                        mm_group(P, bk.f32(bi)[0:64, :], [bk.b[bi]], pairs, reads=hT_bufs[blk * 4:blk * 4 + 4] + [wsb])
                    rope_combine(P, bk, bA, bB, cosT, sinT, cs_b, krT[:, blk * 512:(blk + 1) * 512], krT_b, t1, t1_b, t2, t2_b, blk * 512)
                P.dma(P.sp, KR[:, :], krT[:], reads=[krT_b], sem_buf=krT_b)
                wz = [es.enter_context(nc.sbuf_tensor(f"wz{i}", [128, 16, 128], BF16)) for i in range(2)]
                wz_b = P.bufs("wz", 2)
                gz = [es.enter_context(nc.sbuf_tensor(f"gz{i}", [128, TOK], BF16)) for i in range(2)]
                gz_b = P.bufs("gz", 2)
                P.dma(P.pool, wz[0][:], wz_d[0], writes=[wz_b[0]], sem_buf=wz_b[0])
                for h in range(NH):
                    s = h % 2
                    if h + 1 < NH:
                        P.dma(P.pool, wz[(h + 1) % 2][:], wz_d[h + 1], writes=[wz_b[(h + 1) % 2]], sem_buf=wz_b[(h + 1) % 2])
                    for blk in range(4):
                        bi = bk.next()
                        pairs = [(wz[s][:, kc, :], hT[:, kc, blk * 512:(blk + 1) * 512]) for kc in range(16)]
                        mm_group(P, bk.f32(bi), [bk.b[bi]], pairs, reads=hT_bufs[blk * 4:blk * 4 + 4] + [wz_b[s]])
                        P.op(P.act, lambda e: e.activation(out=gz[s][:, blk * 512:(blk + 1) * 512], in_=bk.f32(bi), func=AF.Silu),
                             reads=[bk.b[bi]], writes=[gz_b[s]])
                    P.dma(P.sp, GZ[h], gz[s][:], reads=[gz_b[s]], sem_buf=gz_b[s])
            P.barrier()
        with ExitStack() as es:
            wq = es.enter_context(nc.sbuf_tensor("wq_sb", [128, 4, 3072], BF16))
            wq_b = P.buf("wq")
            wqr = es.enter_context(nc.sbuf_tensor("wqr_sb", [128, 4, NH, 64], BF16))
            wqr_b = P.buf("wqr")
            wkv = es.enter_context(nc.sbuf_tensor("wkv_sb", [128, 4, 4096], BF16))
            wkv_b = P.buf("wkv")
            for kc in range(4):
                for j in range(6):
                    P.dma(P.pool, wq[:, kc, j * 512:(j + 1) * 512], wq_d[:, kc, j * 512:(j + 1) * 512], writes=[wq_b], sem_buf=wq_b)
            for kc in range(4):
                for j in range(8):
                    P.dma(P.pool, wkv[:, kc, j * 512:(j + 1) * 512], wkv_d[:, kc, j * 512:(j + 1) * 512], writes=[wkv_b], sem_buf=wkv_b)
            for kc in range(4):
                src = wq[:, kc, :].rearrange("p (h c) -> p h c", h=NH)[:, :, 128:192].rearrange("p h (i two) -> p h i two", two=2)
                dst = wqr[:, kc, :, :].rearrange("p h (i two) -> p h i two", two=2)
                P.op(P.act, lambda e: e.mul(out=dst[:, :, :, 0], in_=src[:, :, :, 1], mul=-1.0), reads=[wq_b], writes=[wqr_b])
                P.op(P.act, lambda e: e.copy(out=dst[:, :, :, 1], in_=src[:, :, :, 0]), reads=[wq_b], writes=[wqr_b])
            Vown = es.enter_context(nc.sbuf_tensor("Vown", [128, NH, 16, 128], BF16))
            Vown_b = P.buf("Vown")
            qn = [es.enter_context(nc.sbuf_tensor(f"qn{i}", [128, TOK], BF16)) for i in range(2)]
            qn_b = P.bufs("qn", 2)
            kn = [es.enter_context(nc.sbuf_tensor(f"kn{i}", [128, TOK], BF16)) for i in range(2)]
            kn_b = P.bufs("kn", 2)
            qr = [es.enter_context(nc.sbuf_tensor(f"qr{i}", [64, TOK], BF16)) for i in range(2)]
            qr_b = P.bufs("qr", 2)
            allcn = list(cnT_b)
            for t in range(16):
                for hg in range(4):
                    bi = bk.next()
                    pairs = [(cnT[:, 4 + kc, t * 128:(t + 1) * 128], wkv[:, kc, 2048 + hg * 512:2048 + (hg + 1) * 512]) for kc in range(4)]
                    mm_group(P, bk.f32(bi), [bk.b[bi]], pairs, reads=[cnT_b[t], wkv_b])
                    dst = Vown[:, hg * 4:(hg + 1) * 4, t, :]
                    src = bk.f32(bi).rearrange("p (a d) -> p a d", a=4)
                    if (t * 4 + hg) % 2 == 0:
                        P.op(P.dve, lambda e: e.tensor_copy(out=dst, in_=src), reads=[bk.b[bi]], writes=[Vown_b])
                    else:
                        P.op(P.act, lambda e: e.copy(out=dst, in_=src), reads=[bk.b[bi]], writes=[Vown_b])
            for h in range(NH):
                P.dma(P.sp, VR[h], Vown[:, h, :, :].rearrange("p t d -> p (t d)"), reads=[Vown_b], sem_buf=Vown_b)
            for h in range(NH):
                s = h % 2
                for blk in range(4):
                    tb = cnT_b[blk * 4:blk * 4 + 4]
                    bi = bk.next()
                    pairs = [(wq[:, kc, h * 192:h * 192 + 128], cnT[:, kc, blk * 512:(blk + 1) * 512]) for kc in range(4)]
                    mm_group(P, bk.f32(bi), [bk.b[bi]], pairs, reads=tb + [wq_b])
                    P.op(P.act, lambda e: e.copy(out=qn[s][:, blk * 512:(blk + 1) * 512], in_=bk.f32(bi)), reads=[bk.b[bi]], writes=[qn_b[s]])
                    bi = bk.next()
                    pairs = [(wkv[:, kc, h * 128:(h + 1) * 128], cnT[:, 4 + kc, blk * 512:(blk + 1) * 512]) for kc in range(4)]
                    mm_group(P, bk.f32(bi), [bk.b[bi]], pairs, reads=tb + [wkv_b])
                    P.op(P.dve, lambda e: e.tensor_copy(out=kn[s][:, blk * 512:(blk + 1) * 512], in_=bk.f32(bi)), reads=[bk.b[bi]], writes=[kn_b[s]])
                    bA = bk.next(); bB = bk.next()
                    pairs = [(wq[:, kc, h * 192 + 128:h * 192 + 192], cnT[:, kc, blk * 512:(blk + 1) * 512]) for kc in range(4)]
                    mm_group(P, bk.f32(bA)[0:64, :], [bk.b[bA]], pairs, reads=tb + [wq_b])
                    pairs = [(wqr[:, kc, h, :], cnT[:, kc, blk * 512:(blk + 1) * 512]) for kc in range(4)]
                    mm_group(P, bk.f32(bB)[0:64, :], [bk.b[bB]], pairs, reads=tb + [wqr_b])
                    rope_combine(P, bk, bA, bB, cosT, sinT, cs_b, qr[s][:, blk * 512:(blk + 1) * 512], qr_b[s], t1, t1_b, t2, t2_b, blk * 512)
                P.dma(P.sp, QN[h], qn[s][:], reads=[qn_b[s]], sem_buf=qn_b[s])
                P.dma(P.sp, KN[h], kn[s][:], reads=[kn_b[s]], sem_buf=kn_b[s])
                P.dma(P.sp, QR[h], qr[s][:], reads=[qr_b[s]], sem_buf=qr_b[s])
            P.finish()
    return nc


LK = 8192


def build_mla_b(nheads=NH):
    nc = bass.Bass("TRN2", target_bir_lowering=False)
    x_d = nc.dram_tensor("x", [TOK, D], F32, kind="ExternalInput").ap()
    QN = nc.dram_tensor("QN", [NH, 128, TOK], BF16, kind="ExternalInput").ap()
    QR = nc.dram_tensor("QR", [NH, 64, TOK], BF16, kind="ExternalInput").ap()
    GZ = nc.dram_tensor("GZ", [NH, 128, TOK], BF16, kind="ExternalInput").ap()
    KN = nc.dram_tensor("KN", [NH, 128, LK], BF16, kind="ExternalInput").ap()
    KR = nc.dram_tensor("KR", [64, LK], BF16, kind="ExternalInput").ap()
    VR = nc.dram_tensor("VR", [NH, 128, LK], BF16, kind="ExternalInput").ap()
    w_out = nc.dram_tensor("w_out", [128, 16, D], F32, kind="ExternalInput").ap()
    gpost_d = nc.dram_tensor("gpost", [128, D], F32, kind="ExternalInput").ap()
    xout = nc.dram_tensor("xout", [TOK, D], F32, kind="ExternalOutput").ap()
    gscr = nc.dram_tensor("gscr", [D, TOK], BF16, kind="Internal").ap()
    gscr_b = [Buf(f"gscr{h}") for h in range(NH)]
    scale = float(192 ** -0.5)
    NKC = LK // 128

    P = Prog(nc)
    with ExitStack() as es0:
        bk = Banks(P, es0)
        ones = es0.enter_context(nc.sbuf_tensor("ones_sb", [128, 128], BF16))
        ones_b = P.buf("ones")
        P.op(P.dve, lambda e: e.memset(ones[:], 1.0), writes=[ones_b])
        with ExitStack() as es:
            kr = es.enter_context(nc.sbuf_tensor("kr_sb", [128, LK], BF16))
            kr_b = P.buf("kr")
            P.op(P.pool, lambda e: e.memset(kr[64:128, :], 0.0), writes=[kr_b])
            P.dma(P.sp, kr[0:64, :], KR[:, :], writes=[kr_b], sem_buf=kr_b)
            NS = 2
            kn = [es.enter_context(nc.sbuf_tensor(f"kn{i}", [128, LK], BF16)) for i in range(NS)]
            vv = [es.enter_context(nc.sbuf_tensor(f"vv{i}", [128, NKC, 128], BF16)) for i in range(NS)]
            qn = [es.enter_context(nc.sbuf_tensor(f"qn{i}", [128, TOK], BF16)) for i in range(NS)]
            qr = [es.enter_context(nc.sbuf_tensor(f"qr{i}", [128, TOK], BF16)) for i in range(NS)]
            gz = [es.enter_context(nc.sbuf_tensor(f"gz{i}", [128, TOK], BF16)) for i in range(NS)]
            kn_b = P.bufs("kn", NS); vv_b = P.bufs("vv", NS); qn_b = P.bufs("qn", NS); qr_b = P.bufs("qr", NS); gz_b = P.bufs("gz", NS)
            for i in range(NS):
                P.op(P.pool, lambda e: e.memset(qr[i][64:128, :], 0.0), writes=[qr_b[i]])
            gTh = [es.enter_context(nc.sbuf_tensor(f"gTh{i}", [128, TOK], BF16)) for i in range(2)]
            gTh_b = P.bufs("gTh", 2)
            NPT = 3
            PT = [es.enter_context(nc.sbuf_tensor(f"PT{i}", [128, 512], BF16)) for i in range(NPT)]
            PT_b = P.bufs("PT", NPT)
            rsb = [es.enter_context(nc.sbuf_tensor(f"rsb{i}", [128, 512], F32)) for i in range(2)]
            rsb_b = P.bufs("rsb", 2)
            o1 = [es.enter_context(nc.sbuf_tensor(f"o1{i}", [128, 512], F32)) for i in range(2)]
            o1_b = P.bufs("o1", 2)

            def load_head(h):
                s = h % NS
                for g in range(4):
                    P.dma(P.sp, kn[s][:, g * 2048:(g + 1) * 2048], KN[h, :, g * 2048:(g + 1) * 2048], writes=[kn_b[s]], sem_buf=kn_b[s])
                for g in range(4):
                    P.dma(P.pool, vv[s][:, g * 16:(g + 1) * 16, :], VR[h, :, g * 2048:(g + 1) * 2048].rearrange("p (k d) -> p k d", d=128),
                          writes=[vv_b[s]], sem_buf=vv_b[s])
                P.dma(P.sp, qn[s][:], QN[h], writes=[qn_b[s]], sem_buf=qn_b[s])
                P.dma(P.sp, qr[s][0:64, :], QR[h], writes=[qr_b[s]], sem_buf=qr_b[s])
                P.dma(P.sp, gz[s][:], GZ[h], writes=[gz_b[s]], sem_buf=gz_b[s])

            NSB = 4
            unit = 0
            load_head(0)
            for h in range(nheads):
                s = h % NS
                if h + 1 < nheads:
                    load_head(h + 1)
                for qb in range(4):
                    ob = 4 + (qb % 2)
                    mb = 6 + (qb % 2)
                    q0 = qb * 512

                    def issue_S(kc, u):
                        bi = u % NSB
                        P.op(P.pe, lambda e: e.matmul(bk.f32(bi), kn[s][:, kc * 128:(kc + 1) * 128], qn[s][:, q0:q0 + 512], start=True, stop=False),
                             reads=[kn_b[s], qn_b[s]], writes=[bk.b[bi]], signal=False)
                        P.op(P.pe, lambda e: e.matmul(bk.f32(bi), kr[:, kc * 128:(kc + 1) * 128], qr[s][:, q0:q0 + 512], start=False, stop=True),
                             reads=[kr_b, qr_b[s]], writes=[bk.b[bi]], signal=True)

                    def do_exp(kc, u):
                        bi = u % NSB
                        pt = u % NPT
                        P.op(P.act, lambda e: e.activation(out=PT[pt][:], in_=bk.f32(bi), func=AF.Exp, scale=scale),
                             reads=[bk.b[bi]], writes=[PT_b[pt]])

                    def issue_PV(kc, u):
                        pt = u % NPT
                        P.op(P.pe, lambda e: e.matmul(bk.f32(ob)[:, :], vv[s][:, kc, :], PT[pt][:], start=(kc == 0), stop=(kc == NKC - 1)),
                             reads=[vv_b[s], PT_b[pt]], writes=[bk.b[ob]], signal=False)
                        P.op(P.pe, lambda e: e.matmul(bk.f32(mb)[:, :], ones[:], PT[pt][:], start=(kc == 0), stop=(kc == NKC - 1)),
                             reads=[ones_b, PT_b[pt]], writes=[bk.b[mb]], signal=True)

                    LOOK = 2
                    for kc in range(min(LOOK, NKC)):
                        issue_S(kc, unit + kc)
                    for kc in range(NKC):
                        do_exp(kc, unit + kc)
                        if kc + LOOK < NKC:
                            issue_S(kc + LOOK, unit + kc + LOOK)
                        issue_PV(kc, unit + kc)
                    unit += NKC
                    i2 = qb % 2
                    P.op(P.dve, lambda e: e.reciprocal(out=rsb[i2][:], in_=bk.f32(mb)), reads=[bk.b[mb]], writes=[rsb_b[i2]])
                    P.op(P.dve, lambda e: e.tensor_tensor(out=o1[i2][:], in0=bk.f32(ob), in1=rsb[i2][:], op=ALU.mult),
                         reads=[bk.b[ob], rsb_b[i2]], writes=[o1_b[i2]])
                    P.op(P.pool, lambda e: e.tensor_tensor(out=gTh[h % 2][:, q0:q0 + 512], in0=o1[i2][:], in1=gz[s][:, q0:q0 + 512], op=ALU.mult),
                         reads=[o1_b[i2], gz_b[s]], writes=[gTh_b[h % 2]])
                P.dma(P.sp, gscr[h * 128:(h + 1) * 128, :], gTh[h % 2][:], reads=[gTh_b[h % 2]], writes=[gscr_b[h]], sem_buf=gTh_b[h % 2])
        P.barrier()
        with ExitStack() as es:
            gT = es.enter_context(nc.sbuf_tensor("gT", [128, 16, TOK], BF16))
            gT_bufs = P.bufs("gT", 4)
            wo = es.enter_context(nc.sbuf_tensor("wo", [128, 16, D], BF16))
            wo_b = P.buf("wo")
            gpost_t = es.enter_context(nc.sbuf_tensor("gpost_t", [128, D], F32))
            gpost_b = P.buf("gpost")
            P.dma(P.sp, gpost_t[:], gpost_d[:, :], writes=[gpost_b], sem_buf=gpost_b)
            for kc in range(16):
                P.dma(P.sp, gT[:, kc, :], gscr[kc * 128:(kc + 1) * 128, :], reads=[gscr_b[kc]], writes=[gT_bufs[kc // 4]], sem_buf=gT_bufs[kc // 4])
            for g in range(8):
                P.dma(P.pool, wo[:, 2 * g:2 * g + 2, :], w_out[:, 2 * g:2 * g + 2, :], writes=[wo_b], sem_buf=wo_b)
            phase_out(P, es, bk, gT, gT_bufs, wo, wo_b, gpost_t, gpost_b, x_d, 0, xout, TOK // 128)
            P.finish()
    return nc


def rope_tables_T(q):
    inv_freq = (1.0 / (np.float32(10000.0) ** (np.arange(0, 64, 2, dtype=np.float32) / np.float32(64)))).astype(np.float32)
    pos = np.arange(q * TOK, (q + 1) * TOK, dtype=np.float32)
    ang = (pos[:, None] * inv_freq[None, :]).astype(np.float32)
    c = np.cos(ang).astype(np.float32)
    s = np.sin(ang).astype(np.float32)
    cT = np.repeat(c.T, 2, axis=0)
    sT = np.repeat(s.T, 2, axis=0)
    return np.ascontiguousarray(cT), np.ascontiguousarray(sT)


def run_mla_layer(x, g_pre, g_post, w_in, q_norm, w_q_b, kv_norm, w_kv_b, w_out, nheads=NH):
    B, L, _ = x.shape
    nca = build_mla_a()
    wlat = np.ascontiguousarray(w_in[:, :1088].reshape(16, 128, 1088).transpose(1, 0, 2))
    wz = np.ascontiguousarray(w_in[:, 1088:].reshape(16, 128, NH, 128).transpose(2, 1, 0, 3))
    wq = np.ascontiguousarray(w_q_b.reshape(4, 128, 3072).transpose(1, 0, 2))
    wkv4 = w_kv_b.reshape(4, 128, NH, 2, 128)
    wkv = np.ascontiguousarray(wkv4.transpose(1, 0, 3, 2, 4).reshape(128, 4, 4096))
    gqkv = np.ascontiguousarray(np.broadcast_to(np.concatenate([q_norm, kv_norm])[None, :], (128, 1024)))
    gpre = np.ascontiguousarray(np.broadcast_to(g_pre[None, :], (128, D)))
    gpost = np.ascontiguousarray(np.broadcast_to(g_post[None, :], (128, D)))
    ident = np.eye(128, dtype=np.float32)
    in_maps = []
    for c in range(NCORES):
        b, q = divmod(c, 4)
        cT, sT = rope_tables_T(q)
        in_maps.append({"x": np.ascontiguousarray(x[b, q * TOK:(q + 1) * TOK]), "gpre": gpre, "wlat": wlat, "wz": wz, "wq": wq,
                        "wkv": wkv, "gqkv": gqkv, "cosT": cT, "sinT": sT, "ident": ident})
    ra = run_bass_kernel_spmd(nca, in_maps, core_ids=list(range(NCORES))).results
    ncb = build_mla_b(nheads)
    w_out_r = np.ascontiguousarray(w_out.reshape(16, 128, D).transpose(1, 0, 2))
    in_maps = []
    for b in range(B):
        KN = np.concatenate([ra[4 * b + q]["KN"] for q in range(4)], axis=2)
        KR = np.concatenate([ra[4 * b + q]["KR"] for q in range(4)], axis=1)
        VR = np.concatenate([ra[4 * b + q]["VR"] for q in range(4)], axis=2)
        for q in range(4):
            r = ra[4 * b + q]
            in_maps.append({"x": np.ascontiguousarray(x[b, q * TOK:(q + 1) * TOK]), "QN": r["QN"], "QR": r["QR"], "GZ": r["GZ"],
                            "KN": KN, "KR": KR, "VR": VR, "w_out": w_out_r, "gpost": gpost})
    rb = run_bass_kernel_spmd(ncb, in_maps, core_ids=list(range(NCORES))).results
    out = np.empty_like(x)
    for c in range(NCORES):
        b, q = divmod(c, 4)
        out[b, q * TOK:(q + 1) * TOK] = rb[c]["xout"]
    return out


def kernel(x, norm_pre, norm_post, na_w_in, na_rpb, na_w_out, mla_w_in, mla_q_norm, mla_w_q_b, mla_kv_norm,
           mla_w_kv_b, mla_w_out):
    x = np.ascontiguousarray(np.asarray(x, dtype=np.float32))
    f = lambda a: np.asarray(a, dtype=np.float32)
    norm_pre, norm_post = f(norm_pre), f(norm_post)
    for i in range(4):
        j = i // 2
        if i % 2 == 0:
            x = run_na_layer(x, norm_pre[i], norm_post[i], f(na_w_in[j]), f(na_rpb[j]), f(na_w_out[j]))
        else:
            x = run_mla_layer(x, norm_pre[i], norm_post[i], f(mla_w_in[j]), f(mla_q_norm[j]), f(mla_w_q_b[j]),
                              f(mla_kv_norm[j]), f(mla_w_kv_b[j]), f(mla_w_out[j]))
    return x
```

```python
import numpy as np
import ml_dtypes
from contextlib import ExitStack
import concourse.bass as bass
import concourse.mybir as mybir
from concourse.bass_utils import run_bass_kernel_spmd

F32 = mybir.dt.float32
BF16 = mybir.dt.bfloat16
AF = mybir.ActivationFunctionType
ALU = mybir.AluOpType
AX = mybir.AxisListType

D = 2048
NCORES = 8
TOK = 2048
HALO = 256
TH = TOK + 2 * HALO
NH = 16
EPS = 1e-6
NEG = -30000.0


class Buf:
    __slots__ = ("name", "w", "r", "dsem", "dcount")

    def __init__(self, name):
        self.name = name
        self.w = []
        self.r = []
        self.dsem = None
        self.dcount = 0


class Queue:
    def __init__(self, prog, name, eng):
        self.name = name
        self.eng = eng
        self.sem = prog.nc.alloc_semaphore("q_" + name)
        self.count = 0
        self.known = {}
        self.pending = []


class Prog:
    def __init__(self, nc):
        self.nc = nc
        self.pe = Queue(self, "pe", nc.tensor)
        self.act = Queue(self, "act", nc.scalar)
        self.dve = Queue(self, "dve", nc.vector)
        self.pool = Queue(self, "pool", nc.gpsimd)
        self.sp = Queue(self, "sp", nc.sync)
        self.queues = [self.pe, self.act, self.dve, self.pool, self.sp]
        self.dbufs = []
        self.out_events = []

    def buf(self, name):
        return Buf(name)

    def bufs(self, name, n):
        return [Buf(f"{name}{i}") for i in range(n)]

    def _wait(self, q, events):
        best = {}
        for (sem, val) in events:
            k = id(sem)
            if k not in best or best[k][1] < val:
                best[k] = (sem, val)
        for k, (sem, val) in best.items():
            if q.known.get(k, 0) >= val:
                continue
            q.eng.wait_ge(sem, val)
            q.known[k] = val

    def _deps(self, reads, writes):
        deps = []
        for b in reads:
            deps += b.w
        for b in writes:
            deps += b.w
            deps += b.r
        return deps

    def _check_pending(self, q, writes, reads):
        for qq in self.queues:
            for (b, kind) in qq.pending:
                if qq is q:
                    continue
                for wb in writes:
                    assert wb is not b, f"write to {b.name} while unsignaled op pending on {qq.name}"
                if kind == 'w':
                    for rb in reads:
                        assert rb is not b, f"read of {b.name} while unsignaled write pending on {qq.name}"

    @staticmethod
    def _rec(b, kind, ev):
        if kind == 'r':
            if ev not in b.r:
                b.r.append(ev)
        else:
            if ev not in b.w:
                b.w.append(ev)

    def op(self, q, fn, reads=(), writes=(), signal=True):
        self._check_pending(q, writes, reads)
        self._wait(q, self._deps(reads, writes))
        ins = fn(q.eng)
        for b in writes:
            if b.r:
                b.r = []
                b.w = []
        for b in reads:
            q.pending.append((b, 'r'))
        for b in writes:
            q.pending.append((b, 'w'))
        if signal:
            q.count += 1
            ins.then_inc(q.sem, 1)
            ev = (q.sem, q.count)
            for (b, kind) in q.pending:
                self._rec(b, kind, ev)
            q.pending = []
        return ins

    def dma(self, q, out, in_, reads=(), writes=(), sem_buf=None, **kw):
        self._check_pending(q, writes, reads)
        self._wait(q, self._deps(reads, writes))
        ins = q.eng.dma_start(out=out, in_=in_, **kw)
        sb = sem_buf
        if sb.dsem is None:
            sb.dsem = self.nc.alloc_semaphore("d_" + sb.name)
            self.dbufs.append(sb)
        sb.dcount += 16
        ins.then_inc(sb.dsem, 16)
        ev = (sb.dsem, sb.dcount)
        for b in writes:
            if b.r:
                b.r = []
                b.w = []
        for b in reads:
            self._rec(b, 'r', ev)
        for b in writes:
            self._rec(b, 'w', ev)
        return ev

    def barrier(self):
        evs = []
        for q in self.queues:
            assert not q.pending, f"pending on {q.name} at barrier"
            if q.count:
                evs.append((q.sem, q.count))
        for b in self.dbufs:
            evs.append((b.dsem, b.dcount))
        for q in self.queues:
            self._wait(q, evs)

    def finish(self):
        evs = [(b.dsem, b.dcount) for b in self.dbufs]
        for q in self.queues:
            if q.count:
                evs.append((q.sem, q.count))
        self._wait(self.sp, evs)


def mm_group(P, ps_ap, ps_bufs, pairs, reads, signal_last=True, first_start=True):
    n = len(pairs)
    for i, (l, r) in enumerate(pairs):
        P.op(P.pe, lambda e, l=l, r=r, i=i: e.matmul(ps_ap, l, r, start=(i == 0 and first_start), stop=(i == n - 1)),
             reads=reads, writes=ps_bufs, signal=(signal_last and i == n - 1))


def rstd_from_ss(P, rstd_ap, ss_ap, n, rbufs, sbufs):
    P.op(P.act, lambda e: e.activation(out=rstd_ap, in_=ss_ap, func=AF.Sqrt, scale=1.0 / n, bias=EPS),
         reads=sbufs, writes=rbufs)
    P.op(P.dve, lambda e: e.reciprocal(out=rstd_ap, in_=rstd_ap), reads=rbufs, writes=rbufs)


class Banks:
    def __init__(self, P, es):
        self.t = es.enter_context(P.nc.psum_tensor("psum_all", [128, 8 * 512], F32))
        self.b = P.bufs("bank", 8)
        self.rr = 0

    def f32(self, i, n=1):
        return self.t[:, i * 512:(i + n) * 512]

    def bf16(self, i):
        return self.t[:, i * 512:(i + 1) * 512].bitcast(BF16)

    def next(self):
        i = self.rr
        self.rr = (self.rr + 1) % 8
        return i


def phase_hT(P, es, bk, x_dram, row0, ntiles, gpre_t, gpre_b, ident, ident_b, hT, hT_bufs):
    nc = P.nc
    NS = 3
    xs = [es.enter_context(nc.sbuf_tensor(f"p1_xs{i}", [128, D], F32)) for i in range(NS)]
    xs_b = P.bufs("p1_xs", NS)
    hb = [es.enter_context(nc.sbuf_tensor(f"p1_hb{i}", [128, D], BF16)) for i in range(2)]
    hb_b = P.bufs("p1_hb", 2)
    junk = es.enter_context(nc.sbuf_tensor("p1_junk", [128, D], BF16))
    junk_b = P.buf("p1_junk")
    st = es.enter_context(nc.sbuf_tensor("p1_st", [128, 2 * NS], F32))
    ss_b = P.bufs("p1_ss", NS)
    rs_b = P.bufs("p1_rs", NS)
    for t in range(ntiles):
        s = t % NS
        P.dma(P.sp, xs[s][:], x_dram[row0 + t * 128: row0 + (t + 1) * 128, :], writes=[xs_b[s]], sem_buf=xs_b[s])
        ss_ap = st[:, 2 * s:2 * s + 1]
        rs_ap = st[:, 2 * s + 1:2 * s + 2]
        P.op(P.act, lambda e: e.activation(out=junk[:], in_=xs[s][:], func=AF.Square, accum_out=ss_ap),
             reads=[xs_b[s]], writes=[junk_b, ss_b[s]])
        rstd_from_ss(P, rs_ap, ss_ap, D, [rs_b[s]], [ss_b[s]])
        h = t % 2
        P.op(P.dve, lambda e: e.scalar_tensor_tensor(out=hb[h][:], in0=xs[s][:], scalar=rs_ap, in1=gpre_t[:],
                                                     op0=ALU.mult, op1=ALU.mult),
             reads=[xs_b[s], rs_b[s], gpre_b], writes=[hb_b[h]])
        for half in range(2):
            bi = bk.next()
            pst = bk.bf16(bi)
            for j in range(8):
                kc = half * 8 + j
                P.op(P.pe, lambda e, kc=kc, j=j: e.transpose(pst[:, j * 128:(j + 1) * 128], hb[h][:, kc * 128:(kc + 1) * 128], ident[:]),
                     reads=[hb_b[h], ident_b], writes=[bk.b[bi]], signal=(j == 7))
            eng = P.act if half == 0 else P.dve
            dst = hT[:, half * 8:(half + 1) * 8, t * 128:(t + 1) * 128]
            src = pst.rearrange("p (k n) -> p k n", k=8)
            if eng is P.act:
                P.op(eng, lambda e: e.copy(out=dst, in_=src), reads=[bk.b[bi]], writes=[hT_bufs[t]])
            else:
                P.op(eng, lambda e: e.tensor_copy(out=dst, in_=src), reads=[bk.b[bi]], writes=[hT_bufs[t]])


def phase_out(P, es, bk, gT, gT_bufs, wo, wo_b, gpost_t, gpost_b, x_dram, xrow0, out_dram, ntiles):
    nc = P.nc
    NS = 2
    xs = [es.enter_context(nc.sbuf_tensor(f"p3_xs{i}", [128, D], F32)) for i in range(NS)]
    xs_b = P.bufs("p3_xs", NS)
    ys = [es.enter_context(nc.sbuf_tensor(f"p3_ys{i}", [128, D], F32)) for i in range(NS)]
    ys_b = P.bufs("p3_ys", NS)
    junk = es.enter_context(nc.sbuf_tensor("p3_junk", [128, D], BF16))
    junk_b = P.buf("p3_junk")
    st = es.enter_context(nc.sbuf_tensor("p3_st", [128, 2 * NS], F32))
    ss_b = P.bufs("p3_ss", NS)
    rs_b = P.bufs("p3_rs", NS)
    for t in range(ntiles):
        s = t % NS
        P.dma(P.sp, xs[s][:], x_dram[xrow0 + t * 128: xrow0 + (t + 1) * 128, :], reads=[], writes=[xs_b[s]], sem_buf=xs_b[s])
        b0 = 4 * (t % 2)
        for nb in range(4):
            pairs = [(gT[:, kc, t * 128:(t + 1) * 128], wo[:, kc, nb * 512:(nb + 1) * 512]) for kc in range(16)]
            mm_group(P, bk.f32(b0 + nb), [bk.b[b0 + nb]], pairs, reads=list(gT_bufs) + [wo_b])
        ybanks = [bk.b[b0 + i] for i in range(4)]
        yps = bk.f32(b0, 4)
        ss_ap = st[:, 2 * s:2 * s + 1]
        rs_ap = st[:, 2 * s + 1:2 * s + 2]
        P.op(P.act, lambda e: e.activation(out=junk[:], in_=yps, func=AF.Square, accum_out=ss_ap),
             reads=ybanks, writes=[junk_b, ss_b[s]])
        rstd_from_ss(P, rs_ap, ss_ap, D, [rs_b[s]], [ss_b[s]])
        P.op(P.dve, lambda e: e.scalar_tensor_tensor(out=ys[s][:], in0=yps, scalar=rs_ap, in1=gpost_t[:],
                                                     op0=ALU.mult, op1=ALU.mult),
             reads=ybanks + [rs_b[s], gpost_b], writes=[ys_b[s]])
        P.op(P.pool, lambda e: e.tensor_tensor(out=ys[s][:], in0=ys[s][:], in1=xs[s][:], op=ALU.add),
             reads=[ys_b[s], xs_b[s]], writes=[ys_b[s]])
        P.dma(P.sp, out_dram[t * 128:(t + 1) * 128, :], ys[s][:], reads=[ys_b[s]], writes=[], sem_buf=ys_b[s])


def load_consts(P, es, ident_d):
    nc = P.nc
    ident = es.enter_context(nc.sbuf_tensor("ident_sb", [128, 128], BF16))
    ident_b = P.buf("ident")
    P.dma(P.pool, ident[:], ident_d[:, :], writes=[ident_b], sem_buf=ident_b)
    ones = es.enter_context(nc.sbuf_tensor("ones_sb", [128, 128], BF16))
    ones_b = P.buf("ones")
    P.op(P.dve, lambda e: e.memset(ones[:], 1.0), writes=[ones_b])
    return ident, ident_b, ones, ones_b


def na_chunks(m):
    if m < 2:
        return list(range(m, m + 6))
    if m >= 14:
        return list(range(m - 1, m + 5))
    return list(range(m, m + 5))


def build_na(nheads=NH):
    nc = bass.Bass("TRN2", target_bir_lowering=False)
    xh = nc.dram_tensor("xh", [TH, D], F32, kind="ExternalInput").ap()
    w_in = nc.dram_tensor("w_in", [NH, 128, 16, 512], F32, kind="ExternalInput").ap()
    w_out = nc.dram_tensor("w_out", [128, 16, D], F32, kind="ExternalInput").ap()
    gpre_d = nc.dram_tensor("gpre", [128, D], F32, kind="ExternalInput").ap()
    gpost_d = nc.dram_tensor("gpost", [128, D], F32, kind="ExternalInput").ap()
    bint_d = nc.dram_tensor("bint", [NH, 128, 5 * 128], F32, kind="ExternalInput").ap()
    bedge_d = nc.dram_tensor("bedge", [NH, 128, 4, 6 * 128], F32, kind="ExternalInput").ap()
    ident_d = nc.dram_tensor("ident", [128, 128], F32, kind="ExternalInput").ap()
    xout = nc.dram_tensor("xout", [TOK, D], F32, kind="ExternalOutput").ap()
    gscr = nc.dram_tensor("gscr", [D, TOK], BF16, kind="Internal").ap()
    gscr_b = [Buf(f"gscr{h}") for h in range(NH)]
    scale = float(128 ** -0.5)

    P = Prog(nc)
    with ExitStack() as es0:
        bk = Banks(P, es0)
        ident, ident_b, ones, ones_b = load_consts(P, es0, ident_d)
        with ExitStack() as es1:
            hT = es1.enter_context(nc.sbuf_tensor("hT", [128, 16, TH], BF16))
            hT_bufs = P.bufs("hT", TH // 128)
            with ExitStack() as es:
                gpre_t = es.enter_context(nc.sbuf_tensor("gpre_t", [128, D], F32))
                gpre_b = P.buf("gpre")
                P.dma(P.sp, gpre_t[:], gpre_d[:, :], writes=[gpre_b], sem_buf=gpre_b)
                phase_hT(P, es, bk, xh, 0, TH // 128, gpre_t, gpre_b, ident, ident_b, hT, hT_bufs)
            P.barrier()
            with ExitStack() as es:
                NW = 2
                wb = [es.enter_context(nc.sbuf_tensor(f"wb{i}", [128, 16, 512], BF16)) for i in range(NW)]
                wb_b = P.bufs("wb", NW)
                NHB = 2
                qT = [es.enter_context(nc.sbuf_tensor(f"qT{i}", [128, TOK], BF16)) for i in range(NHB)]
                kT = [es.enter_context(nc.sbuf_tensor(f"kT{i}", [128, TH], BF16)) for i in range(NHB)]
                gz = [es.enter_context(nc.sbuf_tensor(f"gz{i}", [128, TOK], BF16)) for i in range(NHB)]
                V = [es.enter_context(nc.sbuf_tensor(f"V{i}", [128, TH // 128, 128], BF16)) for i in range(NHB)]
                qT_b = P.bufs("qT", NHB); kT_b = P.bufs("kT", NHB); gz_b = P.bufs("gz", NHB); V_b = P.bufs("V", NHB)
                vT = es.enter_context(nc.sbuf_tensor("vT", [128, TH], BF16))
                vT_b = P.bufs("vT", TH // 512)
                gTh = [es.enter_context(nc.sbuf_tensor(f"gTh{i}", [128, TOK], BF16)) for i in range(2)]
                gTh_b = P.bufs("gTh", 2)
                bint = [es.enter_context(nc.sbuf_tensor(f"bint{i}", [128, 5 * 128], F32)) for i in range(2)]
                bint_b = P.bufs("bint", 2)
                bedge = es.enter_context(nc.sbuf_tensor("bedge_sb", [128, 4, 6 * 128], F32))
                bedge_b = P.buf("bedge")
                NSB = 2
                Sb = [es.enter_context(nc.sbuf_tensor(f"Sb{i}", [128, 6 * 128], F32)) for i in range(NSB)]
                Sb_b = P.bufs("Sb", NSB)
                PT = [es.enter_context(nc.sbuf_tensor(f"PT{i}", [128, 6 * 128], BF16)) for i in range(NSB)]
                PT_b = P.bufs("PT", NSB)
                rsb = [es.enter_context(nc.sbuf_tensor(f"rsb{i}", [128, 128], F32)) for i in range(2)]
                rsb_b = P.bufs("rsb", 2)
                o1 = [es.enter_context(nc.sbuf_tensor(f"o1{i}", [128, 128], F32)) for i in range(2)]
                o1_b = P.bufs("o1", 2)

                def load_w(h):
                    s = h % NW
                    for g in range(4):
                        P.dma(P.pool, wb[s][:, 4 * g:4 * g + 4, :], w_in[h, :, 4 * g:4 * g + 4, :], writes=[wb_b[s]], sem_buf=wb_b[s])

                def load_bias(h):
                    P.dma(P.sp, bint[h % 2][:], bint_d[h], writes=[bint_b[h % 2]], sem_buf=bint_b[h % 2])
                    P.dma(P.sp, bedge[:], bedge_d[h], writes=[bedge_b], sem_buf=bedge_b)

                load_w(0)
                for h in range(nheads):
                    s = h % NW
                    hs = h % NHB
                    if h + 1 < nheads:
                        load_w(h + 1)
                    load_bias(h)
                    w = wb[s]
                    allh = list(hT_bufs)
                    own = hT_bufs[2:18]
                    ev = 0
                    for (j, dst, dst_b, ntok, tok0, hb_list, kind) in (
                            (0, qT[hs], qT_b[hs], TOK, HALO, own, "q"),
                            (1, kT[hs], kT_b[hs], TH, 0, allh, "k"),
                            (3, gz[hs], gz_b[hs], TOK, HALO, own, "z")):
                        for blk in range(ntok // 512):
                            bi = bk.next()
                            c0 = tok0 + blk * 512
                            pairs = [(w[:, kc, j * 128:(j + 1) * 128], hT[:, kc, c0:c0 + 512]) for kc in range(16)]
                            mm_group(P, bk.f32(bi), [bk.b[bi]], pairs, reads=hT_bufs[c0 // 128:c0 // 128 + 4] + [wb_b[s]])
                            dap = dst[:, blk * 512:(blk + 1) * 512]
                            if kind == "z":
                                P.op(P.act, lambda e: e.activation(out=dap, in_=bk.f32(bi), func=AF.Silu),
                                     reads=[bk.b[bi]], writes=[dst_b])
                            elif ev % 2 == 0:
                                P.op(P.act, lambda e: e.copy(out=dap, in_=bk.f32(bi)), reads=[bk.b[bi]], writes=[dst_b])
                            else:
                                P.op(P.dve, lambda e: e.tensor_copy(out=dap, in_=bk.f32(bi)), reads=[bk.b[bi]], writes=[dst_b])
                            ev += 1
                    for blk in range(TH // 512):
                        bi = bk.next()
                        c0 = blk * 512
                        pairs = [(w[:, kc, 256:384], hT[:, kc, c0:c0 + 512]) for kc in range(16)]
                        mm_group(P, bk.f32(bi), [bk.b[bi]], pairs, reads=hT_bufs[c0 // 128:c0 // 128 + 4] + [wb_b[s]])
                        dap = vT[:, c0:c0 + 512]
                        if blk % 2 == 0:
                            P.op(P.dve, lambda e: e.tensor_copy(out=dap, in_=bk.f32(bi)), reads=[bk.b[bi]], writes=[vT_b[blk]])
                        else:
                            P.op(P.act, lambda e: e.copy(out=dap, in_=bk.f32(bi)), reads=[bk.b[bi]], writes=[vT_b[blk]])
                    for tg in range(TH // 512):
                        bi = bk.next()
                        pst = bk.bf16(bi)
                        for i in range(4):
                            t = tg * 4 + i
                            P.op(P.pe, lambda e, i=i, t=t: e.transpose(pst[:, i * 128:(i + 1) * 128], vT[:, t * 128:(t + 1) * 128], ident[:]),
                                 reads=[vT_b[tg], ident_b], writes=[bk.b[bi]], signal=(i == 3))
                        dap = V[hs][:, tg * 4:(tg + 1) * 4, :]
                        sap = pst[:, 0:512].rearrange("p (a b) -> p a b", a=4)
                        if tg % 2 == 0:
                            P.op(P.act, lambda e: e.copy(out=dap, in_=sap), reads=[bk.b[bi]], writes=[V_b[hs]])
                        else:
                            P.op(P.dve, lambda e: e.tensor_copy(out=dap, in_=sap), reads=[bk.b[bi]], writes=[V_b[hs]])

                    SB = [(0, 1), (2, 3)]
                    OB = [(4, 5), (6, 7)]

                    def issue_S(m):
                        cl = na_chunks(m)
                        b0, b1 = SB[m % 2]
                        for ci, c in enumerate(cl):
                            bi = b0 if ci < 4 else b1
                            col = (ci % 4) * 128
                            P.op(P.pe, lambda e, c=c, bi=bi, col=col: e.matmul(
                                bk.f32(bi)[:, col:col + 128], kT[hs][:, c * 128:(c + 1) * 128],
                                qT[hs][:, m * 128:(m + 1) * 128], start=True, stop=True),
                                reads=[kT_b[hs], qT_b[hs]], writes=[bk.b[b0], bk.b[b1]], signal=(ci == len(cl) - 1))

                    def softmax(m):
                        cl = na_chunks(m)
                        n = len(cl) * 128
                        b0, b1 = SB[m % 2]
                        sps = bk.f32(b0, 2)[:, 0:n]
                        if m < 2:
                            bt, btb = bedge[:, m, 0:n], bedge_b
                        elif m >= 14:
                            bt, btb = bedge[:, m - 12, 0:n], bedge_b
                        else:
                            bt, btb = bint[h % 2][:, 0:n], bint_b[h % 2]
                        sl = m % NSB
                        P.op(P.dve, lambda e: e.scalar_tensor_tensor(out=Sb[sl][:, 0:n], in0=sps, scalar=scale, in1=bt,
                                                                     op0=ALU.mult, op1=ALU.add),
                             reads=[bk.b[b0], bk.b[b1], btb], writes=[Sb_b[sl]])
                        P.op(P.act, lambda e: e.activation(out=PT[sl][:, 0:n], in_=Sb[sl][:, 0:n], func=AF.Exp),
                             reads=[Sb_b[sl]], writes=[PT_b[sl]])

                    def issue_O(m):
                        cl = na_chunks(m)
                        sl = m % NSB
                        ob, mb = OB[m % 2]
                        pairs = [(V[hs][:, c, :], PT[sl][:, ci * 128:(ci + 1) * 128]) for ci, c in enumerate(cl)]
                        mm_group(P, bk.f32(ob)[:, 0:128], [bk.b[ob]], pairs, reads=[V_b[hs], PT_b[sl]])
                        pairs = [(ones[:], PT[sl][:, ci * 128:(ci + 1) * 128]) for ci, c in enumerate(cl)]
                        mm_group(P, bk.f32(mb)[:, 0:128], [bk.b[mb]], pairs, reads=[ones_b, PT_b[sl]])

                    def fin(m):
                        ob, mb = OB[m % 2]
                        i2 = m % 2
                        P.op(P.dve, lambda e: e.reciprocal(out=rsb[i2][:], in_=bk.f32(mb)[:, 0:128]),
                             reads=[bk.b[mb]], writes=[rsb_b[i2]])
                        P.op(P.dve, lambda e: e.tensor_tensor(out=o1[i2][:], in0=bk.f32(ob)[:, 0:128], in1=rsb[i2][:], op=ALU.mult),
                             reads=[bk.b[ob], rsb_b[i2]], writes=[o1_b[i2]])
                        P.op(P.pool, lambda e: e.tensor_tensor(out=gTh[h % 2][:, m * 128:(m + 1) * 128], in0=o1[i2][:],
                                                               in1=gz[hs][:, m * 128:(m + 1) * 128], op=ALU.mult),
                             reads=[o1_b[i2], gz_b[hs]], writes=[gTh_b[h % 2]])

                    issue_S(0)
                    softmax(0)
                    issue_S(1)
                    for m in range(16):
                        if m + 1 < 16:
                            softmax(m + 1)
                        issue_O(m)
                        if m + 2 < 16:
                            issue_S(m + 2)
                        fin(m)
                    P.dma(P.sp, gscr[h * 128:(h + 1) * 128, :], gTh[h % 2][:], reads=[gTh_b[h % 2]], writes=[gscr_b[h]], sem_buf=gTh_b[h % 2])
            P.barrier()
        with ExitStack() as es:
            gT = es.enter_context(nc.sbuf_tensor("gT", [128, 16, TOK], BF16))
            gT_bufs = P.bufs("gT", 4)
            wo = es.enter_context(nc.sbuf_tensor("wo", [128, 16, D], BF16))
            wo_b = P.buf("wo")
            gpost_t = es.enter_context(nc.sbuf_tensor("gpost_t", [128, D], F32))
            gpost_b = P.buf("gpost")
            P.dma(P.sp, gpost_t[:], gpost_d[:, :], writes=[gpost_b], sem_buf=gpost_b)
            for kc in range(16):
                P.dma(P.sp, gT[:, kc, :], gscr[kc * 128:(kc + 1) * 128, :], reads=[gscr_b[kc]], writes=[gT_bufs[kc // 4]], sem_buf=gT_bufs[kc // 4])
            for g in range(8):
                P.dma(P.pool, wo[:, 2 * g:2 * g + 2, :], w_out[:, 2 * g:2 * g + 2, :], writes=[wo_b], sem_buf=wo_b)
            phase_out(P, es, bk, gT, gT_bufs, wo, wo_b, gpost_t, gpost_b, xh, HALO, xout, TOK // 128)
            P.finish()
    return nc


def na_bias_tiles(rpb):
    H = rpb.shape[0]
    rows = 128
    out = []
    kc = np.arange(64)
    qc = np.arange(64)
    col_start = np.clip(qc - 8, 0, 48)
    col_ok = (kc[:, None] >= col_start[None, :]) & (kc[:, None] < col_start[None, :] + 16)
    dc = np.clip(kc[:, None] - qc[None, :] + 15, 0, 30)

    def tile_for(Mg, chunks_g):
        t = np.full((H, 128, len(chunks_g), 128), NEG, np.float32)
        for ci, cg in enumerate(chunks_g):
            for kr2 in range(2):
                kr = 2 * cg + kr2
                if kr < 0 or kr >= rows:
                    continue
                for qr2 in range(2):
                    r = 2 * Mg + qr2
                    rs = min(max(r - 4, 0), rows - 8)
                    if not (rs <= kr < rs + 8):
                        continue
                    dr = kr - r + 7
                    vals = rpb[:, dr, :][:, dc]
                    vals = np.where(col_ok[None], vals, np.float32(NEG))
                    t[:, kr2 * 64:(kr2 + 1) * 64, ci, qr2 * 64:(qr2 + 1) * 64] = vals
        return t

    for q in range(4):
        Mi = 16 * q + 8
        bint = tile_for(Mi, [Mi - 2 + i for i in range(5)]).reshape(H, 128, 640)
        be = np.empty((H, 128, 4, 768), np.float32)
        for idx, m in enumerate((0, 1, 14, 15)):
            Mg = 16 * q + m
            cl = na_chunks(m)
            be[:, :, idx, :] = tile_for(Mg, [16 * q + (c - 2) for c in cl]).reshape(H, 128, 768)
        out.append((np.ascontiguousarray(bint), np.ascontiguousarray(be)))
    return out


def run_na_layer(x, g_pre, g_post, w_in, rpb, w_out, nheads=NH):
    B, L, _ = x.shape
    nc = build_na(nheads)
    w_in_r = np.ascontiguousarray(w_in.reshape(16, 128, 4, NH, 128).transpose(3, 1, 0, 2, 4).reshape(NH, 128, 16, 512))
    w_out_r = np.ascontiguousarray(w_out.reshape(16, 128, D).transpose(1, 0, 2))
    gpre = np.ascontiguousarray(np.broadcast_to(g_pre[None, :], (128, D)))
    gpost = np.ascontiguousarray(np.broadcast_to(g_post[None, :], (128, D)))
    ident = np.eye(128, dtype=np.float32)
    tiles = na_bias_tiles(rpb)
    in_maps = []
    for c in range(NCORES):
        b, q = divmod(c, 4)
        xh = np.zeros((TH, D), np.float32)
        lo = q * TOK - HALO
        hi = lo + TH
        slo, shi = max(lo, 0), min(hi, L)
        xh[slo - lo: shi - lo] = x[b, slo:shi]
        in_maps.append({"xh": xh, "w_in": w_in_r, "w_out": w_out_r, "gpre": gpre, "gpost": gpost,
                        "bint": tiles[q][0], "bedge": tiles[q][1], "ident": ident})
    res = run_bass_kernel_spmd(nc, in_maps, core_ids=list(range(NCORES)))
    out = np.empty_like(x)
    for c in range(NCORES):
        b, q = divmod(c, 4)
        out[b, q * TOK:(q + 1) * TOK] = res.results[c]["xout"]
    return out


def rope_combine(P, bk, bA, bB, cosT, sinT, c_b, dst_ap, dst_b, t1, t1_b, t2, t2_b, c0, n=512):
    P.op(P.dve, lambda e: e.tensor_tensor(out=t1[:, 0:n], in0=bk.f32(bA)[0:64, 0:n], in1=cosT[:, c0:c0 + n], op=ALU.mult),
         reads=[bk.b[bA], c_b], writes=[t1_b])
    P.op(P.dve, lambda e: e.tensor_tensor(out=t2[:, 0:n], in0=bk.f32(bB)[0:64, 0:n], in1=sinT[:, c0:c0 + n], op=ALU.mult),
         reads=[bk.b[bB], c_b], writes=[t2_b])
    P.op(P.pool, lambda e: e.tensor_tensor(out=dst_ap, in0=t1[:, 0:n], in1=t2[:, 0:n], op=ALU.add),
         reads=[t1_b, t2_b], writes=[dst_b])


def build_mla_a():
    nc = bass.Bass("TRN2", target_bir_lowering=False)
    x_d = nc.dram_tensor("x", [TOK, D], F32, kind="ExternalInput").ap()
    gpre_d = nc.dram_tensor("gpre", [128, D], F32, kind="ExternalInput").ap()
    wlat_d = nc.dram_tensor("wlat", [128, 16, 1088], F32, kind="ExternalInput").ap()
    wz_d = nc.dram_tensor("wz", [NH, 128, 16, 128], F32, kind="ExternalInput").ap()
    wq_d = nc.dram_tensor("wq", [128, 4, 3072], F32, kind="ExternalInput").ap()
    wkv_d = nc.dram_tensor("wkv", [128, 4, 4096], F32, kind="ExternalInput").ap()
    gq_d = nc.dram_tensor("gqkv", [128, 1024], F32, kind="ExternalInput").ap()
    cos_d = nc.dram_tensor("cosT", [64, TOK], F32, kind="ExternalInput").ap()
    sin_d = nc.dram_tensor("sinT", [64, TOK], F32, kind="ExternalInput").ap()
    ident_d = nc.dram_tensor("ident", [128, 128], F32, kind="ExternalInput").ap()
    QN = nc.dram_tensor("QN", [NH, 128, TOK], BF16, kind="ExternalOutput").ap()
    QR = nc.dram_tensor("QR", [NH, 64, TOK], BF16, kind="ExternalOutput").ap()
    KN = nc.dram_tensor("KN", [NH, 128, TOK], BF16, kind="ExternalOutput").ap()
    KR = nc.dram_tensor("KR", [64, TOK], BF16, kind="ExternalOutput").ap()
    VR = nc.dram_tensor("VR", [NH, 128, 16 * 128], BF16, kind="ExternalOutput").ap()
    GZ = nc.dram_tensor("GZ", [NH, 128, TOK], BF16, kind="ExternalOutput").ap()

    P = Prog(nc)
    with ExitStack() as es0:
        bk = Banks(P, es0)
        ident, ident_b, ones, ones_b = load_consts(P, es0, ident_d)
        cnT = es0.enter_context(nc.sbuf_tensor("cnT", [128, 8, TOK], BF16))
        cnT_b = P.bufs("cnT", 16)
        cosT = es0.enter_context(nc.sbuf_tensor("cos_sb", [64, TOK], F32))
        sinT = es0.enter_context(nc.sbuf_tensor("sin_sb", [64, TOK], F32))
        cs_b = P.buf("cossin")
        P.dma(P.sp, cosT[:], cos_d[:, :], writes=[cs_b], sem_buf=cs_b)
        P.dma(P.sp, sinT[:], sin_d[:, :], writes=[cs_b], sem_buf=cs_b)
        t1 = es0.enter_context(nc.sbuf_tensor("rope_t1", [64, 512], F32))
        t2 = es0.enter_context(nc.sbuf_tensor("rope_t2", [64, 512], F32))
        t1_b = P.buf("t1"); t2_b = P.buf("t2")
        with ExitStack() as es1:
            hT = es1.enter_context(nc.sbuf_tensor("hT", [128, 16, TOK], BF16))
            hT_bufs = P.bufs("hT", 16)
            with ExitStack() as es:
                gpre_t = es.enter_context(nc.sbuf_tensor("gpre_t", [128, D], F32))
                gpre_b = P.buf("gpre")
                P.dma(P.sp, gpre_t[:], gpre_d[:, :], writes=[gpre_b], sem_buf=gpre_b)
                phase_hT(P, es, bk, x_d, 0, 16, gpre_t, gpre_b, ident, ident_b, hT, hT_bufs)
            P.barrier()
            with ExitStack() as es:
                wlat = es.enter_context(nc.sbuf_tensor("wlat_sb", [128, 16, 1024], BF16))
                wlat_b = P.buf("wlat")
                wkr = es.enter_context(nc.sbuf_tensor("wkr_sb", [128, 16, 64], BF16))
                wkr_b = P.buf("wkr")
                wkrr = es.enter_context(nc.sbuf_tensor("wkrr_sb", [128, 16, 64], BF16))
                wkrr_b = P.buf("wkrr")
                gq_t = es.enter_context(nc.sbuf_tensor("gq_t", [128, 1024], F32))
                gq_b = P.buf("gq")
                P.dma(P.sp, gq_t[:], gq_d[:, :], writes=[gq_b], sem_buf=gq_b)
                for g in range(4):
                    for half in range(2):
                        P.dma(P.pool, wlat[:, 4 * g:4 * g + 4, half * 512:(half + 1) * 512],
                              wlat_d[:, 4 * g:4 * g + 4, half * 512:(half + 1) * 512], writes=[wlat_b], sem_buf=wlat_b)
                P.dma(P.pool, wkr[:], wlat_d[:, :, 1024:1088], writes=[wkr_b], sem_buf=wkr_b)
                wv = wkr[:].rearrange("p k (i two) -> p k i two", two=2)
                rv = wkrr[:].rearrange("p k (i two) -> p k i two", two=2)
                P.op(P.act, lambda e: e.mul(out=rv[:, :, :, 0], in_=wv[:, :, :, 1], mul=-1.0), reads=[wkr_b], writes=[wkrr_b])
                P.op(P.act, lambda e: e.copy(out=rv[:, :, :, 1], in_=wv[:, :, :, 0]), reads=[wkr_b], writes=[wkrr_b])
                junk = es.enter_context(nc.sbuf_tensor("a_junk", [128, 512], BF16))
                junk_b = P.buf("a_junk")
                st = es.enter_context(nc.sbuf_tensor("a_st", [128, 8], F32))
                ss_b = P.bufs("a_ss", 2); rs_b = P.bufs("a_rs", 2)
                cn = [es.enter_context(nc.sbuf_tensor(f"a_cn{i}", [128, 1024], BF16)) for i in range(2)]
                cn_b = P.bufs("a_cn", 2)
                for t in range(16):
                    s = t % 2
                    bq = bk.next(); bkv = bk.next()
                    for (bi, c0) in ((bq, 0), (bkv, 512)):
                        pairs = [(hT[:, kc, t * 128:(t + 1) * 128], wlat[:, kc, c0:c0 + 512]) for kc in range(16)]
                        mm_group(P, bk.f32(bi), [bk.b[bi]], pairs, reads=[hT_bufs[t], wlat_b])
                    ss_ap = st[:, 4 * s:4 * s + 2]
                    rs_ap = st[:, 4 * s + 2:4 * s + 4]
                    P.op(P.act, lambda e: e.activation(out=junk[:], in_=bk.f32(bq), func=AF.Square, accum_out=ss_ap[:, 0:1]),
                         reads=[bk.b[bq]], writes=[junk_b, ss_b[s]])
                    P.op(P.act, lambda e: e.activation(out=junk[:], in_=bk.f32(bkv), func=AF.Square, accum_out=ss_ap[:, 1:2]),
                         reads=[bk.b[bkv]], writes=[junk_b, ss_b[s]])
                    rstd_from_ss(P, rs_ap, ss_ap, 512, [rs_b[s]], [ss_b[s]])
                    P.op(P.dve, lambda e: e.scalar_tensor_tensor(out=cn[s][:, 0:512], in0=bk.f32(bq), scalar=rs_ap[:, 0:1],
                                                                 in1=gq_t[:, 0:512], op0=ALU.mult, op1=ALU.mult),
                         reads=[bk.b[bq], rs_b[s], gq_b], writes=[cn_b[s]])
                    P.op(P.dve, lambda e: e.scalar_tensor_tensor(out=cn[s][:, 512:1024], in0=bk.f32(bkv), scalar=rs_ap[:, 1:2],
                                                                 in1=gq_t[:, 512:1024], op0=ALU.mult, op1=ALU.mult),
                         reads=[bk.b[bkv], rs_b[s], gq_b], writes=[cn_b[s]])
                    bt = bk.next()
                    pst = bk.bf16(bt)
                    for j in range(8):
                        P.op(P.pe, lambda e, j=j: e.transpose(pst[:, j * 128:(j + 1) * 128], cn[s][:, j * 128:(j + 1) * 128], ident[:]),
                             reads=[cn_b[s], ident_b], writes=[bk.b[bt]], signal=(j == 7))
                    P.op(P.act, lambda e: e.copy(out=cnT[:, :, t * 128:(t + 1) * 128], in_=pst.rearrange("p (k n) -> p k n", k=8)),
                         reads=[bk.b[bt]], writes=[cnT_b[t]])
                krT = es.enter_context(nc.sbuf_tensor("krT", [64, TOK], BF16))
                krT_b = P.buf("krT")
                for blk in range(4):
                    bA = bk.next(); bB = bk.next()
                    for (bi, wsrc, wsb) in ((bA, wkr, wkr_b), (bB, wkrr, wkrr_b)):
                        pairs = [(wsrc[:, kc, :], hT[:, kc, blk * 512:(blk + 1) * 512]) for kc in range(16)]
                        mm_group(P, bk.f32(bi)[0:64, :], [bk.b[bi]], pairs, reads=hT_bufs[blk * 4:blk * 4 + 4] + [wsb])
                    rope_combine(P, bk, bA, bB, cosT, sinT, cs_b, krT[:, blk * 512:(blk + 1) * 512], krT_b, t1, t1_b, t2, t2_b, blk * 512)
                P.dma(P.sp, KR[:, :], krT[:], reads=[krT_b], sem_buf=krT_b)
                wz = [es.enter_context(nc.sbuf_tensor(f"wz{i}", [128, 16, 128], BF16)) for i in range(2)]
                wz_b = P.bufs("wz", 2)
                gz = [es.enter_context(nc.sbuf_tensor(f"gz{i}", [128, TOK], BF16)) for i in range(2)]
                gz_b = P.bufs("gz", 2)
                P.dma(P.pool, wz[0][:], wz_d[0], writes=[wz_b[0]], sem_buf=wz_b[0])
                for h in range(NH):
                    s = h % 2
                    if h + 1 < NH:
                        P.dma(P.pool, wz[(h + 1) % 2][:], wz_d[h + 1], writes=[wz_b[(h + 1) % 2]], sem_buf=wz_b[(h + 1) % 2])
                    for blk in range(4):
                        bi = bk.next()
                        pairs = [(wz[s][:, kc, :], hT[:, kc, blk * 512:(blk + 1) * 512]) for kc in range(16)]
                        mm_group(P, bk.f32(bi), [bk.b[bi]], pairs, reads=hT_bufs[blk * 4:blk * 4 + 4] + [wz_b[s]])
                        P.op(P.act, lambda e: e.activation(out=gz[s][:, blk * 512:(blk + 1) * 512], in_=bk.f32(bi), func=AF.Silu),
                             reads=[bk.b[bi]], writes=[gz_b[s]])
                    P.dma(P.sp, GZ[h], gz[s][:], reads=[gz_b[s]], sem_buf=gz_b[s])
            P.barrier()
        with ExitStack() as es:
            wq = es.enter_context(nc.sbuf_tensor("wq_sb", [128, 4, 3072], BF16))
            wq_b = P.buf("wq")
            wqr = es.enter_context(nc.sbuf_tensor("wqr_sb", [128, 4, NH, 64], BF16))
            wqr_b = P.buf("wqr")
            wkv = es.enter_context(nc.sbuf_tensor("wkv_sb", [128, 4, 4096], BF16))
            wkv_b = P.buf("wkv")
            for kc in range(4):
                for j in range(6):
                    P.dma(P.pool, wq[:, kc, j * 512:(j + 1) * 512], wq_d[:, kc, j * 512:(j + 1) * 512], writes=[wq_b], sem_buf=wq_b)
            for kc in range(4):
                for j in range(8):
                    P.dma(P.pool, wkv[:, kc, j * 512:(j + 1) * 512], wkv_d[:, kc, j * 512:(j + 1) * 512], writes=[wkv_b], sem_buf=wkv_b)
            for kc in range(4):
                src = wq[:, kc, :].rearrange("p (h c) -> p h c", h=NH)[:, :, 128:192].rearrange("p h (i two) -> p h i two", two=2)
                dst = wqr[:, kc, :, :].rearrange("p h (i two) -> p h i two", two=2)
                P.op(P.act, lambda e: e.mul(out=dst[:, :, :, 0], in_=src[:, :, :, 1], mul=-1.0), reads=[wq_b], writes=[wqr_b])
                P.op(P.act, lambda e: e.copy(out=dst[:, :, :, 1], in_=src[:, :, :, 0]), reads=[wq_b], writes=[wqr_b])
            Vown = es.enter_context(nc.sbuf_tensor("Vown", [128, NH, 16, 128], BF16))
            Vown_b = P.buf("Vown")
            qn = [es.enter_context(nc.sbuf_tensor(f"qn{i}", [128, TOK], BF16)) for i in range(2)]
            qn_b = P.bufs("qn", 2)
            kn = [es.enter_context(nc.sbuf_tensor(f"kn{i}", [128, TOK], BF16)) for i in range(2)]
            kn_b = P.bufs("kn", 2)
            qr = [es.enter_context(nc.sbuf_tensor(f"qr{i}", [64, TOK], BF16)) for i in range(2)]
            qr_b = P.bufs("qr", 2)
            allcn = list(cnT_b)
            for t in range(16):
                for hg in range(4):
                    bi = bk.next()
                    pairs = [(cnT[:, 4 + kc, t * 128:(t + 1) * 128], wkv[:, kc, 2048 + hg * 512:2048 + (hg + 1) * 512]) for kc in range(4)]
                    mm_group(P, bk.f32(bi), [bk.b[bi]], pairs, reads=[cnT_b[t], wkv_b])
                    dst = Vown[:, hg * 4:(hg + 1) * 4, t, :]
                    src = bk.f32(bi).rearrange("p (a d) -> p a d", a=4)
                    if (t * 4 + hg) % 2 == 0:
                        P.op(P.dve, lambda e: e.tensor_copy(out=dst, in_=src), reads=[bk.b[bi]], writes=[Vown_b])
                    else:
                        P.op(P.act, lambda e: e.copy(out=dst, in_=src), reads=[bk.b[bi]], writes=[Vown_b])
            for h in range(NH):
                P.dma(P.sp, VR[h], Vown[:, h, :, :].rearrange("p t d -> p (t d)"), reads=[Vown_b], sem_buf=Vown_b)
            for h in range(NH):
                s = h % 2
                for blk in range(4):
                    tb = cnT_b[blk * 4:blk * 4 + 4]
                    bi = bk.next()
                    pairs = [(wq[:, kc, h * 192:h * 192 + 128], cnT[:, kc, blk * 512:(blk + 1) * 512]) for kc in range(4)]
                    mm_group(P, bk.f32(bi), [bk.b[bi]], pairs, reads=tb + [wq_b])
                    P.op(P.act, lambda e: e.copy(out=qn[s][:, blk * 512:(blk + 1) * 512], in_=bk.f32(bi)), reads=[bk.b[bi]], writes=[qn_b[s]])
                    bi = bk.next()
                    pairs = [(wkv[:, kc, h * 128:(h + 1) * 128], cnT[:, 4 + kc, blk * 512:(blk + 1) * 512]) for kc in range(4)]
                    mm_group(P, bk.f32(bi), [bk.b[bi]], pairs, reads=tb + [wkv_b])
                    P.op(P.dve, lambda e: e.tensor_copy(out=kn[s][:, blk * 512:(blk + 1) * 512], in_=bk.f32(bi)), reads=[bk.b[bi]], writes=[kn_b[s]])
                    bA = bk.next(); bB = bk.next()
                    pairs = [(wq[:, kc, h * 192 + 128:h * 192 + 192], cnT[:, kc, blk * 512:(blk + 1) * 512]) for kc in range(4)]
                    mm_group(P, bk.f32(bA)[0:64, :], [bk.b[bA]], pairs, reads=tb + [wq_b])
                    pairs = [(wqr[:, kc, h, :], cnT[:, kc, blk * 512:(blk + 1) * 512]) for kc in range(4)]
                    mm_group(P, bk.f32(bB)[0:64, :], [bk.b[bB]], pairs, reads=tb + [wqr_b])
                    rope_combine(P, bk, bA, bB, cosT, sinT, cs_b, qr[s][:, blk * 512:(blk + 1) * 512], qr_b[s], t1, t1_b, t2, t2_b, blk * 512)
                P.dma(P.sp, QN[h], qn[s][:], reads=[qn_b[s]], sem_buf=qn_b[s])
                P.dma(P.sp, KN[h], kn[s][:], reads=[kn_b[s]], sem_buf=kn_b[s])
                P.dma(P.sp, QR[h], qr[s][:], reads=[qr_b[s]], sem_buf=qr_b[s])
            P.finish()
    return nc


LK = 8192


def build_mla_b(nheads=NH):
    nc = bass.Bass("TRN2", target_bir_lowering=False)
    x_d = nc.dram_tensor("x", [TOK, D], F32, kind="ExternalInput").ap()
    QN = nc.dram_tensor("QN", [NH, 128, TOK], BF16, kind="ExternalInput").ap()
    QR = nc.dram_tensor("QR", [NH, 64, TOK], BF16, kind="ExternalInput").ap()
    GZ = nc.dram_tensor("GZ", [NH, 128, TOK], BF16, kind="ExternalInput").ap()
    KN = nc.dram_tensor("KN", [NH, 128, LK], BF16, kind="ExternalInput").ap()
    KR = nc.dram_tensor("KR", [64, LK], BF16, kind="ExternalInput").ap()
    VR = nc.dram_tensor("VR", [NH, 128, LK], BF16, kind="ExternalInput").ap()
    w_out = nc.dram_tensor("w_out", [128, 16, D], F32, kind="ExternalInput").ap()
    gpost_d = nc.dram_tensor("gpost", [128, D], F32, kind="ExternalInput").ap()
    xout = nc.dram_tensor("xout", [TOK, D], F32, kind="ExternalOutput").ap()
    gscr = nc.dram_tensor("gscr", [D, TOK], BF16, kind="Internal").ap()
    gscr_b = [Buf(f"gscr{h}") for h in range(NH)]
    scale = float(192 ** -0.5)
    NKC = LK // 128

    P = Prog(nc)
    with ExitStack() as es0:
        bk = Banks(P, es0)
        ones = es0.enter_context(nc.sbuf_tensor("ones_sb", [128, 128], BF16))
        ones_b = P.buf("ones")
        P.op(P.dve, lambda e: e.memset(ones[:], 1.0), writes=[ones_b])
        with ExitStack() as es:
            kr = es.enter_context(nc.sbuf_tensor("kr_sb", [128, LK], BF16))
            kr_b = P.buf("kr")
            P.op(P.pool, lambda e: e.memset(kr[64:128, :], 0.0), writes=[kr_b])
            P.dma(P.sp, kr[0:64, :], KR[:, :], writes=[kr_b], sem_buf=kr_b)
            NS = 2
            kn = [es.enter_context(nc.sbuf_tensor(f"kn{i}", [128, LK], BF16)) for i in range(NS)]
            vv = [es.enter_context(nc.sbuf_tensor(f"vv{i}", [128, NKC, 128], BF16)) for i in range(NS)]
            qn = [es.enter_context(nc.sbuf_tensor(f"qn{i}", [128, TOK], BF16)) for i in range(NS)]
            qr = [es.enter_context(nc.sbuf_tensor(f"qr{i}", [128, TOK], BF16)) for i in range(NS)]
            gz = [es.enter_context(nc.sbuf_tensor(f"gz{i}", [128, TOK], BF16)) for i in range(NS)]
            kn_b = P.bufs("kn", NS); vv_b = P.bufs("vv", NS); qn_b = P.bufs("qn", NS); qr_b = P.bufs("qr", NS); gz_b = P.bufs("gz", NS)
            for i in range(NS):
                P.op(P.pool, lambda e: e.memset(qr[i][64:128, :], 0.0), writes=[qr_b[i]])
            gTh = [es.enter_context(nc.sbuf_tensor(f"gTh{i}", [128, TOK], BF16)) for i in range(2)]
            gTh_b = P.bufs("gTh", 2)
            NPT = 3
            PT = [es.enter_context(nc.sbuf_tensor(f"PT{i}", [128, 512], BF16)) for i in range(NPT)]
            PT_b = P.bufs("PT", NPT)
            rsb = [es.enter_context(nc.sbuf_tensor(f"rsb{i}", [128, 512], F32)) for i in range(2)]
            rsb_b = P.bufs("rsb", 2)
            o1 = [es.enter_context(nc.sbuf_tensor(f"o1{i}", [128, 512], F32)) for i in range(2)]
            o1_b = P.bufs("o1", 2)

            def load_head(h):
                s = h % NS
                for g in range(4):
                    P.dma(P.sp, kn[s][:, g * 2048:(g + 1) * 2048], KN[h, :, g * 2048:(g + 1) * 2048], writes=[kn_b[s]], sem_buf=kn_b[s])
                for g in range(4):
                    P.dma(P.pool, vv[s][:, g * 16:(g + 1) * 16, :], VR[h, :, g * 2048:(g + 1) * 2048].rearrange("p (k d) -> p k d", d=128),
                          writes=[vv_b[s]], sem_buf=vv_b[s])
                P.dma(P.sp, qn[s][:], QN[h], writes=[qn_b[s]], sem_buf=qn_b[s])
                P.dma(P.sp, qr[s][0:64, :], QR[h], writes=[qr_b[s]], sem_buf=qr_b[s])
                P.dma(P.sp, gz[s][:], GZ[h], writes=[gz_b[s]], sem_buf=gz_b[s])

            NSB = 4
            unit = 0
            load_head(0)
            for h in range(nheads):
                s = h % NS
                if h + 1 < nheads:
                    load_head(h + 1)
                for qb in range(4):
                    ob = 4 + (qb % 2)
                    mb = 6 + (qb % 2)
                    q0 = qb * 512

                    def issue_S(kc, u):
                        bi = u % NSB
                        P.op(P.pe, lambda e: e.matmul(bk.f32(bi), kn[s][:, kc * 128:(kc + 1) * 128], qn[s][:, q0:q0 + 512], start=True, stop=False),
                             reads=[kn_b[s], qn_b[s]], writes=[bk.b[bi]], signal=False)
                        P.op(P.pe, lambda e: e.matmul(bk.f32(bi), kr[:, kc * 128:(kc + 1) * 128], qr[s][:, q0:q0 + 512], start=False, stop=True),
                             reads=[kr_b, qr_b[s]], writes=[bk.b[bi]], signal=True)

                    def do_exp(kc, u):
                        bi = u % NSB
                        pt = u % NPT
                        P.op(P.act, lambda e: e.activation(out=PT[pt][:], in_=bk.f32(bi), func=AF.Exp, scale=scale),
                             reads=[bk.b[bi]], writes=[PT_b[pt]])

                    def issue_PV(kc, u):
                        pt = u % NPT
                        P.op(P.pe, lambda e: e.matmul(bk.f32(ob)[:, :], vv[s][:, kc, :], PT[pt][:], start=(kc == 0), stop=(kc == NKC - 1)),
                             reads=[vv_b[s], PT_b[pt]], writes=[bk.b[ob]], signal=False)
                        P.op(P.pe, lambda e: e.matmul(bk.f32(mb)[:, :], ones[:], PT[pt][:], start=(kc == 0), stop=(kc == NKC - 1)),
                             reads=[ones_b, PT_b[pt]], writes=[bk.b[mb]], signal=True)

                    LOOK = 2
                    for kc in range(min(LOOK, NKC)):
                        issue_S(kc, unit + kc)
                    for kc in range(NKC):
                        do_exp(kc, unit + kc)
                        if kc + LOOK < NKC:
                            issue_S(kc + LOOK, unit + kc + LOOK)
                        issue_PV(kc, unit + kc)
                    unit += NKC
                    i2 = qb % 2
                    P.op(P.dve, lambda e: e.reciprocal(out=rsb[i2][:], in_=bk.f32(mb)), reads=[bk.b[mb]], writes=[rsb_b[i2]])
                    P.op(P.dve, lambda e: e.tensor_tensor(out=o1[i2][:], in0=bk.f32(ob), in1=rsb[i2][:], op=ALU.mult),
                         reads=[bk.b[ob], rsb_b[i2]], writes=[o1_b[i2]])
                    P.op(P.pool, lambda e: e.tensor_tensor(out=gTh[h % 2][:, q0:q0 + 512], in0=o1[i2][:], in1=gz[s][:, q0:q0 + 512], op=ALU.mult),
                         reads=[o1_b[i2], gz_b[s]], writes=[gTh_b[h % 2]])
                P.dma(P.sp, gscr[h * 128:(h + 1) * 128, :], gTh[h % 2][:], reads=[gTh_b[h % 2]], writes=[gscr_b[h]], sem_buf=gTh_b[h % 2])
        P.barrier()
        with ExitStack() as es:
            gT = es.enter_context(nc.sbuf_tensor("gT", [128, 16, TOK], BF16))
            gT_bufs = P.bufs("gT", 4)
            wo = es.enter_context(nc.sbuf_tensor("wo", [128, 16, D], BF16))
            wo_b = P.buf("wo")
            gpost_t = es.enter_context(nc.sbuf_tensor("gpost_t", [128, D], F32))
            gpost_b = P.buf("gpost")
            P.dma(P.sp, gpost_t[:], gpost_d[:, :], writes=[gpost_b], sem_buf=gpost_b)
            for kc in range(16):
                P.dma(P.sp, gT[:, kc, :], gscr[kc * 128:(kc + 1) * 128, :], reads=[gscr_b[kc]], writes=[gT_bufs[kc // 4]], sem_buf=gT_bufs[kc // 4])
            for g in range(8):
                P.dma(P.pool, wo[:, 2 * g:2 * g + 2, :], w_out[:, 2 * g:2 * g + 2, :], writes=[wo_b], sem_buf=wo_b)
            phase_out(P, es, bk, gT, gT_bufs, wo, wo_b, gpost_t, gpost_b, x_d, 0, xout, TOK // 128)
            P.finish()
    return nc


def rope_tables_T(q):
    inv_freq = (1.0 / (np.float32(10000.0) ** (np.arange(0, 64, 2, dtype=np.float32) / np.float32(64)))).astype(np.float32)
    pos = np.arange(q * TOK, (q + 1) * TOK, dtype=np.float32)
    ang = (pos[:, None] * inv_freq[None, :]).astype(np.float32)
    c = np.cos(ang).astype(np.float32)
    s = np.sin(ang).astype(np.float32)
    cT = np.repeat(c.T, 2, axis=0)
    sT = np.repeat(s.T, 2, axis=0)
    return np.ascontiguousarray(cT), np.ascontiguousarray(sT)


def run_mla_layer(x, g_pre, g_post, w_in, q_norm, w_q_b, kv_norm, w_kv_b, w_out, nheads=NH):
    B, L, _ = x.shape
    nca = build_mla_a()
    wlat = np.ascontiguousarray(w_in[:, :1088].reshape(16, 128, 1088).transpose(1, 0, 2))
    wz = np.ascontiguousarray(w_in[:, 1088:].reshape(16, 128, NH, 128).transpose(2, 1, 0, 3))
    wq = np.ascontiguousarray(w_q_b.reshape(4, 128, 3072).transpose(1, 0, 2))
    wkv4 = w_kv_b.reshape(4, 128, NH, 2, 128)
    wkv = np.ascontiguousarray(wkv4.transpose(1, 0, 3, 2, 4).reshape(128, 4, 4096))
    gqkv = np.ascontiguousarray(np.broadcast_to(np.concatenate([q_norm, kv_norm])[None, :], (128, 1024)))
    gpre = np.ascontiguousarray(np.broadcast_to(g_pre[None, :], (128, D)))
    gpost = np.ascontiguousarray(np.broadcast_to(g_post[None, :], (128, D)))
    ident = np.eye(128, dtype=np.float32)
    in_maps = []
    for c in range(NCORES):
        b, q = divmod(c, 4)
        cT, sT = rope_tables_T(q)
        in_maps.append({"x": np.ascontiguousarray(x[b, q * TOK:(q + 1) * TOK]), "gpre": gpre, "wlat": wlat, "wz": wz, "wq": wq,
                        "wkv": wkv, "gqkv": gqkv, "cosT": cT, "sinT": sT, "ident": ident})
    ra = run_bass_kernel_spmd(nca, in_maps, core_ids=list(range(NCORES))).results
    ncb = build_mla_b(nheads)
    w_out_r = np.ascontiguousarray(w_out.reshape(16, 128, D).transpose(1, 0, 2))
    in_maps = []
    for b in range(B):
        KN = np.concatenate([ra[4 * b + q]["KN"] for q in range(4)], axis=2)
        KR = np.concatenate([ra[4 * b + q]["KR"] for q in range(4)], axis=1)
        VR = np.concatenate([ra[4 * b + q]["VR"] for q in range(4)], axis=2)
        for q in range(4):
            r = ra[4 * b + q]
            in_maps.append({"x": np.ascontiguousarray(x[b, q * TOK:(q + 1) * TOK]), "QN": r["QN"], "QR": r["QR"], "GZ": r["GZ"],
                            "KN": KN, "KR": KR, "VR": VR, "w_out": w_out_r, "gpost": gpost})
    rb = run_bass_kernel_spmd(ncb, in_maps, core_ids=list(range(NCORES))).results
    out = np.empty_like(x)
    for c in range(NCORES):
        b, q = divmod(c, 4)
        out[b, q * TOK:(q + 1) * TOK] = rb[c]["xout"]
    return out


def kernel(x, norm_pre, norm_post, na_w_in, na_rpb, na_w_out, mla_w_in, mla_q_norm, mla_w_q_b, mla_kv_norm,
           mla_w_kv_b, mla_w_out):
    x = np.ascontiguousarray(np.asarray(x, dtype=np.float32))
    f = lambda a: np.asarray(a, dtype=np.float32)
    norm_pre, norm_post = f(norm_pre), f(norm_post)
    for i in range(4):
        j = i // 2
        if i % 2 == 0:
            x = run_na_layer(x, norm_pre[i], norm_post[i], f(na_w_in[j]), f(na_rpb[j]), f(na_w_out[j]))
        else:
            x = run_mla_layer(x, norm_pre[i], norm_post[i], f(mla_w_in[j]), f(mla_q_norm[j]), f(mla_w_q_b[j]),
                              f(mla_kv_norm[j]), f(mla_w_kv_b[j]), f(mla_w_out[j]))
    return x
```

```python
import numpy as np
import ml_dtypes
from contextlib import ExitStack
import concourse.bass as bass
import concourse.mybir as mybir
from concourse.bass_utils import run_bass_kernel_spmd

F32 = mybir.dt.float32
BF16 = mybir.dt.bfloat16
AF = mybir.ActivationFunctionType
ALU = mybir.AluOpType
AX = mybir.AxisListType

D = 2048
NCORES = 8
TOK = 2048
HALO = 256
TH = TOK + 2 * HALO
NH = 16
EPS = 1e-6
NEG = -30000.0


class Buf:
    __slots__ = ("name", "w", "r", "dsem", "dcount")

    def __init__(self, name):
        self.name = name
        self.w = []
        self.r = []
        self.dsem = None
        self.dcount = 0


class Queue:
    def __init__(self, prog, name, eng):
        self.name = name
        self.eng = eng
        self.sem = prog.nc.alloc_semaphore("q_" + name)
        self.count = 0
        self.known = {}
        self.pending = []


class Prog:
    def __init__(self, nc):
        self.nc = nc
        self.pe = Queue(self, "pe", nc.tensor)
        self.act = Queue(self, "act", nc.scalar)
        self.dve = Queue(self, "dve", nc.vector)
        self.pool = Queue(self, "pool", nc.gpsimd)
        self.sp = Queue(self, "sp", nc.sync)
        self.queues = [self.pe, self.act, self.dve, self.pool, self.sp]
        self.dbufs = []
        self.out_events = []

    def buf(self, name):
        return Buf(name)

    def bufs(self, name, n):
        return [Buf(f"{name}{i}") for i in range(n)]

    def _wait(self, q, events):
        best = {}
        for (sem, val) in events:
            k = id(sem)
            if k not in best or best[k][1] < val:
                best[k] = (sem, val)
        for k, (sem, val) in best.items():
            if q.known.get(k, 0) >= val:
                continue
            q.eng.wait_ge(sem, val)
            q.known[k] = val

    def _deps(self, reads, writes):
        deps = []
        for b in reads:
            deps += b.w
        for b in writes:
            deps += b.w
            deps += b.r
        return deps

    def _check_pending(self, q, writes, reads):
        for qq in self.queues:
            for (b, kind) in qq.pending:
                if qq is q:
                    continue
                for wb in writes:
                    assert wb is not b, f"write to {b.name} while unsignaled op pending on {qq.name}"
                if kind == 'w':
                    for rb in reads:
                        assert rb is not b, f"read of {b.name} while unsignaled write pending on {qq.name}"

    @staticmethod
    def _rec(b, kind, ev):
        if kind == 'r':
            if ev not in b.r:
                b.r.append(ev)
        else:
            if ev not in b.w:
                b.w.append(ev)

    def op(self, q, fn, reads=(), writes=(), signal=True):
        self._check_pending(q, writes, reads)
        self._wait(q, self._deps(reads, writes))
        ins = fn(q.eng)
        for b in writes:
            if b.r:
                b.r = []
                b.w = []
        for b in reads:
            q.pending.append((b, 'r'))
        for b in writes:
            q.pending.append((b, 'w'))
        if signal:
            q.count += 1
            ins.then_inc(q.sem, 1)
            ev = (q.sem, q.count)
            for (b, kind) in q.pending:
                self._rec(b, kind, ev)
            q.pending = []
        return ins

    def dma(self, q, out, in_, reads=(), writes=(), sem_buf=None, **kw):
        self._check_pending(q, writes, reads)
        self._wait(q, self._deps(reads, writes))
        ins = q.eng.dma_start(out=out, in_=in_, **kw)
        sb = sem_buf
        if sb.dsem is None:
            sb.dsem = self.nc.alloc_semaphore("d_" + sb.name)
            self.dbufs.append(sb)
        sb.dcount += 16
        ins.then_inc(sb.dsem, 16)
        ev = (sb.dsem, sb.dcount)
        for b in writes:
            if b.r:
                b.r = []
                b.w = []
        for b in reads:
            self._rec(b, 'r', ev)
        for b in writes:
            self._rec(b, 'w', ev)
        return ev

    def barrier(self):
        evs = []
        for q in self.queues:
            assert not q.pending, f"pending on {q.name} at barrier"
            if q.count:
                evs.append((q.sem, q.count))
        for b in self.dbufs:
            evs.append((b.dsem, b.dcount))
        for q in self.queues:
            self._wait(q, evs)

    def finish(self):
        evs = [(b.dsem, b.dcount) for b in self.dbufs]
        for q in self.queues:
            if q.count:
                evs.append((q.sem, q.count))
        self._wait(self.sp, evs)


def mm_group(P, ps_ap, ps_bufs, pairs, reads, signal_last=True, first_start=True):
    n = len(pairs)
    for i, (l, r) in enumerate(pairs):
        P.op(P.pe, lambda e, l=l, r=r, i=i: e.matmul(ps_ap, l, r, start=(i == 0 and first_start), stop=(i == n - 1)),
             reads=reads, writes=ps_bufs, signal=(signal_last and i == n - 1))


def rstd_from_ss(P, rstd_ap, ss_ap, n, rbufs, sbufs):
    P.op(P.act, lambda e: e.activation(out=rstd_ap, in_=ss_ap, func=AF.Sqrt, scale=1.0 / n, bias=EPS),
         reads=sbufs, writes=rbufs)
    P.op(P.dve, lambda e: e.reciprocal(out=rstd_ap, in_=rstd_ap), reads=rbufs, writes=rbufs)


class Banks:
    def __init__(self, P, es):
        self.t = es.enter_context(P.nc.psum_tensor("psum_all", [128, 8 * 512], F32))
        self.b = P.bufs("bank", 8)
        self.rr = 0

    def f32(self, i, n=1):
        return self.t[:, i * 512:(i + n) * 512]

    def bf16(self, i):
        return self.t[:, i * 512:(i + 1) * 512].bitcast(BF16)

    def next(self):
        i = self.rr
        self.rr = (self.rr + 1) % 8
        return i


def phase_hT(P, es, bk, x_dram, row0, ntiles, gpre_t, gpre_b, ident, ident_b, hT, hT_bufs):
    nc = P.nc
    NS = 3
    xs = [es.enter_context(nc.sbuf_tensor(f"p1_xs{i}", [128, D], F32)) for i in range(NS)]
    xs_b = P.bufs("p1_xs", NS)
    hb = [es.enter_context(nc.sbuf_tensor(f"p1_hb{i}", [128, D], BF16)) for i in range(2)]
    hb_b = P.bufs("p1_hb", 2)
    junk = es.enter_context(nc.sbuf_tensor("p1_junk", [128, D], BF16))
    junk_b = P.buf("p1_junk")
    st = es.enter_context(nc.sbuf_tensor("p1_st", [128, 2 * NS], F32))
    ss_b = P.bufs("p1_ss", NS)
    rs_b = P.bufs("p1_rs", NS)
    for t in range(ntiles):
        s = t % NS
        P.dma(P.sp, xs[s][:], x_dram[row0 + t * 128: row0 + (t + 1) * 128, :], writes=[xs_b[s]], sem_buf=xs_b[s])
        ss_ap = st[:, 2 * s:2 * s + 1]
        rs_ap = st[:, 2 * s + 1:2 * s + 2]
        P.op(P.act, lambda e: e.activation(out=junk[:], in_=xs[s][:], func=AF.Square, accum_out=ss_ap),
             reads=[xs_b[s]], writes=[junk_b, ss_b[s]])
        rstd_from_ss(P, rs_ap, ss_ap, D, [rs_b[s]], [ss_b[s]])
        h = t % 2
        P.op(P.dve, lambda e: e.scalar_tensor_tensor(out=hb[h][:], in0=xs[s][:], scalar=rs_ap, in1=gpre_t[:],
                                                     op0=ALU.mult, op1=ALU.mult),
             reads=[xs_b[s], rs_b[s], gpre_b], writes=[hb_b[h]])
        for half in range(2):
            bi = bk.next()
            pst = bk.bf16(bi)
            for j in range(8):
                kc = half * 8 + j
                P.op(P.pe, lambda e, kc=kc, j=j: e.transpose(pst[:, j * 128:(j + 1) * 128], hb[h][:, kc * 128:(kc + 1) * 128], ident[:]),
                     reads=[hb_b[h], ident_b], writes=[bk.b[bi]], signal=(j == 7))
            eng = P.act if half == 0 else P.dve
            dst = hT[:, half * 8:(half + 1) * 8, t * 128:(t + 1) * 128]
            src = pst.rearrange("p (k n) -> p k n", k=8)
            if eng is P.act:
                P.op(eng, lambda e: e.copy(out=dst, in_=src), reads=[bk.b[bi]], writes=[hT_bufs[t]])
            else:
                P.op(eng, lambda e: e.tensor_copy(out=dst, in_=src), reads=[bk.b[bi]], writes=[hT_bufs[t]])


def phase_out(P, es, bk, gT, gT_bufs, wo, wo_b, gpost_t, gpost_b, x_dram, xrow0, out_dram, ntiles):
    nc = P.nc
    NS = 2
    xs = [es.enter_context(nc.sbuf_tensor(f"p3_xs{i}", [128, D], F32)) for i in range(NS)]
    xs_b = P.bufs("p3_xs", NS)
    ys = [es.enter_context(nc.sbuf_tensor(f"p3_ys{i}", [128, D], F32)) for i in range(NS)]
    ys_b = P.bufs("p3_ys", NS)
    junk = es.enter_context(nc.sbuf_tensor("p3_junk", [128, D], BF16))
    junk_b = P.buf("p3_junk")
    st = es.enter_context(nc.sbuf_tensor("p3_st", [128, 2 * NS], F32))
    ss_b = P.bufs("p3_ss", NS)
    rs_b = P.bufs("p3_rs", NS)
    for t in range(ntiles):
        s = t % NS
        P.dma(P.sp, xs[s][:], x_dram[xrow0 + t * 128: xrow0 + (t + 1) * 128, :], reads=[], writes=[xs_b[s]], sem_buf=xs_b[s])
        b0 = 4 * (t % 2)
        for nb in range(4):
            pairs = [(gT[:, kc, t * 128:(t + 1) * 128], wo[:, kc, nb * 512:(nb + 1) * 512]) for kc in range(16)]
            mm_group(P, bk.f32(b0 + nb), [bk.b[b0 + nb]], pairs, reads=list(gT_bufs) + [wo_b])
        ybanks = [bk.b[b0 + i] for i in range(4)]
        yps = bk.f32(b0, 4)
        ss_ap = st[:, 2 * s:2 * s + 1]
        rs_ap = st[:, 2 * s + 1:2 * s + 2]
        P.op(P.act, lambda e: e.activation(out=junk[:], in_=yps, func=AF.Square, accum_out=ss_ap),
             reads=ybanks, writes=[junk_b, ss_b[s]])
        rstd_from_ss(P, rs_ap, ss_ap, D, [rs_b[s]], [ss_b[s]])
        P.op(P.dve, lambda e: e.scalar_tensor_tensor(out=ys[s][:], in0=yps, scalar=rs_ap, in1=gpost_t[:],
                                                     op0=ALU.mult, op1=ALU.mult),
             reads=ybanks + [rs_b[s], gpost_b], writes=[ys_b[s]])
        P.op(P.pool, lambda e: e.tensor_tensor(out=ys[s][:], in0=ys[s][:], in1=xs[s][:], op=ALU.add),
             reads=[ys_b[s], xs_b[s]], writes=[ys_b[s]])
        P.dma(P.sp, out_dram[t * 128:(t + 1) * 128, :], ys[s][:], reads=[ys_b[s]], writes=[], sem_buf=ys_b[s])


def load_consts(P, es, ident_d):
    nc = P.nc
    ident = es.enter_context(nc.sbuf_tensor("ident_sb", [128, 128], BF16))
    ident_b = P.buf("ident")
    P.dma(P.pool, ident[:], ident_d[:, :], writes=[ident_b], sem_buf=ident_b)
    ones = es.enter_context(nc.sbuf_tensor("ones_sb", [128, 128], BF16))
    ones_b = P.buf("ones")
    P.op(P.dve, lambda e: e.memset(ones[:], 1.0), writes=[ones_b])
    return ident, ident_b, ones, ones_b


def na_chunks(m):
    if m < 2:
        return list(range(m, m + 6))
    if m >= 14:
        return list(range(m - 1, m + 5))
    return list(range(m, m + 5))


def build_na(nheads=NH):
    nc = bass.Bass("TRN2", target_bir_lowering=False)
    xh = nc.dram_tensor("xh", [TH, D], F32, kind="ExternalInput").ap()
    w_in = nc.dram_tensor("w_in", [NH, 128, 16, 512], F32, kind="ExternalInput").ap()
    w_out = nc.dram_tensor("w_out", [128, 16, D], F32, kind="ExternalInput").ap()
    gpre_d = nc.dram_tensor("gpre", [128, D], F32, kind="ExternalInput").ap()
    gpost_d = nc.dram_tensor("gpost", [128, D], F32, kind="ExternalInput").ap()
    bint_d = nc.dram_tensor("bint", [NH, 128, 5 * 128], F32, kind="ExternalInput").ap()
    bedge_d = nc.dram_tensor("bedge", [NH, 128, 4, 6 * 128], F32, kind="ExternalInput").ap()
    ident_d = nc.dram_tensor("ident", [128, 128], F32, kind="ExternalInput").ap()
    xout = nc.dram_tensor("xout", [TOK, D], F32, kind="ExternalOutput").ap()
    gscr = nc.dram_tensor("gscr", [D, TOK], BF16, kind="Internal").ap()
    gscr_b = [Buf(f"gscr{h}") for h in range(NH)]
    scale = float(128 ** -0.5)

    P = Prog(nc)
    with ExitStack() as es0:
        bk = Banks(P, es0)
        ident, ident_b, ones, ones_b = load_consts(P, es0, ident_d)
        with ExitStack() as es1:
            hT = es1.enter_context(nc.sbuf_tensor("hT", [128, 16, TH], BF16))
            hT_bufs = P.bufs("hT", TH // 128)
            with ExitStack() as es:
                gpre_t = es.enter_context(nc.sbuf_tensor("gpre_t", [128, D], F32))
                gpre_b = P.buf("gpre")
                P.dma(P.sp, gpre_t[:], gpre_d[:, :], writes=[gpre_b], sem_buf=gpre_b)
                phase_hT(P, es, bk, xh, 0, TH // 128, gpre_t, gpre_b, ident, ident_b, hT, hT_bufs)
            P.barrier()
            with ExitStack() as es:
                NW = 2
                wb = [es.enter_context(nc.sbuf_tensor(f"wb{i}", [128, 16, 512], BF16)) for i in range(NW)]
                wb_b = P.bufs("wb", NW)
                NHB = 2
                qT = [es.enter_context(nc.sbuf_tensor(f"qT{i}", [128, TOK], BF16)) for i in range(NHB)]
                kT = [es.enter_context(nc.sbuf_tensor(f"kT{i}", [128, TH], BF16)) for i in range(NHB)]
                gz = [es.enter_context(nc.sbuf_tensor(f"gz{i}", [128, TOK], BF16)) for i in range(NHB)]
                V = [es.enter_context(nc.sbuf_tensor(f"V{i}", [128, TH // 128, 128], BF16)) for i in range(NHB)]
                qT_b = P.bufs("qT", NHB); kT_b = P.bufs("kT", NHB); gz_b = P.bufs("gz", NHB); V_b = P.bufs("V", NHB)
                vT = es.enter_context(nc.sbuf_tensor("vT", [128, TH], BF16))
                vT_b = P.bufs("vT", TH // 512)
                gTh = [es.enter_context(nc.sbuf_tensor(f"gTh{i}", [128, TOK], BF16)) for i in range(2)]
                gTh_b = P.bufs("gTh", 2)
                bint = [es.enter_context(nc.sbuf_tensor(f"bint{i}", [128, 5 * 128], F32)) for i in range(2)]
                bint_b = P.bufs("bint", 2)
                bedge = es.enter_context(nc.sbuf_tensor("bedge_sb", [128, 4, 6 * 128], F32))
                bedge_b = P.buf("bedge")
                NSB = 2
                Sb = [es.enter_context(nc.sbuf_tensor(f"Sb{i}", [128, 6 * 128], F32)) for i in range(NSB)]
                Sb_b = P.bufs("Sb", NSB)
                PT = [es.enter_context(nc.sbuf_tensor(f"PT{i}", [128, 6 * 128], BF16)) for i in range(NSB)]
                PT_b = P.bufs("PT", NSB)
                rsb = [es.enter_context(nc.sbuf_tensor(f"rsb{i}", [128, 128], F32)) for i in range(2)]
                rsb_b = P.bufs("rsb", 2)
                o1 = [es.enter_context(nc.sbuf_tensor(f"o1{i}", [128, 128], F32)) for i in range(2)]
                o1_b = P.bufs("o1", 2)

                def load_w(h):
                    s = h % NW
                    for g in range(4):
                        P.dma(P.pool, wb[s][:, 4 * g:4 * g + 4, :], w_in[h, :, 4 * g:4 * g + 4, :], writes=[wb_b[s]], sem_buf=wb_b[s])

                def load_bias(h):
                    P.dma(P.sp, bint[h % 2][:], bint_d[h], writes=[bint_b[h % 2]], sem_buf=bint_b[h % 2])
                    P.dma(P.sp, bedge[:], bedge_d[h], writes=[bedge_b], sem_buf=bedge_b)

                load_w(0)
                for h in range(nheads):
                    s = h % NW
                    hs = h % NHB
                    if h + 1 < nheads:
                        load_w(h + 1)
                    load_bias(h)
                    w = wb[s]
                    allh = list(hT_bufs)
                    own = hT_bufs[2:18]
                    ev = 0
                    for (j, dst, dst_b, ntok, tok0, hb_list, kind) in (
                            (0, qT[hs], qT_b[hs], TOK, HALO, own, "q"),
                            (1, kT[hs], kT_b[hs], TH, 0, allh, "k"),
                            (3, gz[hs], gz_b[hs], TOK, HALO, own, "z")):
                        for blk in range(ntok // 512):
                            bi = bk.next()
                            c0 = tok0 + blk * 512
                            pairs = [(w[:, kc, j * 128:(j + 1) * 128], hT[:, kc, c0:c0 + 512]) for kc in range(16)]
                            mm_group(P, bk.f32(bi), [bk.b[bi]], pairs, reads=hT_bufs[c0 // 128:c0 // 128 + 4] + [wb_b[s]])
                            dap = dst[:, blk * 512:(blk + 1) * 512]
                            if kind == "z":
                                P.op(P.act, lambda e: e.activation(out=dap, in_=bk.f32(bi), func=AF.Silu),
                                     reads=[bk.b[bi]], writes=[dst_b])
                            elif ev % 2 == 0:
                                P.op(P.act, lambda e: e.copy(out=dap, in_=bk.f32(bi)), reads=[bk.b[bi]], writes=[dst_b])
                            else:
                                P.op(P.dve, lambda e: e.tensor_copy(out=dap, in_=bk.f32(bi)), reads=[bk.b[bi]], writes=[dst_b])
                            ev += 1
                    for blk in range(TH // 512):
                        bi = bk.next()
                        c0 = blk * 512
                        pairs = [(w[:, kc, 256:384], hT[:, kc, c0:c0 + 512]) for kc in range(16)]
                        mm_group(P, bk.f32(bi), [bk.b[bi]], pairs, reads=hT_bufs[c0 // 128:c0 // 128 + 4] + [wb_b[s]])
                        dap = vT[:, c0:c0 + 512]
                        if blk % 2 == 0:
                            P.op(P.dve, lambda e: e.tensor_copy(out=dap, in_=bk.f32(bi)), reads=[bk.b[bi]], writes=[vT_b[blk]])
                        else:
                            P.op(P.act, lambda e: e.copy(out=dap, in_=bk.f32(bi)), reads=[bk.b[bi]], writes=[vT_b[blk]])
                    for tg in range(TH // 512):
                        bi = bk.next()
                        pst = bk.bf16(bi)
                        for i in range(4):
                            t = tg * 4 + i
                            P.op(P.pe, lambda e, i=i, t=t: e.transpose(pst[:, i * 128:(i + 1) * 128], vT[:, t * 128:(t + 1) * 128], ident[:]),
                                 reads=[vT_b[tg], ident_b], writes=[bk.b[bi]], signal=(i == 3))
                        dap = V[hs][:, tg * 4:(tg + 1) * 4, :]
                        sap = pst[:, 0:512].rearrange("p (a b) -> p a b", a=4)
                        if tg % 2 == 0:
                            P.op(P.act, lambda e: e.copy(out=dap, in_=sap), reads=[bk.b[bi]], writes=[V_b[hs]])
                        else:
                            P.op(P.dve, lambda e: e.tensor_copy(out=dap, in_=sap), reads=[bk.b[bi]], writes=[V_b[hs]])

                    SB = [(0, 1), (2, 3)]
                    OB = [(4, 5), (6, 7)]

                    def issue_S(m):
                        cl = na_chunks(m)
                        b0, b1 = SB[m % 2]
                        for ci, c in enumerate(cl):
                            bi = b0 if ci < 4 else b1
                            col = (ci % 4) * 128
                            P.op(P.pe, lambda e, c=c, bi=bi, col=col: e.matmul(
                                bk.f32(bi)[:, col:col + 128], kT[hs][:, c * 128:(c + 1) * 128],
                                qT[hs][:, m * 128:(m + 1) * 128], start=True, stop=True),
                                reads=[kT_b[hs], qT_b[hs]], writes=[bk.b[b0], bk.b[b1]], signal=(ci == len(cl) - 1))

                    def softmax(m):
                        cl = na_chunks(m)
                        n = len(cl) * 128
                        b0, b1 = SB[m % 2]
                        sps = bk.f32(b0, 2)[:, 0:n]
                        if m < 2:
                            bt, btb = bedge[:, m, 0:n], bedge_b
                        elif m >= 14:
                            bt, btb = bedge[:, m - 12, 0:n], bedge_b
                        else:
                            bt, btb = bint[h % 2][:, 0:n], bint_b[h % 2]
                        sl = m % NSB
                        P.op(P.dve, lambda e: e.scalar_tensor_tensor(out=Sb[sl][:, 0:n], in0=sps, scalar=scale, in1=bt,
                                                                     op0=ALU.mult, op1=ALU.add),
                             reads=[bk.b[b0], bk.b[b1], btb], writes=[Sb_b[sl]])
                        P.op(P.act, lambda e: e.activation(out=PT[sl][:, 0:n], in_=Sb[sl][:, 0:n], func=AF.Exp),
                             reads=[Sb_b[sl]], writes=[PT_b[sl]])

                    def issue_O(m):
                        cl = na_chunks(m)
                        sl = m % NSB
                        ob, mb = OB[m % 2]
                        pairs = [(V[hs][:, c, :], PT[sl][:, ci * 128:(ci + 1) * 128]) for ci, c in enumerate(cl)]
                        mm_group(P, bk.f32(ob)[:, 0:128], [bk.b[ob]], pairs, reads=[V_b[hs], PT_b[sl]])
                        pairs = [(ones[:], PT[sl][:, ci * 128:(ci + 1) * 128]) for ci, c in enumerate(cl)]
                        mm_group(P, bk.f32(mb)[:, 0:128], [bk.b[mb]], pairs, reads=[ones_b, PT_b[sl]])

                    def fin(m):
                        ob, mb = OB[m % 2]
                        i2 = m % 2
                        P.op(P.dve, lambda e: e.reciprocal(out=rsb[i2][:], in_=bk.f32(mb)[:, 0:128]),
                             reads=[bk.b[mb]], writes=[rsb_b[i2]])
                        P.op(P.dve, lambda e: e.tensor_tensor(out=o1[i2][:], in0=bk.f32(ob)[:, 0:128], in1=rsb[i2][:], op=ALU.mult),
                             reads=[bk.b[ob], rsb_b[i2]], writes=[o1_b[i2]])
                        P.op(P.pool, lambda e: e.tensor_tensor(out=gTh[h % 2][:, m * 128:(m + 1) * 128], in0=o1[i2][:],
                                                               in1=gz[hs][:, m * 128:(m + 1) * 128], op=ALU.mult),
                             reads=[o1_b[i2], gz_b[hs]], writes=[gTh_b[h % 2]])

                    issue_S(0)
                    softmax(0)
                    issue_S(1)
                    for m in range(16):
                        if m + 1 < 16:
                            softmax(m + 1)
                        issue_O(m)
                        if m + 2 < 16:
                            issue_S(m + 2)
                        fin(m)
                    P.dma(P.sp, gscr[h * 128:(h + 1) * 128, :], gTh[h % 2][:], reads=[gTh_b[h % 2]], writes=[gscr_b[h]], sem_buf=gTh_b[h % 2])
            P.barrier()
        with ExitStack() as es:
            gT = es.enter_context(nc.sbuf_tensor("gT", [128, 16, TOK], BF16))
            gT_bufs = P.bufs("gT", 4)
            wo = es.enter_context(nc.sbuf_tensor("wo", [128, 16, D], BF16))
            wo_b = P.buf("wo")
            gpost_t = es.enter_context(nc.sbuf_tensor("gpost_t", [128, D], F32))
            gpost_b = P.buf("gpost")
            P.dma(P.sp, gpost_t[:], gpost_d[:, :], writes=[gpost_b], sem_buf=gpost_b)
            for kc in range(16):
                P.dma(P.sp, gT[:, kc, :], gscr[kc * 128:(kc + 1) * 128, :], reads=[gscr_b[kc]], writes=[gT_bufs[kc // 4]], sem_buf=gT_bufs[kc // 4])
            for g in range(8):
                P.dma(P.pool, wo[:, 2 * g:2 * g + 2, :], w_out[:, 2 * g:2 * g + 2, :], writes=[wo_b], sem_buf=wo_b)
            phase_out(P, es, bk, gT, gT_bufs, wo, wo_b, gpost_t, gpost_b, xh, HALO, xout, TOK // 128)
            P.finish()
    return nc


def na_bias_tiles(rpb):
    H = rpb.shape[0]
    rows = 128
    out = []
    kc = np.arange(64)
    qc = np.arange(64)
    col_start = np.clip(qc - 8, 0, 48)
    col_ok = (kc[:, None] >= col_start[None, :]) & (kc[:, None] < col_start[None, :] + 16)
    dc = np.clip(kc[:, None] - qc[None, :] + 15, 0, 30)

    def tile_for(Mg, chunks_g):
        t = np.full((H, 128, len(chunks_g), 128), NEG, np.float32)
        for ci, cg in enumerate(chunks_g):
            for kr2 in range(2):
                kr = 2 * cg + kr2
                if kr < 0 or kr >= rows:
                    continue
                for qr2 in range(2):
                    r = 2 * Mg + qr2
                    rs = min(max(r - 4, 0), rows - 8)
                    if not (rs <= kr < rs + 8):
                        continue
                    dr = kr - r + 7
                    vals = rpb[:, dr, :][:, dc]
                    vals = np.where(col_ok[None], vals, np.float32(NEG))
                    t[:, kr2 * 64:(kr2 + 1) * 64, ci, qr2 * 64:(qr2 + 1) * 64] = vals
        return t

    for q in range(4):
        Mi = 16 * q + 8
        bint = tile_for(Mi, [Mi - 2 + i for i in range(5)]).reshape(H, 128, 640)
        be = np.empty((H, 128, 4, 768), np.float32)
        for idx, m in enumerate((0, 1, 14, 15)):
            Mg = 16 * q + m
            cl = na_chunks(m)
            be[:, :, idx, :] = tile_for(Mg, [16 * q + (c - 2) for c in cl]).reshape(H, 128, 768)
        out.append((np.ascontiguousarray(bint), np.ascontiguousarray(be)))
    return out


def run_na_layer(x, g_pre, g_post, w_in, rpb, w_out, nheads=NH):
    B, L, _ = x.shape
    nc = build_na(nheads)
    w_in_r = np.ascontiguousarray(w_in.reshape(16, 128, 4, NH, 128).transpose(3, 1, 0, 2, 4).reshape(NH, 128, 16, 512))
    w_out_r = np.ascontiguousarray(w_out.reshape(16, 128, D).transpose(1, 0, 2))
    gpre = np.ascontiguousarray(np.broadcast_to(g_pre[None, :], (128, D)))
    gpost = np.ascontiguousarray(np.broadcast_to(g_post[None, :], (128, D)))
    ident = np.eye(128, dtype=np.float32)
    tiles = na_bias_tiles(rpb)
    in_maps = []
    for c in range(NCORES):
        b, q = divmod(c, 4)
        xh = np.zeros((TH, D), np.float32)
        lo = q * TOK - HALO
        hi = lo + TH
        slo, shi = max(lo, 0), min(hi, L)
        xh[slo - lo: shi - lo] = x[b, slo:shi]
        in_maps.append({"xh": xh, "w_in": w_in_r, "w_out": w_out_r, "gpre": gpre, "gpost": gpost,
                        "bint": tiles[q][0], "bedge": tiles[q][1], "ident": ident})
    res = run_bass_kernel_spmd(nc, in_maps, core_ids=list(range(NCORES)))
    out = np.empty_like(x)
    for c in range(NCORES):
        b, q = divmod(c, 4)
        out[b, q * TOK:(q + 1) * TOK] = res.results[c]["xout"]
    return out


def rope_combine(P, bk, bA, bB, cosT, sinT, c_b, dst_ap, dst_b, t1, t1_b, t2, t2_b, c0, n=512):
    P.op(P.dve, lambda e: e.tensor_tensor(out=t1[:, 0:n], in0=bk.f32(bA)[0:64, 0:n], in1=cosT[:, c0:c0 + n], op=ALU.mult),
         reads=[bk.b[bA], c_b], writes=[t1_b])
    P.op(P.dve, lambda e: e.tensor_tensor(out=t2[:, 0:n], in0=bk.f32(bB)[0:64, 0:n], in1=sinT[:, c0:c0 + n], op=ALU.mult),
         reads=[bk.b[bB], c_b], writes=[t2_b])
    P.op(P.pool, lambda e: e.tensor_tensor(out=dst_ap, in0=t1[:, 0:n], in1=t2[:, 0:n], op=ALU.add),
         reads=[t1_b, t2_b], writes=[dst_b])


def build_mla_a():
    nc = bass.Bass("TRN2", target_bir_lowering=False)
    x_d = nc.dram_tensor("x", [TOK, D], F32, kind="ExternalInput").ap()
    gpre_d = nc.dram_tensor("gpre", [128, D], F32, kind="ExternalInput").ap()
    wlat_d = nc.dram_tensor("wlat", [128, 16, 1088], F32, kind="ExternalInput").ap()
    wz_d = nc.dram_tensor("wz", [NH, 128, 16, 128], F32, kind="ExternalInput").ap()
    wq_d = nc.dram_tensor("wq", [128, 4, 3072], F32, kind="ExternalInput").ap()
    wkv_d = nc.dram_tensor("wkv", [128, 4, 4096], F32, kind="ExternalInput").ap()
    gq_d = nc.dram_tensor("gqkv", [128, 1024], F32, kind="ExternalInput").ap()
    cos_d = nc.dram_tensor("cosT", [64, TOK], F32, kind="ExternalInput").ap()
    sin_d = nc.dram_tensor("sinT", [64, TOK], F32, kind="ExternalInput").ap()
    ident_d = nc.dram_tensor("ident", [128, 128], F32, kind="ExternalInput").ap()
    QN = nc.dram_tensor("QN", [NH, 128, TOK], BF16, kind="ExternalOutput").ap()
    QR = nc.dram_tensor("QR", [NH, 64, TOK], BF16, kind="ExternalOutput").ap()
    KN = nc.dram_tensor("KN", [NH, 128, TOK], BF16, kind="ExternalOutput").ap()
    KR = nc.dram_tensor("KR", [64, TOK], BF16, kind="ExternalOutput").ap()
    VR = nc.dram_tensor("VR", [NH, 128, 16 * 128], BF16, kind="ExternalOutput").ap()
    GZ = nc.dram_tensor("GZ", [NH, 128, TOK], BF16, kind="ExternalOutput").ap()

    P = Prog(nc)
    with ExitStack() as es0:
        bk = Banks(P, es0)
        ident, ident_b, ones, ones_b = load_consts(P, es0, ident_d)
        cnT = es0.enter_context(nc.sbuf_tensor("cnT", [128, 8, TOK], BF16))
        cnT_b = P.bufs("cnT", 16)
        cosT = es0.enter_context(nc.sbuf_tensor("cos_sb", [64, TOK], F32))
        sinT = es0.enter_context(nc.sbuf_tensor("sin_sb", [64, TOK], F32))
        cs_b = P.buf("cossin")
        P.dma(P.sp, cosT[:], cos_d[:, :], writes=[cs_b], sem_buf=cs_b)
        P.dma(P.sp, sinT[:], sin_d[:, :], writes=[cs_b], sem_buf=cs_b)
        t1 = es0.enter_context(nc.sbuf_tensor("rope_t1", [64, 512], F32))
        t2 = es0.enter_context(nc.sbuf_tensor("rope_t2", [64, 512], F32))
        t1_b = P.buf("t1"); t2_b = P.buf("t2")
        with ExitStack() as es1:
            hT = es1.enter_context(nc.sbuf_tensor("hT", [128, 16, TOK], BF16))
            hT_bufs = P.bufs("hT", 16)
            with ExitStack() as es:
                gpre_t = es.enter_context(nc.sbuf_tensor("gpre_t", [128, D], F32))
                gpre_b = P.buf("gpre")
                P.dma(P.sp, gpre_t[:], gpre_d[:, :], writes=[gpre_b], sem_buf=gpre_b)
                phase_hT(P, es, bk, x_d, 0, 16, gpre_t, gpre_b, ident, ident_b, hT, hT_bufs)
            P.barrier()
            with ExitStack() as es:
                wlat = es.enter_context(nc.sbuf_tensor("wlat_sb", [128, 16, 1024], BF16))
                wlat_b = P.buf("wlat")
                wkr = es.enter_context(nc.sbuf_tensor("wkr_sb", [128, 16, 64], BF16))
                wkr_b = P.buf("wkr")
                wkrr = es.enter_context(nc.sbuf_tensor("wkrr_sb", [128, 16, 64], BF16))
                wkrr_b = P.buf("wkrr")
                gq_t = es.enter_context(nc.sbuf_tensor("gq_t", [128, 1024], F32))
                gq_b = P.buf("gq")
                P.dma(P.sp, gq_t[:], gq_d[:, :], writes=[gq_b], sem_buf=gq_b)
                for g in range(4):
                    for half in range(2):
                        P.dma(P.pool, wlat[:, 4 * g:4 * g + 4, half * 512:(half + 1) * 512],
                              wlat_d[:, 4 * g:4 * g + 4, half * 512:(half + 1) * 512], writes=[wlat_b], sem_buf=wlat_b)
                P.dma(P.pool, wkr[:], wlat_d[:, :, 1024:1088], writes=[wkr_b], sem_buf=wkr_b)
                wv = wkr[:].rearrange("p k (i two) -> p k i two", two=2)
                rv = wkrr[:].rearrange("p k (i two) -> p k i two", two=2)
                P.op(P.act, lambda e: e.mul(out=rv[:, :, :, 0], in_=wv[:, :, :, 1], mul=-1.0), reads=[wkr_b], writes=[wkrr_b])
                P.op(P.act, lambda e: e.copy(out=rv[:, :, :, 1], in_=wv[:, :, :, 0]), reads=[wkr_b], writes=[wkrr_b])
                junk = es.enter_context(nc.sbuf_tensor("a_junk", [128, 512], BF16))
                junk_b = P.buf("a_junk")
                st = es.enter_context(nc.sbuf_tensor("a_st", [128, 8], F32))
                ss_b = P.bufs("a_ss", 2); rs_b = P.bufs("a_rs", 2)
                cn = [es.enter_context(nc.sbuf_tensor(f"a_cn{i}", [128, 1024], BF16)) for i in range(2)]
                cn_b = P.bufs("a_cn", 2)
                for t in range(16):
                    s = t % 2
                    bq = bk.next(); bkv = bk.next()
                    for (bi, c0) in ((bq, 0), (bkv, 512)):
                        pairs = [(hT[:, kc, t * 128:(t + 1) * 128], wlat[:, kc, c0:c0 + 512]) for kc in range(16)]
                        mm_group(P, bk.f32(bi), [bk.b[bi]], pairs, reads=[hT_bufs[t], wlat_b])
                    ss_ap = st[:, 4 * s:4 * s + 2]
                    rs_ap = st[:, 4 * s + 2:4 * s + 4]
                    P.op(P.act, lambda e: e.activation(out=junk[:], in_=bk.f32(bq), func=AF.Square, accum_out=ss_ap[:, 0:1]),
                         reads=[bk.b[bq]], writes=[junk_b, ss_b[s]])
                    P.op(P.act, lambda e: e.activation(out=junk[:], in_=bk.f32(bkv), func=AF.Square, accum_out=ss_ap[:, 1:2]),
                         reads=[bk.b[bkv]], writes=[junk_b, ss_b[s]])
                    rstd_from_ss(P, rs_ap, ss_ap, 512, [rs_b[s]], [ss_b[s]])
                    P.op(P.dve, lambda e: e.scalar_tensor_tensor(out=cn[s][:, 0:512], in0=bk.f32(bq), scalar=rs_ap[:, 0:1],
                                                                 in1=gq_t[:, 0:512], op0=ALU.mult, op1=ALU.mult),
                         reads=[bk.b[bq], rs_b[s], gq_b], writes=[cn_b[s]])
                    P.op(P.dve, lambda e: e.scalar_tensor_tensor(out=cn[s][:, 512:1024], in0=bk.f32(bkv), scalar=rs_ap[:, 1:2],
                                                                 in1=gq_t[:, 512:1024], op0=ALU.mult, op1=ALU.mult),
                         reads=[bk.b[bkv], rs_b[s], gq_b], writes=[cn_b[s]])
                    bt = bk.next()
                    pst = bk.bf16(bt)
                    for j in range(8):
                        P.op(P.pe, lambda e, j=j: e.transpose(pst[:, j * 128:(j + 1) * 128], cn[s][:, j * 128:(j + 1) * 128], ident[:]),
                             reads=[cn_b[s], ident_b], writes=[bk.b[bt]], signal=(j == 7))
                    P.op(P.act, lambda e: e.copy(out=cnT[:, :, t * 128:(t + 1) * 128], in_=pst.rearrange("p (k n) -> p k n", k=8)),
                         reads=[bk.b[bt]], writes=[cnT_b[t]])
                krT = es.enter_context(nc.sbuf_tensor("krT", [64, TOK], BF16))
                krT_b = P.buf("krT")
                for blk in range(4):
                    bA = bk.next(); bB = bk.next()
                    for (bi, wsrc, wsb) in ((bA, wkr, wkr_b), (bB, wkrr, wkrr_b)):
                        pairs = [(wsrc[:, kc, :], hT[:, kc, blk * 512:(blk + 1) * 512]) for kc in range(16)]
                        mm_group(P, bk.f32(bi)[0:64, :], [bk.b[bi]], pairs, reads=hT_bufs[blk * 4:blk * 4 + 4] + [wsb])
                    rope_combine(P, bk, bA, bB, cosT, sinT, cs_b, krT[:, blk * 512:(blk + 1) * 512], krT_b, t1, t1_b, t2, t2_b, blk * 512)
                P.dma(P.sp, KR[:, :], krT[:], reads=[krT_b], sem_buf=krT_b)
                wz = [es.enter_context(nc.sbuf_tensor(f"wz{i}", [128, 16, 128], BF16)) for i in range(2)]
                wz_b = P.bufs("wz", 2)
                gz = [es.enter_context(nc.sbuf_tensor(f"gz{i}", [128, TOK], BF16)) for i in range(2)]
                gz_b = P.bufs("gz", 2)
                P.dma(P.pool, wz[0][:], wz_d[0], writes=[wz_b[0]], sem_buf=wz_b[0])
                for h in range(NH):
                    s = h % 2
                    if h + 1 < NH:
                        P.dma(P.pool, wz[(h + 1) % 2][:], wz_d[h + 1], writes=[wz_b[(h + 1) % 2]], sem_buf=wz_b[(h + 1) % 2])
                    for blk in range(4):
                        bi = bk.next()
                        pairs = [(wz[s][:, kc, :], hT[:, kc, blk * 512:(blk + 1) * 512]) for kc in range(16)]
                        mm_group(P, bk.f32(bi), [bk.b[bi]], pairs, reads=hT_bufs[blk * 4:blk * 4 + 4] + [wz_b[s]])
                        P.op(P.act, lambda e: e.activation(out=gz[s][:, blk * 512:(blk + 1) * 512], in_=bk.f32(bi), func=AF.Silu),
                             reads=[bk.b[bi]], writes=[gz_b[s]])
                    P.dma(P.sp, GZ[h], gz[s][:], reads=[gz_b[s]], sem_buf=gz_b[s])
            P.barrier()
        with ExitStack() as es:
            wq = es.enter_context(nc.sbuf_tensor("wq_sb", [128, 4, 3072], BF16))
            wq_b = P.buf("wq")
            wqr = es.enter_context(nc.sbuf_tensor("wqr_sb", [128, 4, NH, 64], BF16))
            wqr_b = P.buf("wqr")
            wkv = es.enter_context(nc.sbuf_tensor("wkv_sb", [128, 4, 4096], BF16))
            wkv_b = P.buf("wkv")
            for kc in range(4):
                for j in range(6):
                    P.dma(P.pool, wq[:, kc, j * 512:(j + 1) * 512], wq_d[:, kc, j * 512:(j + 1) * 512], writes=[wq_b], sem_buf=wq_b)
            for kc in range(4):
                for j in range(8):
                    P.dma(P.pool, wkv[:, kc, j * 512:(j + 1) * 512], wkv_d[:, kc, j * 512:(j + 1) * 512], writes=[wkv_b], sem_buf=wkv_b)
            for kc in range(4):
                src = wq[:, kc, :].rearrange("p (h c) -> p h c", h=NH)[:, :, 128:192].rearrange("p h (i two) -> p h i two", two=2)
                dst = wqr[:, kc, :, :].rearrange("p h (i two) -> p h i two", two=2)
                P.op(P.act, lambda e: e.mul(out=dst[:, :, :, 0], in_=src[:, :, :, 1], mul=-1.0), reads=[wq_b], writes=[wqr_b])
                P.op(P.act, lambda e: e.copy(out=dst[:, :, :, 1], in_=src[:, :, :, 0]), reads=[wq_b], writes=[wqr_b])
            Vown = es.enter_context(nc.sbuf_tensor("Vown", [128, NH, 16, 128], BF16))
            Vown_b = P.buf("Vown")
            qn = [es.enter_context(nc.sbuf_tensor(f"qn{i}", [128, TOK], BF16)) for i in range(2)]
            qn_b = P.bufs("qn", 2)
            kn = [es.enter_context(nc.sbuf_tensor(f"kn{i}", [128, TOK], BF16)) for i in range(2)]
            kn_b = P.bufs("kn", 2)
            qr = [es.enter_context(nc.sbuf_tensor(f"qr{i}", [64, TOK], BF16)) for i in range(2)]
            qr_b = P.bufs("qr", 2)
            allcn = list(cnT_b)
            for t in range(16):
                for hg in range(4):
                    bi = bk.next()
                    pairs = [(cnT[:, 4 + kc, t * 128:(t + 1) * 128], wkv[:, kc, 2048 + hg * 512:2048 + (hg + 1) * 512]) for kc in range(4)]
                    mm_group(P, bk.f32(bi), [bk.b[bi]], pairs, reads=[cnT_b[t], wkv_b])
                    dst = Vown[:, hg * 4:(hg + 1) * 4, t, :]
                    src = bk.f32(bi).rearrange("p (a d) -> p a d", a=4)
                    if (t * 4 + hg) % 2 == 0:
                        P.op(P.dve, lambda e: e.tensor_copy(out=dst, in_=src), reads=[bk.b[bi]], writes=[Vown_b])
                    else:
                        P.op(P.act, lambda e: e.copy(out=dst, in_=src), reads=[bk.b[bi]], writes=[Vown_b])
            for h in range(NH):
                P.dma(P.sp, VR[h], Vown[:, h, :, :].rearrange("p t d -> p (t d)"), reads=[Vown_b], sem_buf=Vown_b)
            for h in range(NH):
                s = h % 2
                for blk in range(4):
                    tb = cnT_b[blk * 4:blk * 4 + 4]
                    bi = bk.next()
                    pairs = [(wq[:, kc, h * 192:h * 192 + 128], cnT[:, kc, blk * 512:(blk + 1) * 512]) for kc in range(4)]
                    mm_group(P, bk.f32(bi), [bk.b[bi]], pairs, reads=tb + [wq_b])
                    P.op(P.act, lambda e: e.copy(out=qn[s][:, blk * 512:(blk + 1) * 512], in_=bk.f32(bi)), reads=[bk.b[bi]], writes=[qn_b[s]])
                    bi = bk.next()
                    pairs = [(wkv[:, kc, h * 128:(h + 1) * 128], cnT[:, 4 + kc, blk * 512:(blk + 1) * 512]) for kc in range(4)]
                    mm_group(P, bk.f32(bi), [bk.b[bi]], pairs, reads=tb + [wkv_b])
                    P.op(P.dve, lambda e: e.tensor_copy(out=kn[s][:, blk * 512:(blk + 1) * 512], in_=bk.f32(bi)), reads=[bk.b[bi]], writes=[kn_b[s]])
                    bA = bk.next(); bB = bk.next()
                    pairs = [(wq[:, kc, h * 192 + 128:h * 192 + 192], cnT[:, kc, blk * 512:(blk + 1) * 512]) for kc in range(4)]
                    mm_group(P, bk.f32(bA)[0:64, :], [bk.b[bA]], pairs, reads=tb + [wq_b])
                    pairs = [(wqr[:, kc, h, :], cnT[:, kc, blk * 512:(blk + 1) * 512]) for kc in range(4)]
                    mm_group(P, bk.f32(bB)[0:64, :], [bk.b[bB]], pairs, reads=tb + [wqr_b])
                    rope_combine(P, bk, bA, bB, cosT, sinT, cs_b, qr[s][:, blk * 512:(blk + 1) * 512], qr_b[s], t1, t1_b, t2, t2_b, blk * 512)
                P.dma(P.sp, QN[h], qn[s][:], reads=[qn_b[s]], sem_buf=qn_b[s])
                P.dma(P.sp, KN[h], kn[s][:], reads=[kn_b[s]], sem_buf=kn_b[s])
                P.dma(P.sp, QR[h], qr[s][:], reads=[qr_b[s]], sem_buf=qr_b[s])
            P.finish()
    return nc


LK = 8192


def build_mla_b(nheads=NH):
    nc = bass.Bass("TRN2", target_bir_lowering=False)
    x_d = nc.dram_tensor("x", [TOK, D], F32, kind="ExternalInput").ap()
    QN = nc.dram_tensor("QN", [NH, 128, TOK], BF16, kind="ExternalInput").ap()
    QR = nc.dram_tensor("QR", [NH, 64, TOK], BF16, kind="ExternalInput").ap()
    GZ = nc.dram_tensor("GZ", [NH, 128, TOK], BF16, kind="ExternalInput").ap()
    KN = nc.dram_tensor("KN", [NH, 128, LK], BF16, kind="ExternalInput").ap()
    KR = nc.dram_tensor("KR", [64, LK], BF16, kind="ExternalInput").ap()
    VR = nc.dram_tensor("VR", [NH, 128, LK], BF16, kind="ExternalInput").ap()
    w_out = nc.dram_tensor("w_out", [128, 16, D], F32, kind="ExternalInput").ap()
    gpost_d = nc.dram_tensor("gpost", [128, D], F32, kind="ExternalInput").ap()
    xout = nc.dram_tensor("xout", [TOK, D], F32, kind="ExternalOutput").ap()
    gscr = nc.dram_tensor("gscr", [D, TOK], BF16, kind="Internal").ap()
    gscr_b = [Buf(f"gscr{h}") for h in range(NH)]
    scale = float(192 ** -0.5)
    NKC = LK // 128

    P = Prog(nc)
    with ExitStack() as es0:
        bk = Banks(P, es0)
        ones = es0.enter_context(nc.sbuf_tensor("ones_sb", [128, 128], BF16))
        ones_b = P.buf("ones")
        P.op(P.dve, lambda e: e.memset(ones[:], 1.0), writes=[ones_b])
        with ExitStack() as es:
            kr = es.enter_context(nc.sbuf_tensor("kr_sb", [128, LK], BF16))
            kr_b = P.buf("kr")
            P.op(P.pool, lambda e: e.memset(kr[64:128, :], 0.0), writes=[kr_b])
            P.dma(P.sp, kr[0:64, :], KR[:, :], writes=[kr_b], sem_buf=kr_b)
            NS = 2
            kn = [es.enter_context(nc.sbuf_tensor(f"kn{i}", [128, LK], BF16)) for i in range(NS)]
            vv = [es.enter_context(nc.sbuf_tensor(f"vv{i}", [128, NKC, 128], BF16)) for i in range(NS)]
            qn = [es.enter_context(nc.sbuf_tensor(f"qn{i}", [128, TOK], BF16)) for i in range(NS)]
            qr = [es.enter_context(nc.sbuf_tensor(f"qr{i}", [128, TOK], BF16)) for i in range(NS)]
            gz = [es.enter_context(nc.sbuf_tensor(f"gz{i}", [128, TOK], BF16)) for i in range(NS)]
            kn_b = P.bufs("kn", NS); vv_b = P.bufs("vv", NS); qn_b = P.bufs("qn", NS); qr_b = P.bufs("qr", NS); gz_b = P.bufs("gz", NS)
            for i in range(NS):
                P.op(P.pool, lambda e: e.memset(qr[i][64:128, :], 0.0), writes=[qr_b[i]])
            gTh = [es.enter_context(nc.sbuf_tensor(f"gTh{i}", [128, TOK], BF16)) for i in range(2)]
            gTh_b = P.bufs("gTh", 2)
            NPT = 6
            PT = [es.enter_context(nc.sbuf_tensor(f"PT{i}", [128, 512], BF16)) for i in range(NPT)]
            PT_b = P.bufs("PT", NPT)
            rsb = [es.enter_context(nc.sbuf_tensor(f"rsb{i}", [128, 512], F32)) for i in range(2)]
            rsb_b = P.bufs("rsb", 2)
            o1 = [es.enter_context(nc.sbuf_tensor(f"o1{i}", [128, 512], F32)) for i in range(2)]
            o1_b = P.bufs("o1", 2)
            NPS = 3
            PS = [es.enter_context(nc.sbuf_tensor(f"PS{i}", [128, 512], BF16)) for i in range(NPS)]
            PS_b = P.bufs("PS", NPS)
            GRP = 8
            grp_i = 0

            def load_head(h):
                s = h % NS
                for g in range(4):
                    P.dma(P.sp, kn[s][:, g * 2048:(g + 1) * 2048], KN[h, :, g * 2048:(g + 1) * 2048], writes=[kn_b[s]], sem_buf=kn_b[s])
                for g in range(4):
                    P.dma(P.pool, vv[s][:, g * 16:(g + 1) * 16, :], VR[h, :, g * 2048:(g + 1) * 2048].rearrange("p (k d) -> p k d", d=128),
                          writes=[vv_b[s]], sem_buf=vv_b[s])
                P.dma(P.sp, qn[s][:], QN[h], writes=[qn_b[s]], sem_buf=qn_b[s])
                P.dma(P.sp, qr[s][0:64, :], QR[h], writes=[qr_b[s]], sem_buf=qr_b[s])
                P.dma(P.sp, gz[s][:], GZ[h], writes=[gz_b[s]], sem_buf=gz_b[s])

            NSB = 4
            unit = 0
            load_head(0)
            for h in range(nheads):
                s = h % NS
                if h + 1 < nheads:
                    load_head(h + 1)
                for qb in range(4):
                    ob = 4 + (qb % 2)
                    mb = 6 + (qb % 2)
                    q0 = qb * 512

                    def issue_S(kc, u):
                        bi = u % NSB
                        P.op(P.pe, lambda e: e.matmul(bk.f32(bi), kn[s][:, kc * 128:(kc + 1) * 128], qn[s][:, q0:q0 + 512], start=True, stop=False),
                             reads=[kn_b[s], qn_b[s]], writes=[bk.b[bi]], signal=False)
                        P.op(P.pe, lambda e: e.matmul(bk.f32(bi), kr[:, kc * 128:(kc + 1) * 128], qr[s][:, q0:q0 + 512], start=False, stop=True),
                             reads=[kr_b, qr_b[s]], writes=[bk.b[bi]], signal=True)

                    def do_exp(kc, u):
                        bi = u % NSB
                        pt = u % NPT
                        P.op(P.act, lambda e: e.activation(out=PT[pt][:], in_=bk.f32(bi), func=AF.Exp, scale=scale),
                             reads=[bk.b[bi]], writes=[PT_b[pt]])

                    def issue_PV(kc, u):
                        pt = u % NPT
                        P.op(P.pe, lambda e: e.matmul(bk.f32(ob)[:, :], vv[s][:, kc, :], PT[pt][:], start=(kc == 0), stop=(kc == NKC - 1)),
                             reads=[vv_b[s], PT_b[pt]], writes=[bk.b[ob]], signal=True)

                    def group_add(kc, u, gi):
                        r = kc % GRP
                        ps = gi % NPS
                        if r == 0:
                            return
                        pt = u % NPT
                        if r == 1:
                            pprev = (u - 1) % NPT
                            P.op(P.dve, lambda e: e.tensor_tensor(out=PS[ps][:], in0=PT[pprev][:], in1=PT[pt][:], op=ALU.add),
                                 reads=[PT_b[pprev], PT_b[pt]], writes=[PS_b[ps]])
                        else:
                            P.op(P.dve, lambda e: e.tensor_tensor(out=PS[ps][:], in0=PS[ps][:], in1=PT[pt][:], op=ALU.add),
                                 reads=[PS_b[ps], PT_b[pt]], writes=[PS_b[ps]])

                    def issue_M(g, gi):
                        ps = gi % NPS
                        ng = NKC // GRP
                        P.op(P.pe, lambda e: e.matmul(bk.f32(mb)[:, :], ones[:], PS[ps][:], start=(g == 0), stop=(g == ng - 1)),
                             reads=[ones_b, PS_b[ps]], writes=[bk.b[mb]], signal=True)

                    LOOK = 2
                    for kc in range(min(LOOK, NKC)):
                        issue_S(kc, unit + kc)
                    for kc in range(NKC):
                        do_exp(kc, unit + kc)
                        group_add(kc, unit + kc, grp_i + kc // GRP)
                        if kc + LOOK < NKC:
                            issue_S(kc + LOOK, unit + kc + LOOK)
                        issue_PV(kc, unit + kc)
                        if kc % GRP == 0 and kc > 0:
                            issue_M(kc // GRP - 1, grp_i + kc // GRP - 1)
                    issue_M(NKC // GRP - 1, grp_i + NKC // GRP - 1)
                    grp_i += NKC // GRP
                    unit += NKC
                    i2 = qb % 2
                    P.op(P.dve, lambda e: e.reciprocal(out=rsb[i2][:], in_=bk.f32(mb)), reads=[bk.b[mb]], writes=[rsb_b[i2]])
                    P.op(P.dve, lambda e: e.tensor_tensor(out=o1[i2][:], in0=bk.f32(ob), in1=rsb[i2][:], op=ALU.mult),
                         reads=[bk.b[ob], rsb_b[i2]], writes=[o1_b[i2]])
                    P.op(P.pool, lambda e: e.tensor_tensor(out=gTh[h % 2][:, q0:q0 + 512], in0=o1[i2][:], in1=gz[s][:, q0:q0 + 512], op=ALU.mult),
                         reads=[o1_b[i2], gz_b[s]], writes=[gTh_b[h % 2]])
                P.dma(P.sp, gscr[h * 128:(h + 1) * 128, :], gTh[h % 2][:], reads=[gTh_b[h % 2]], writes=[gscr_b[h]], sem_buf=gTh_b[h % 2])
        P.barrier()
        with ExitStack() as es:
            gT = es.enter_context(nc.sbuf_tensor("gT", [128, 16, TOK], BF16))
            gT_bufs = P.bufs("gT", 4)
            wo = es.enter_context(nc.sbuf_tensor("wo", [128, 16, D], BF16))
            wo_b = P.buf("wo")
            gpost_t = es.enter_context(nc.sbuf_tensor("gpost_t", [128, D], F32))
            gpost_b = P.buf("gpost")
            P.dma(P.sp, gpost_t[:], gpost_d[:, :], writes=[gpost_b], sem_buf=gpost_b)
            for kc in range(16):
                P.dma(P.sp, gT[:, kc, :], gscr[kc * 128:(kc + 1) * 128, :], reads=[gscr_b[kc]], writes=[gT_bufs[kc // 4]], sem_buf=gT_bufs[kc // 4])
            for g in range(8):
                P.dma(P.pool, wo[:, 2 * g:2 * g + 2, :], w_out[:, 2 * g:2 * g + 2, :], writes=[wo_b], sem_buf=wo_b)
            phase_out(P, es, bk, gT, gT_bufs, wo, wo_b, gpost_t, gpost_b, x_d, 0, xout, TOK // 128)
            P.finish()
    return nc


def rope_tables_T(q):
    inv_freq = (1.0 / (np.float32(10000.0) ** (np.arange(0, 64, 2, dtype=np.float32) / np.float32(64)))).astype(np.float32)
    pos = np.arange(q * TOK, (q + 1) * TOK, dtype=np.float32)
    ang = (pos[:, None] * inv_freq[None, :]).astype(np.float32)
    c = np.cos(ang).astype(np.float32)
    s = np.sin(ang).astype(np.float32)
    cT = np.repeat(c.T, 2, axis=0)
    sT = np.repeat(s.T, 2, axis=0)
    return np.ascontiguousarray(cT), np.ascontiguousarray(sT)


def run_mla_layer(x, g_pre, g_post, w_in, q_norm, w_q_b, kv_norm, w_kv_b, w_out, nheads=NH):
    B, L, _ = x.shape
    nca = build_mla_a()
    wlat = np.ascontiguousarray(w_in[:, :1088].reshape(16, 128, 1088).transpose(1, 0, 2))
    wz = np.ascontiguousarray(w_in[:, 1088:].reshape(16, 128, NH, 128).transpose(2, 1, 0, 3))
    wq = np.ascontiguousarray(w_q_b.reshape(4, 128, 3072).transpose(1, 0, 2))
    wkv4 = w_kv_b.reshape(4, 128, NH, 2, 128)
    wkv = np.ascontiguousarray(wkv4.transpose(1, 0, 3, 2, 4).reshape(128, 4, 4096))
    gqkv = np.ascontiguousarray(np.broadcast_to(np.concatenate([q_norm, kv_norm])[None, :], (128, 1024)))
    gpre = np.ascontiguousarray(np.broadcast_to(g_pre[None, :], (128, D)))
    gpost = np.ascontiguousarray(np.broadcast_to(g_post[None, :], (128, D)))
    ident = np.eye(128, dtype=np.float32)
    in_maps = []
    for c in range(NCORES):
        b, q = divmod(c, 4)
        cT, sT = rope_tables_T(q)
        in_maps.append({"x": np.ascontiguousarray(x[b, q * TOK:(q + 1) * TOK]), "gpre": gpre, "wlat": wlat, "wz": wz, "wq": wq,
                        "wkv": wkv, "gqkv": gqkv, "cosT": cT, "sinT": sT, "ident": ident})
    ra = run_bass_kernel_spmd(nca, in_maps, core_ids=list(range(NCORES))).results
    ncb = build_mla_b(nheads)
    w_out_r = np.ascontiguousarray(w_out.reshape(16, 128, D).transpose(1, 0, 2))
    in_maps = []
    for b in range(B):
        KN = np.concatenate([ra[4 * b + q]["KN"] for q in range(4)], axis=2)
        KR = np.concatenate([ra[4 * b + q]["KR"] for q in range(4)], axis=1)
        VR = np.concatenate([ra[4 * b + q]["VR"] for q in range(4)], axis=2)
        for q in range(4):
            r = ra[4 * b + q]
            in_maps.append({"x": np.ascontiguousarray(x[b, q * TOK:(q + 1) * TOK]), "QN": r["QN"], "QR": r["QR"], "GZ": r["GZ"],
                            "KN": KN, "KR": KR, "VR": VR, "w_out": w_out_r, "gpost": gpost})
    rb = run_bass_kernel_spmd(ncb, in_maps, core_ids=list(range(NCORES))).results
    out = np.empty_like(x)
    for c in range(NCORES):
        b, q = divmod(c, 4)
        out[b, q * TOK:(q + 1) * TOK] = rb[c]["xout"]
    return out


def kernel(x, norm_pre, norm_post, na_w_in, na_rpb, na_w_out, mla_w_in, mla_q_norm, mla_w_q_b, mla_kv_norm,
           mla_w_kv_b, mla_w_out):
    x = np.ascontiguousarray(np.asarray(x, dtype=np.float32))
    f = lambda a: np.asarray(a, dtype=np.float32)
    norm_pre, norm_post = f(norm_pre), f(norm_post)
    for i in range(4):
        j = i // 2
        if i % 2 == 0:
            x = run_na_layer(x, norm_pre[i], norm_post[i], f(na_w_in[j]), f(na_rpb[j]), f(na_w_out[j]))
        else:
            x = run_mla_layer(x, norm_pre[i], norm_post[i], f(mla_w_in[j]), f(mla_q_norm[j]), f(mla_w_q_b[j]),
                              f(mla_kv_norm[j]), f(mla_w_kv_b[j]), f(mla_w_out[j]))
    return x
```
